# Optimizing a Trainium2 kernel written in Bass

```python
import math
import jax, jax.numpy as jnp
from jax import lax
import numpy as np

D_MODEL = 2048
BATCH = 4
SEQ = 8192
DEPTH = 1
DEC_BATCH = 2
DEC_SEQ = 4096
PAST_LEN = 128

MIX_WIDTH = D_MODEL
FOURIER_WIDTH = MIX_WIDTH // 2
ATTN_WIDTH = MIX_WIDTH - FOURIER_WIDTH
N_FOURIER_GROUPS = 8
FOURIER_GROUP_DIM = FOURIER_WIDTH // N_FOURIER_GROUPS
N_DIFF_HEADS = 8
DIFF_V_DIM = ATTN_WIDTH // N_DIFF_HEADS
DIFF_QK_DIM = DIFF_V_DIM // 2
IN_PROJ_COLS = FOURIER_WIDTH + 3 * ATTN_WIDTH
N_MEM = 256
N_CROSS_HEADS = 4
CROSS_HEAD_DIM = D_MODEL // N_CROSS_HEADS
D_FF = -(-8 * D_MODEL // (3 * 256)) * 256
ROPE_THETA = 10000.0
Q_BLOCK = 128
EPS = 1e-6
SUBLN_EPS = 1e-5

kernel_name = "hybrid_fourier_diffattn_encoder"


def rmsnorm(x, g, eps=EPS):
    xf = x.astype(jnp.float32)
    y = xf * lax.rsqrt(jnp.mean(xf * xf, axis=-1, keepdims=True) + eps)
    return (y * g.astype(jnp.float32)).astype(x.dtype)


def rope_tables(seq, dim):
    inv = 1.0 / (ROPE_THETA ** (jnp.arange(0, dim, 2, dtype=jnp.float32) / dim))
    ang = jnp.arange(seq, dtype=jnp.float32)[:, None] * inv[None, :]
    ang = jnp.concatenate([ang, ang], axis=-1)
    return jnp.cos(ang), jnp.sin(ang)


def apply_rope(t, cos, sin):
    half = t.shape[-1] // 2
    tf = t.astype(jnp.float32)
    rot = jnp.concatenate([-tf[..., half:], tf[..., :half]], axis=-1)
    c = cos[None, :, None, None, :]
    s = sin[None, :, None, None, :]
    return (tf * c + rot * s).astype(t.dtype)


def fourier_mix(u, w_f):
    B, S, _ = u.shape
    ug = u.reshape(B, S, N_FOURIER_GROUPS, FOURIER_GROUP_DIM).astype(jnp.float32)
    f = jnp.fft.fft2(ug, axes=(1, 3), norm="ortho").real.astype(u.dtype)
    y = jnp.einsum("bsgc,gcd->bsgd", f, w_f)
    return y.reshape(B, S, FOURIER_WIDTH)


def diff_attention(q, k, v, lam):
    B, S, H, _, dh = q.shape
    nblk = S // Q_BLOCK
    qb = q.reshape(B, nblk, Q_BLOCK, H, 2, dh).transpose(1, 0, 2, 3, 4, 5)
    scale = dh ** -0.5

    def block(qi):
        s = jnp.einsum("bqhcd,bkhcd->bhcqk", qi, k, preferred_element_type=jnp.float32) * scale
        p = jax.nn.softmax(s, axis=-1)
        a = p[:, :, 0] - lam * p[:, :, 1]
        return jnp.einsum("bhqk,bkhe->bqhe", a.astype(v.dtype), v)

    ob = lax.map(block, qb)
    return ob.transpose(1, 0, 2, 3, 4).reshape(B, S, H, v.shape[-1])


def cross_attention(h, mem_n, w_cq, w_ck, w_cv, w_co):
    B, S, _ = h.shape
    M = mem_n.shape[1]
    q = (h @ w_cq).reshape(B, S, N_CROSS_HEADS, CROSS_HEAD_DIM)
    k = (mem_n @ w_ck).reshape(B, M, N_CROSS_HEADS, CROSS_HEAD_DIM)
    v = (mem_n @ w_cv).reshape(B, M, N_CROSS_HEADS, CROSS_HEAD_DIM)
    s = jnp.einsum("bqhd,bkhd->bhqk", q, k, preferred_element_type=jnp.float32) * (CROSS_HEAD_DIM ** -0.5)
    p = jax.nn.softmax(s, axis=-1).astype(v.dtype)
    o = jnp.einsum("bhqk,bkhd->bqhd", p, v).reshape(B, S, D_MODEL)
    return o @ w_co


def encoder_layer(x, mem, l, g_mix, w_in, w_fourier, lambda_q1, lambda_k1, lambda_q2, lambda_k2,
                  g_subln, w_out, g_cross, g_mem, w_cq, w_ck, w_cv, w_co, g_ffn, w_gate, w_up, w_down):
    B, S, _ = x.shape
    lam_init = 0.8 - 0.6 * math.exp(-0.3 * l)
    h = rmsnorm(x, g_mix)
    proj = h @ w_in
    u_f = proj[..., :FOURIER_WIDTH]
    q = proj[..., FOURIER_WIDTH:FOURIER_WIDTH + ATTN_WIDTH].reshape(B, S, N_DIFF_HEADS, 2, DIFF_QK_DIM)
    k = proj[..., FOURIER_WIDTH + ATTN_WIDTH:FOURIER_WIDTH + 2 * ATTN_WIDTH].reshape(B, S, N_DIFF_HEADS, 2, DIFF_QK_DIM)
    v = proj[..., FOURIER_WIDTH + 2 * ATTN_WIDTH:].reshape(B, S, N_DIFF_HEADS, DIFF_V_DIM)

    y_f = fourier_mix(u_f, w_fourier)

    cos, sin = rope_tables(S, DIFF_QK_DIM)
    q = apply_rope(q, cos, sin)
    k = apply_rope(k, cos, sin)
    lq1 = lambda_q1.astype(jnp.float32); lk1 = lambda_k1.astype(jnp.float32)
    lq2 = lambda_q2.astype(jnp.float32); lk2 = lambda_k2.astype(jnp.float32)
    lam = jnp.exp(jnp.sum(lq1 * lk1)) - jnp.exp(jnp.sum(lq2 * lk2)) + lam_init
    o = diff_attention(q, k, v, lam)
    o = rmsnorm(o, g_subln, SUBLN_EPS) * (1.0 - lam_init)
    mix = jnp.concatenate([y_f, o.reshape(B, S, ATTN_WIDTH).astype(y_f.dtype)], axis=-1) @ w_out
    x = x + mix
    x = x + cross_attention(rmsnorm(x, g_cross), rmsnorm(mem, g_mem), w_cq, w_ck, w_cv, w_co)
    hf = rmsnorm(x, g_ffn)
    x = x + (jax.nn.silu(hf @ w_gate) * (hf @ w_up)) @ w_down
    return x


def setup_inputs(seed: int = 0) -> dict:
    key = jax.random.key(seed)
    ks = jax.random.split(key, 24)
    f32 = jnp.float32

    def w(k, shape, fan_in):
        return jax.random.normal(k, shape, f32) * (fan_in ** -0.5)

    def gain(k, shape):
        return 1.0 + 0.02 * jax.random.normal(k, shape, f32)

    return {
        "x_prompt": jax.random.normal(ks[0], (BATCH, SEQ, D_MODEL), f32),
        "x_sample": jax.random.normal(ks[1], (DEC_BATCH, DEC_SEQ, D_MODEL), f32),
        "mem_prompt": jax.random.normal(ks[2], (BATCH, N_MEM, D_MODEL), f32),
        "mem_sample": jax.random.normal(ks[3], (DEC_BATCH, N_MEM, D_MODEL), f32),
        "g_mix": gain(ks[4], (DEPTH, D_MODEL)),
        "w_in": w(ks[5], (DEPTH, D_MODEL, IN_PROJ_COLS), D_MODEL),
        "w_fourier": w(ks[6], (DEPTH, N_FOURIER_GROUPS, FOURIER_GROUP_DIM, FOURIER_GROUP_DIM), FOURIER_GROUP_DIM),
        "lambda_q1": 0.1 * jax.random.normal(ks[7], (DEPTH, DIFF_QK_DIM), f32),
        "lambda_k1": 0.1 * jax.random.normal(ks[8], (DEPTH, DIFF_QK_DIM), f32),
        "lambda_q2": 0.1 * jax.random.normal(ks[9], (DEPTH, DIFF_QK_DIM), f32),
        "lambda_k2": 0.1 * jax.random.normal(ks[10], (DEPTH, DIFF_QK_DIM), f32),
        "g_subln": gain(ks[11], (DEPTH, DIFF_V_DIM)),
        "w_out": w(ks[12], (DEPTH, MIX_WIDTH, D_MODEL), MIX_WIDTH),
        "g_cross": gain(ks[13], (DEPTH, D_MODEL)),
        "g_mem": gain(ks[14], (DEPTH, D_MODEL)),
        "w_cq": w(ks[15], (DEPTH, D_MODEL, D_MODEL), D_MODEL),
        "w_ck": w(ks[16], (DEPTH, D_MODEL, D_MODEL), D_MODEL),
        "w_cv": w(ks[17], (DEPTH, D_MODEL, D_MODEL), D_MODEL),
        "w_co": w(ks[18], (DEPTH, D_MODEL, D_MODEL), D_MODEL),
        "g_ffn": gain(ks[19], (DEPTH, D_MODEL)),
        "w_gate": w(ks[20], (DEPTH, D_MODEL, D_FF), D_MODEL),
        "w_up": w(ks[21], (DEPTH, D_MODEL, D_FF), D_MODEL),
        "w_down": w(ks[22], (DEPTH, D_FF, D_MODEL), D_FF),
        "g_final": gain(ks[23], (D_MODEL,)),
    }


def _trunk(x, mem, g_mix, w_in, w_fourier, lambda_q1, lambda_k1, lambda_q2, lambda_k2, g_subln, w_out,
           g_cross, g_mem, w_cq, w_ck, w_cv, w_co, g_ffn, w_gate, w_up, w_down, g_final):
    for l in range(DEPTH):
        x = encoder_layer(x, mem, l, g_mix[l], w_in[l], w_fourier[l], lambda_q1[l], lambda_k1[l],
                          lambda_q2[l], lambda_k2[l], g_subln[l], w_out[l], g_cross[l], g_mem[l],
                          w_cq[l], w_ck[l], w_cv[l], w_co[l], g_ffn[l], w_gate[l], w_up[l], w_down[l])
    return rmsnorm(x, g_final)


def reference(x_prompt, x_sample, mem_prompt, mem_sample, g_mix, w_in, w_fourier, lambda_q1, lambda_k1,
              lambda_q2, lambda_k2, g_subln, w_out, g_cross, g_mem, w_cq, w_ck, w_cv, w_co, g_ffn,
              w_gate, w_up, w_down, g_final):
    y_prompt = _trunk(x_prompt, mem_prompt, g_mix, w_in, w_fourier, lambda_q1, lambda_k1, lambda_q2, lambda_k2,
                      g_subln, w_out, g_cross, g_mem, w_cq, w_ck, w_cv, w_co, g_ffn, w_gate, w_up, w_down, g_final)
    y_sample = _trunk(x_sample, mem_sample, g_mix, w_in, w_fourier, lambda_q1, lambda_k1, lambda_q2, lambda_k2,
                      g_subln, w_out, g_cross, g_mem, w_cq, w_ck, w_cv, w_co, g_ffn, w_gate, w_up, w_down, g_final)
    return (y_prompt, y_sample)
```

```python
import numpy as np
import ml_dtypes
import concourse.bass as bass
import concourse.mybir as mybir
from concourse.bass_utils import run_bass_kernel_spmd
from concourse.alu_op_type import AluOpType as ALU

F32 = mybir.dt.float32
BF16 = mybir.dt.bfloat16
AF = mybir.ActivationFunctionType
bf16_np = ml_dtypes.bfloat16

D = 2048
DFF = 5632
NH = 8
NMEM = 256
ENGS = ("pe", "act", "dve", "pool", "sp")


class Instr:
    __slots__ = ("eng", "fn", "raw", "other", "dma_sem", "dma_val", "milestone", "tick", "extra_waits")

    def __init__(self, eng, fn):
        self.eng = eng
        self.fn = fn
        self.raw = []
        self.other = []
        self.dma_sem = None
        self.dma_val = 0
        self.milestone = False
        self.tick = 0
        self.extra_waits = []


class Prog:
    def __init__(self, nc):
        self.nc = nc
        self.streams = {e: [] for e in ENGS}
        self.last_writer = {}
        self.readers = {}
        self.dma_count = {}
        self.pending = {e: [] for e in ENGS}

    def op(self, eng, fn, reads=(), writes=(), dma_sem=None):
        ins = Instr(eng, fn)
        if dma_sem is not None:
            c = self.dma_count.get(dma_sem, 0) + 16
            self.dma_count[dma_sem] = c
            ins.dma_sem = dma_sem
            ins.dma_val = c
        raw = {}
        oth = {}
        for k in reads:
            w = self.last_writer.get(k)
            if w is not None:
                raw[id(w)] = w
        for k in writes:
            w = self.last_writer.get(k)
            if w is not None:
                oth[id(w)] = w
            rd = self.readers.get(k)
            if rd:
                for r in rd.values():
                    oth[id(r)] = r
        ins.raw = list(raw.values())
        ins.other = [d for i, d in oth.items() if i not in raw]
        if self.pending[eng]:
            ins.extra_waits = self.pending[eng]
            self.pending[eng] = []
        for k in writes:
            self.last_writer[k] = ins
            self.readers[k] = {}
        rk = (eng, dma_sem)
        for k in reads:
            self.readers.setdefault(k, {})[rk] = ins
        self.streams[eng].append(ins)
        return ins

    def barrier(self):
        lasts = []
        for e in ENGS:
            if e == "sp":
                continue
            if self.streams[e]:
                l = self.streams[e][-1]
                l.milestone = True
                lasts.append(l)
        dm = list(self.dma_count.items())
        for e in ENGS:
            self.pending[e] = self.pending[e] + [("ins", l) for l in lasts if l.eng != e] + [("dma", k, v) for k, v in dm]
        self.last_writer = {}
        self.readers = {}

    def emit(self, sems_eng, sems_dma):
        nc = self.nc
        for e in ENGS:
            for ins in self.streams[e]:
                for d in ins.raw:
                    if d.dma_sem is None and not (d.eng == e and e == "pe"):
                        d.milestone = True
                for d in ins.other:
                    if d.dma_sem is None and d.eng != e:
                        d.milestone = True
        for e in ENGS:
            t = 0
            for ins in self.streams[e]:
                if ins.milestone:
                    t += 1
                    ins.tick = t

        def run(e, eng):
            waited = {}
            for ins in self.streams[e]:
                need = {}
                for d in ins.raw:
                    if d.dma_sem is not None:
                        sk, v = ("d", d.dma_sem), d.dma_val
                    elif d.eng == e and e == "pe":
                        continue
                    else:
                        sk, v = ("e", d.eng), d.tick
                    if need.get(sk, 0) < v:
                        need[sk] = v
                for d in ins.other:
                    if d.dma_sem is not None:
                        sk, v = ("d", d.dma_sem), d.dma_val
                    elif d.eng != e:
                        sk, v = ("e", d.eng), d.tick
                    else:
                        continue
                    if need.get(sk, 0) < v:
                        need[sk] = v
                for w in ins.extra_waits:
                    if w[0] == "ins":
                        sk, v = ("e", w[1].eng), w[1].tick
                    else:
                        sk, v = ("d", w[1]), w[2]
                    if need.get(sk, 0) < v:
                        need[sk] = v
                for sk, val in need.items():
                    if waited.get(sk, 0) >= val:
                        continue
                    waited[sk] = val
                    sem = sems_eng[sk[1]] if sk[0] == "e" else sems_dma[sk[1]]
                    eng.wait_ge(sem, val)
                r = ins.fn(eng)
                if ins.dma_sem is not None:
                    r.then_inc(sems_dma[ins.dma_sem], 16)
                elif ins.milestone:
                    r.then_inc(sems_eng[e], 1)
            if e == "sp":
                for k, v in self.dma_count.items():
                    if waited.get(("d", k), 0) < v:
                        eng.wait_ge(sems_dma[k], v)

        with nc.Block() as block:
            @block.tensor
            def _(eng):
                run("pe", eng)

            @block.scalar
            def _(eng):
                run("act", eng)

            @block.vector
            def _(eng):
                run("dve", eng)

            @block.gpsimd
            def _(eng):
                run("pool", eng)

            @block.sync
            def _(eng):
                run("sp", eng)


class Arena:
    def __init__(self, ap, nbytes):
        self.ap = ap
        self.cap = nbytes
        self.off = 0

    def reset(self, base=0):
        self.off = base

    def alloc(self, shape, dt):
        n = 1
        for s in shape[1:]:
            n *= s
        nb = n * (4 if dt == F32 else 2)
        a = self.ap[:, self.off // 2:(self.off + nb) // 2]
        self.off += (nb + 63) // 64 * 64
        assert self.off <= self.cap, ("SBUF arena overflow", self.off, self.cap)
        if dt == F32:
            a = a.bitcast(F32)
        if len(shape) == 3:
            a = a.rearrange("p (a b) -> p a b", a=shape[1])
        return a


DMA_SEMS = (["w%d" % i for i in range(6)] + ["x0", "x1", "rt0", "rt1", "st0", "st1", "st2", "st3",
            "c0", "c1", "c2", "hd0", "hd1", "z0", "df0", "df1", "df2", "misc", "out0", "out1", "cv0", "cv1", "cs0", "cs1"])


class Builder:
    def __init__(self, groups, stop_after=None, debug=()):
        self.groups = groups
        self.stop_after = stop_after
        self.debug = debug
        self.nc = bass.Bass("TRN2", target_bir_lowering=False)
        self.evc = 0

    def dma(self, out, in_, sem, reads=(), writes=()):
        return self.P.op("sp", lambda e: e.dma_start(out=out, in_=in_), reads, writes, dma_sem=sem)

    @staticmethod
    def seal(instrs):
        v = max(i.dma_val for i in instrs)
        for i in instrs:
            i.dma_val = v

    def mm(self, out, lhsT, rhs, start, stop, reads, writes, skip=False):
        if skip:
            return self.P.op("pe", lambda e: e.matmul(out, lhsT=lhsT, rhs=rhs, start=start, stop=stop, skip_group_check=True), reads, writes)
        return self.P.op("pe", lambda e: e.matmul(out, lhsT=lhsT, rhs=rhs, start=start, stop=stop), reads, writes)

    def tr(self, out, in_, reads, writes):
        ident = self.ident
        return self.P.op("pe", lambda e: e.transpose(out, in_, ident), list(reads) + ["ident"], writes)

    def act(self, out, in_, func, reads, writes, scale=None, accum=None):
        kw = {}
        if scale is not None:
            kw["scale"] = scale
        if accum is not None:
            kw["accum_out"] = accum
        return self.P.op("act", lambda e: e.activation(out=out, in_=in_, func=func, **kw), reads, writes)

    def tcopy(self, eng, out, in_, reads, writes):
        if eng == "act":
            return self.P.op("act", lambda e: e.activation(out=out, in_=in_, func=AF.Copy), reads, writes)
        return self.P.op(eng, lambda e: e.tensor_copy(out=out, in_=in_), reads, writes)

    def evac(self, out, in_, reads, writes, act_share=1, of=2):
        self.evc += 1
        eng = "act" if (self.evc % of) < act_share else "dve"
        return self.tcopy(eng, out, in_, reads, writes)

    def tt(self, eng, out, in0, in1, op, reads, writes):
        return self.P.op(eng, lambda e: e.tensor_tensor(out=out, in0=in0, in1=in1, op=op), reads, writes)

    def ts(self, out, in0, s1, s2, op0, op1, reads, writes, eng="dve"):
        if op1 is None:
            return self.P.op(eng, lambda e: e.tensor_scalar(out=out, in0=in0, scalar1=s1, scalar2=None, op0=op0), reads, writes)
        return self.P.op(eng, lambda e: e.tensor_scalar(out=out, in0=in0, scalar1=s1, scalar2=s2, op0=op0, op1=op1), reads, writes)

    def stt(self, out, in0, scalar, in1, op0, op1, reads, writes):
        return self.P.op("dve", lambda e: e.scalar_tensor_tensor(out=out, in0=in0, scalar=scalar, in1=in1, op0=op0, op1=op1), reads, writes)

    def rstd_from_ss(self, ss, mse, rstd, inv_n, eps, kss, kmse, krstd):
        self.ts(mse, ss, inv_n, eps, ALU.mult, ALU.add, [kss], [kmse])
        nh = self.negh
        self.P.op("pool", lambda e: e.tensor_tensor(out=rstd, in0=mse, in1=nh, op=ALU.pow), [kmse, "negh"], [krstd])

    def build(self):
        nc = self.nc
        dt = nc.dram_tensor
        G = self.groups
        self.din = {}

        def inp(name, shape, dtype=F32):
            self.din[name] = dt(name, list(shape), dtype, kind="ExternalInput").ap()
            return self.din[name]

        def scr(name, shape, dtype=BF16):
            return dt(name, list(shape), dtype, kind="Internal").ap()

        self.x = [inp("x%d" % g, [S, D]) for g, (S, T) in enumerate(G)]
        self.mem = [inp("mem%d" % g, [NMEM, D]) for g in range(len(G))]
        self.rc = [inp("rc%d" % g, [128, S]) for g, (S, T) in enumerate(G)]
        self.rs = [inp("rs%d" % g, [128, S]) for g, (S, T) in enumerate(G)]
        self.dc = [inp("dc%d" % g, [T // 128, 128, S // 128, 128], BF16) for g, (S, T) in enumerate(G)]
        self.ds = [inp("ds%d" % g, [T // 128, 128, S // 128, 128], BF16) for g, (S, T) in enumerate(G)]
        self.y = [dt("y%d" % g, [T, D], F32, kind="ExternalOutput").ap() for g, (S, T) in enumerate(G)]
        w_in = inp("w_in", [D, 4096])
        w_f = inp("w_f", [8, 128, 128])
        ccsc = inp("ccsc", [128, 256])
        lam_in = inp("lam", [1, 256])
        gvec = inp("gvec", [5, D])
        g_sub = inp("g_sub", [128])
        identd = inp("ident", [128, 128], BF16)
        pmatd = inp("pmat", [128, 128], BF16)
        wsrc = {n: inp(n, [D, D]) for n in ("w_out", "w_cq", "w_ck", "w_cv", "w_co")}
        wsrc["w_gate"] = inp("w_gate", [D, DFF])
        wsrc["w_up"] = inp("w_up", [D, DFF])
        wsrc["w_down"] = inp("w_down", [DFF, D])
        WA = scr("WA", [8, 128, 16, 512])
        WB = {n: scr("S_" + n, [4, 128, 16, 512]) for n in ("w_out", "w_cq", "w_ck", "w_cv", "w_co")}
        WB["w_gate"] = scr("S_w_gate", [11, 128, 16, 512])
        WB["w_up"] = scr("S_w_up", [11, 128, 16, 512])
        WB["w_down"] = scr("S_w_down", [4, 128, 44, 512])
        lam_scr = scr("lam_scr", [1, 1], F32)
        KT = [scr("KT%d" % g, [NH, 128, S]) for g, (S, T) in enumerate(G)]
        QT = [scr("QT%d" % g, [NH, 128, T]) for g, (S, T) in enumerate(G)]
        V = [scr("V%d" % g, [S, 1024]) for g, (S, T) in enumerate(G)]
        Z = [scr("Z%d" % g, [S, 2048]) for g, (S, T) in enumerate(G)]
        MIXT = [scr("MIXT%d" % g, [D, T]) for g, (S, T) in enumerate(G)]

        ARENA_BYTES = 199 * 1024
        with (
            nc.sbuf_tensor("arena", [128, ARENA_BYTES // 2], BF16) as arena_t,
            nc.sbuf_tensor("consts", [128, 3072], BF16) as consts_t,
            nc.psum_tensor("psum", [128, 8, 512], F32) as psum,
        ):
            import contextlib
            with contextlib.ExitStack() as es:
                sems_eng = {e: es.enter_context(nc.semaphore("se_" + e)) for e in ENGS}
                sems_dma = {k: es.enter_context(nc.semaphore("sd_" + k)) for k in DMA_SEMS}
                self.P = P = Prog(nc)
                A = Arena(arena_t, ARENA_BYTES)
                C = Arena(consts_t, 6144)
                self.ident = C.alloc([128, 128], BF16)
                self.negh = C.alloc([128, 1], F32)
                lam_b = C.alloc([128, 1], F32)
                gsb = C.alloc([128, 128], F32)
                ones_c = C.alloc([128, 1], BF16)
                self.pmat = C.alloc([128, 128], BF16)
                self.AB = C.alloc([128, 8, 256], BF16)
                grp0 = [self.dma(self.ident, identd, "misc", writes=["ident"])]
                negh = self.negh
                P.op("pool", lambda e: e.memset(negh, -0.5), writes=["negh"])
                P.op("pool", lambda e: e.memset(ones_c, 1.0), writes=["ones_c"])

                def ps_bank(b):
                    return psum[:, b, :]

                def ps_bf(b):
                    return psum[:, b, :].bitcast(BF16).rearrange("p (a b) -> p a b", a=8)

                CONV_W = 3072
                A.reset()
                cw32 = [A.alloc([128, CONV_W], F32) for _ in range(2)]
                cw16 = [A.alloc([128, CONV_W], BF16) for _ in range(2)]
                self.conv_bytes = A.off
                lam_sb = A.alloc([128, 256], F32)
                gs_raw = A.alloc([128, 128], F32)
                cc32 = A.alloc([128, 256], F32)
                wf32 = A.alloc([128, 8, 128], F32)
                grp0.append(self.dma(self.pmat, pmatd, "misc", writes=["pmat"]))
                grp0.append(self.dma(lam_sb[0:1, :], lam_in, "misc", writes=["lam_sb"]))
                grp0.append(self.dma(gs_raw, g_sub.partition_broadcast(128), "misc", writes=["gs_raw"]))
                grp0.append(self.dma(cc32, ccsc, "misc", writes=["cc32"]))
                grp0.append(self.dma(wf32, w_f.rearrange("g c d -> c g d"), "misc", writes=["wf32"]))
                self.seal(grp0)
                tasks = []
                for kc in range(16):
                    rows = slice(kc * 128, (kc + 1) * 128)
                    tasks.append((w_in[rows, 0:2048], 2048,
                                  [(WA[0:2, :, kc, :].rearrange("b p c -> p b c"), 0, 1024),
                                   (WA[6:8, :, kc, :].rearrange("b p c -> p b c"), 1024, 1024)]))
                    tasks.append((w_in[rows, 2048:4096], 2048,
                                  [(WA[2:4, :, kc, :].rearrange("b p c -> p b c"), 0, 1024),
                                   (WA[4:6, :, kc, :].rearrange("b p c -> p b c"), 1024, 1024)]))
                n_fore = len(tasks)
                for kc in range(16):
                    rows = slice(kc * 128, (kc + 1) * 128)
                    for n in ("w_out", "w_cq", "w_ck", "w_cv", "w_co"):
                        tasks.append((wsrc[n][rows, :], 2048, [(WB[n][:, :, kc, :].rearrange("b p c -> p b c"), 0, 2048)]))
                    for n in ("w_gate", "w_up"):
                        tasks.append((wsrc[n][rows, 0:3072], 3072, [(WB[n][0:6, :, kc, :].rearrange("b p c -> p b c"), 0, 3072)]))
                        tasks.append((wsrc[n][rows, 3072:DFF], 2560, [(WB[n][6:11, :, kc, :].rearrange("b p c -> p b c"), 0, 2560)]))
                for kc in range(44):
                    rows = slice(kc * 128, (kc + 1) * 128)
                    tasks.append((wsrc["w_down"][rows, :], 2048, [(WB["w_down"][:, :, kc, :].rearrange("b p c -> p b c"), 0, 2048)]))
                cstate = {"i": 0, "loaded": 0, "stored": 0}

                def conv_store(i):
                    for (dst, c0, nsub) in tasks[i][2]:
                        self.dma(dst, cw16[i % 2][:, c0:c0 + nsub].rearrange("p (b c) -> p b c", c=512), "cs%d" % (i % 2),
                                 reads=[("cw16", i % 2)])

                def conv_step(n, engines):
                    for _ in range(n):
                        i = cstate["i"]
                        if i >= len(tasks):
                            break
                        while cstate["loaded"] <= min(i + 1, len(tasks) - 1):
                            l = cstate["loaded"]
                            self.dma(cw32[l % 2][:, 0:tasks[l][1]], tasks[l][0], "cv%d" % (l % 2), writes=[("cw32", l % 2)])
                            cstate["loaded"] += 1
                        ncols = tasks[i][1]
                        self.tcopy(engines[i % len(engines)], cw16[i % 2][:, 0:ncols], cw32[i % 2][:, 0:ncols], [("cw32", i % 2)], [("cw16", i % 2)])
                        while cstate["stored"] < i:
                            conv_store(cstate["stored"])
                            cstate["stored"] += 1
                        cstate["i"] += 1
                    if cstate["i"] >= len(tasks):
                        while cstate["stored"] < len(tasks):
                            conv_store(cstate["stored"])
                            cstate["stored"] += 1

                def conv_flush():
                    while cstate["stored"] < cstate["i"]:
                        conv_store(cstate["stored"])
                        cstate["stored"] += 1

                self.conv_step = conv_step
                self.conv_left = lambda: len(tasks) - cstate["i"]
                conv_step(n_fore, ("dve", "act", "pool"))
                conv_flush()

                lj = A.alloc([128, 64], F32)
                lsum = A.alloc([128, 2], F32)
                lexp = A.alloc([128, 2], F32)
                lval = A.alloc([128, 1], F32)
                for i in range(2):
                    a0 = lam_sb[0:1, i * 128:i * 128 + 64]
                    a1 = lam_sb[0:1, i * 128 + 64:i * 128 + 128]
                    acc_ = lsum[0:1, i:i + 1]
                    P.op("dve", lambda e, a0=a0, a1=a1, acc_=acc_: e.scalar_tensor_tensor(
                        out=lj[0:1, :], in0=a0, scalar=1.0, in1=a1, op0=ALU.mult, op1=ALU.mult, accum_out=acc_),
                        ["lam_sb"], ["lj", ("lsum", i)])
                self.act(lexp[0:1, :], lsum[0:1, :], AF.Exp, [("lsum", 0), ("lsum", 1)], ["lexp"])
                self.stt(lval[0:1, :], lexp[0:1, 0:1], 0.2, lexp[0:1, 1:2], ALU.add, ALU.subtract, ["lexp"], ["lval"])
                self.dma(lam_scr, lval[0:1, :], "c2", reads=["lval"], writes=["lam_scr"])
                self.dma(lam_b, lam_scr[0, :].partition_broadcast(128), "c2", reads=["lam_scr"], writes=["lam_b"])
                self.ts(gsb, gs_raw, 0.8, None, ALU.mult, None, ["gs_raw"], ["gsb"])
                AB = self.AB
                for g in range(8):
                    for t in range(2):
                        b = (g * 2 + t) % 2
                        self.mm(ps_bank(b)[:, 0:128], cc32[:, t * 128:(t + 1) * 128], wf32[:, g, :], True, True, ["cc32", "wf32"], [("ps", b)])
                        self.tcopy("dve", AB[:, g, t * 128:(t + 1) * 128], ps_bank(b)[:, 0:128], [("ps", b)], [("AB", g, t)])
                P.barrier()

                stop = self.stop_after
                for gi, (S, T) in enumerate(G):
                    if stop == "W":
                        break
                    self.phase_A(A, psum, gi, S, T, gvec, WA, KT[gi], QT[gi], V[gi], Z[gi])
                    self.conv_step(self.conv_left(), ("pool", "act", "dve"))
                    P.barrier()
                    if stop == "A":
                        break
                    self.phase_B(A, psum, gi, S, T, KT[gi], QT[gi], V[gi], MIXT[gi], lam_b, gsb)
                    P.barrier()
                    if stop == "B":
                        break
                    self.phase_C(A, psum, gi, S, T, Z[gi], MIXT[gi])
                    P.barrier()
                    if stop == "C":
                        break
                    self.phase_D(A, psum, gi, S, T, gvec, WB, MIXT[gi], ones_c)
                    P.barrier()
                if self.debug:
                    dbg_src = {"WA": WA, "KT": KT[0], "QT": QT[0], "V": V[0], "Z": Z[0], "MIXT": MIXT[0], "Wout": WB["w_out"], "Wdown": WB["w_down"]}
                    for i, nm in enumerate(self.debug):
                        src = dbg_src[nm]
                        shp = list(src.shape)
                        dst = dt("dbg_" + nm, shp, BF16, kind="ExternalOutput").ap()
                        if len(shp) == 4:
                            for b_ in range(shp[0]):
                                self.dma(dst[b_], src[b_], "misc")
                        elif len(shp) == 3:
                            for b_ in range(shp[0]):
                                self.dma(dst[b_], src[b_], "misc")
                        else:
                            self.dma(dst, src, "misc")
                P.emit(sems_eng, sems_dma)
        return nc

    def norm_transpose(self, psum, xin, kx, gb, sm, hbf, khbf, hT, khT, tcol, tbanks):
        ss, mse, rstd, ksm = sm
        self.act(hbf, xin, AF.Square, [kx], [khbf, (ksm, "ss")], accum=ss)
        self.rstd_from_ss(ss, mse, rstd, 1.0 / D, 1e-6, (ksm, "ss"), (ksm, "mse"), (ksm, "rstd"))
        self.stt(hbf, xin, rstd, gb, ALU.mult, ALU.mult, [kx, (ksm, "rstd"), "gb"], [khbf])
        for half in range(2):
            b = tbanks[half]
            pst = psum[:, b, :].bitcast(BF16).rearrange("p (a b) -> p a b", a=8)
            for j in range(8):
                kc = half * 8 + j
                self.tr(pst[:, j, :], hbf[:, kc * 128:(kc + 1) * 128], [khbf], [("ps", b)])
            self.evac(hT[:, half * 8:(half + 1) * 8, tcol:tcol + 128], pst, [("ps", b)], [(khT, half, tcol)])

    def norm_only(self, xin, kx, gb, sm, hbf, khbf):
        ss, mse, rstd, ksm = sm
        self.act(hbf, xin, AF.Square, [kx], [khbf, (ksm, "ss")], accum=ss)
        self.rstd_from_ss(ss, mse, rstd, 1.0 / D, 1e-6, (ksm, "ss"), (ksm, "mse"), (ksm, "rstd"))
        self.stt(hbf, xin, rstd, gb, ALU.mult, ALU.mult, [kx, (ksm, "rstd"), "gb"], [khbf])

    def transpose_only(self, psum, hbf, khbf, hT, khT, tcol, tbanks):
        for half in range(2):
            b = tbanks[half]
            pst = psum[:, b, :].bitcast(BF16).rearrange("p (a b) -> p a b", a=8)
            for j in range(8):
                kc = half * 8 + j
                self.tr(pst[:, j, :], hbf[:, kc * 128:(kc + 1) * 128], [khbf], [("ps", b)])
            self.evac(hT[:, half * 8:(half + 1) * 8, tcol:tcol + 128], pst, [("ps", b)], [(khT, half, tcol)])

    def phase_A(self, A, psum, gi, S, T, gvec, WA, KT, QT, V, Z):
        P = self.P
        A.reset(self.conv_bytes if self.conv_left() > 0 else 0)
        ntile = S // 512
        nown = T // 512
        gb = A.alloc([128, D], F32)
        self.dma(gb, gvec[0, :].partition_broadcast(128), "misc", writes=["gb"])
        xin = [A.alloc([128, D], F32) for _ in range(2)]
        hbf = [A.alloc([128, D], BF16) for _ in range(4)]
        sms = [(A.alloc([128, 1], F32), A.alloc([128, 1], F32), A.alloc([128, 1], F32), ("smA", i)) for i in range(4)]
        hT = [A.alloc([128, 16, 512], BF16) for _ in range(2)]
        R = 6
        wblk = [A.alloc([128, 8, 512], BF16) for _ in range(R)]
        rct = [A.alloc([128, 512], F32) for _ in range(2)]
        rst = [A.alloc([128, 512], F32) for _ in range(2)]
        stg = [A.alloc([128, 4, 512], BF16) for _ in range(2)]
        zstg = [A.alloc([128, 4, 1024], BF16) for _ in range(2)]
        fsb = [A.alloc([128, 512], BF16) for _ in range(2)]
        t1 = [A.alloc([128, 512], F32) for _ in range(2)]
        t2 = [A.alloc([128, 512], F32) for _ in range(2)]
        AB = self.AB
        pmat = self.pmat
        xcnt = [0]

        def prologue(ti):
            hs = ti % 2
            self.seal([self.dma(rct[hs], self.rc[gi][:, ti * 512:(ti + 1) * 512], "rt%d" % hs, writes=[("rct", hs)]),
                       self.dma(rst[hs], self.rs[gi][:, ti * 512:(ti + 1) * 512], "rt%d" % hs, writes=[("rst", hs)])])
            for st in range(4):
                xs = xcnt[0] % 2
                xcnt[0] += 1
                r0 = ti * 512 + st * 128
                self.dma(xin[xs], self.x[gi][r0:r0 + 128, :], "x%d" % xs, writes=[("xin", xs)])
                self.norm_only(xin[xs], ("xin", xs), gb, sms[st], hbf[st], ("hbf", st))

        def prologue_tr(ti):
            hs = ti % 2
            for st in range(4):
                self.transpose_only(psum, hbf[st], ("hbf", st), hT[hs], ("hT", hs), st * 128, (6, 7))

        jobs = []
        for ti in range(ntile):
            for b in range(2):
                jobs.append((ti, "four", b, ("Z", b)))
            for b in range(2):
                jobs.append((ti, "tok", 4 + b, ("V", b)))
            for b in range(2):
                jobs.append((ti, "rope", 2 + b, ("K", b)))
            if ti < nown:
                for b in range(2):
                    jobs.append((ti, "rope", 6 + b, ("Q", b)))
        akinds = getattr(self, "akinds", None)
        if akinds is not None:
            jobs = [j for j in jobs if j[1] in akinds]
        issued = [0]
        nfills = 2 * len(jobs)

        def ensure(upto):
            while issued[0] <= min(upto, nfills - 1):
                f = issued[0]
                blk = jobs[f // 2][2]
                kh = f % 2
                s = f % R
                self.dma(wblk[s], WA[blk, :, kh * 8:(kh + 1) * 8, :], "w%d" % s, writes=[("wblk", s)])
                issued[0] += 1

        psc = [0]

        def bank():
            b = psc[0] % 6
            psc[0] += 1
            return b

        pend = []
        ucnt = [0]

        def flush():
            while pend:
                pend.pop(0)()

        prologue(0)
        prologue_tr(0)
        cur_t = -1
        for ji, (ti, kind, blk, dest) in enumerate(jobs):
            if ti != cur_t:
                cur_t = ti
                flush()
                if ti + 1 < ntile:
                    prologue(ti + 1)
            if ti + 1 < ntile and (ji + 1 == len(jobs) or jobs[ji + 1][0] != ti):
                prologue_tr(ti + 1)
            hs = ti % 2
            f0 = 2 * ji
            ensure(f0 + R - 1)
            if self.conv_left() > 0:
                self.conv_step(-(-self.conv_left() // max(1, (len(jobs) - ji) - 8)) if len(jobs) - ji > 8 else self.conv_left(),
                               ("pool", "act", "pool", "dve"))
            ss_ = ji % 2
            r0 = ti * 512
            if kind == "tok":
                banks = [bank() for _ in range(4)]
                for kh in range(2):
                    s = (f0 + kh) % R
                    for st in range(4):
                        for k8 in range(8):
                            kc = kh * 8 + k8
                            self.mm(psum[:, banks[st], :], hT[hs][:, kc, st * 128:(st + 1) * 128], wblk[s][:, k8, :],
                                    kc == 0, kc == 15, [(("hT", hs), kc // 8, st * 128), ("wblk", s)], [("ps", banks[st])])
                flush()
                for st in range(4):
                    self.evac(stg[ss_][:, st, :], psum[:, banks[st], :], [("ps", banks[st])], [("stg", ss_, st)], act_share=2, of=3)
                dst = V[r0:r0 + 512, dest[1] * 512:(dest[1] + 1) * 512]
                self.dma(dst.rearrange("(s p) c -> p s c", p=128), stg[ss_], "st%d" % ss_, reads=[("stg", ss_, st) for st in range(4)])
                continue
            for oc in range(4):
                b = bank()
                for kh in range(2):
                    s = (f0 + kh) % R
                    for k8 in range(8):
                        kc = kh * 8 + k8
                        self.mm(psum[:, b, :], wblk[s][:, k8, oc * 128:(oc + 1) * 128], hT[hs][:, kc, :],
                                kc == 0, kc == 15, [(("hT", hs), kc // 8, c) for c in (0, 128, 256, 384)] + [("wblk", s)], [("ps", b)])
                flush()
                u = ucnt[0] % 2
                ucnt[0] += 1
                self.tcopy("act", fsb[u], psum[:, b, :], [("ps", b)], [("fsb", u), ("ps", b)])
                if kind == "rope":
                    self.tt("dve", t1[u], psum[:, b, :], rct[hs], ALU.mult, [("ps", b), ("rct", hs)], [("t1", u)])

                    def stage2(u=u, oc=oc, hs=hs, ss_=ss_, dest=dest, ti=ti):
                        b2 = bank()
                        self.mm(psum[:, b2, :], pmat, fsb[u], True, True, ["pmat", ("fsb", u)], [("ps", b2)])
                        self.tt("dve", t2[u], psum[:, b2, :], rst[hs], ALU.mult, [("ps", b2), ("rst", hs)], [("t2", u)])
                        self.tt("pool", stg[ss_][:, oc, :], t1[u], t2[u], ALU.add, [("t1", u), ("t2", u)], [("stg", ss_, oc)])
                        if oc == 3:
                            h0 = dest[1] * 4
                            dd = KT if dest[0] == "K" else QT
                            dst = dd[h0:h0 + 4, :, ti * 512:(ti + 1) * 512]
                            self.dma(dst.rearrange("h p c -> p h c"), stg[ss_], "st%d" % ss_, reads=[("stg", ss_, o_) for o_ in range(4)])
                    pend.append(stage2)
                else:
                    def stage2(u=u, oc=oc, ss_=ss_, dest=dest, r0=r0, blk=blk):
                        g = blk * 4 + oc
                        for sp in range(2):
                            b2 = bank()
                            for q_ in range(2):
                                st = sp * 2 + q_
                                self.mm(psum[:, b2, q_ * 256:(q_ + 1) * 256], fsb[u][:, st * 128:(st + 1) * 128], AB[:, g, :], True, True,
                                        [("fsb", u), ("AB", g)], [("ps", b2)])
                            self.evac(zstg[ss_][:, sp * 2:sp * 2 + 2, oc * 256:(oc + 1) * 256], psum[:, b2, :].rearrange("p (a b) -> p a b", a=2),
                                      [("ps", b2)], [("zstg", ss_, oc, sp)], act_share=1, of=3)
                        if oc == 3:
                            dst = Z[r0:r0 + 512, dest[1] * 1024:(dest[1] + 1) * 1024]
                            self.dma(dst.rearrange("(s p) c -> p s c", p=128), zstg[ss_], "c%d" % ss_,
                                     reads=[("zstg", ss_, o_, p_) for o_ in range(4) for p_ in range(2)])
                    pend.append(stage2)
        flush()

    def phase_B(self, A, psum, gi, S, T, KT, QT, V, MIXT, lam_b, gsb):
        P = self.P
        A.reset()
        nkt = S // 128
        nqb = T // 512
        KTs = [A.alloc([128, S], BF16) for _ in range(2)]
        QTs = [A.alloc([128, T], BF16) for _ in range(2)]
        VA = [A.alloc([128, nkt, 130], BF16) for _ in range(2)]
        pt = [A.alloc([128, 1024], BF16) for _ in range(3)]
        t32 = [A.alloc([128, 128], F32) for _ in range(2)]
        o32 = [A.alloc([128, 128], F32) for _ in range(2)]
        jk = A.alloc([128, 128], F32)
        onb = [A.alloc([128, 4, 128], BF16) for _ in range(2)]
        oTs = [A.alloc([128, 512], BF16) for _ in range(2)]
        sm = [[A.alloc([128, 1], F32) for _ in range(6)] for _ in range(2)]
        for s in range(2):
            va = VA[s]
            P.op("pool", lambda e, va=va: e.memset(va[:, :, 128:129], 1.0), writes=[("VAone", s)])

        def load_head(h):
            s = h % 2
            grp = [self.dma(KTs[s], KT[h], "hd%d" % s, writes=[("KTs", s)]),
                   self.dma(QTs[s], QT[h], "hd%d" % s, writes=[("QTs", s)])]
            step = 16
            for k0 in range(0, nkt, step):
                k1 = min(nkt, k0 + step)
                grp.append(self.dma(VA[s][:, k0:k1, 0:128], V[k0 * 128:k1 * 128, h * 128:(h + 1) * 128].rearrange("(k p) c -> p k c", p=128),
                                    "hd%d" % s, writes=[("VA", s, k0)]))
            self.seal(grp)

        def acc_ap(c, j):
            a = c * 4 + j
            return psum[:, 4 + a // 3, (a % 3) * 129:(a % 3) * 129 + 129], 4 + a // 3, (a % 3 == 0)

        load_head(0)
        fin = [0]
        for h in range(NH):
            s = h % 2
            if h + 1 < NH:
                load_head(h + 1)
            vkeys = [("VA", s, k0) for k0 in range(0, nkt, 16)] + [("VAone", s)]
            for qb in range(nqb):
                qsl = slice(qb * 512, (qb + 1) * 512)

                def ST(kt, c):
                    sl = kt % 2
                    self.mm(psum[:, sl * 2 + c, :], KTs[s][c * 64:(c + 1) * 64, kt * 128:(kt + 1) * 128], QTs[s][c * 64:(c + 1) * 64, qsl],
                            True, True, [("KTs", s), ("QTs", s)], [("pss", sl, c)])

                for kt0 in range(min(2, nkt)):
                    ST(kt0, 0)
                    ST(kt0, 1)
                for kt in range(nkt):
                    sl = kt % 2
                    p3 = kt % 3
                    for c in range(2):
                        self.act(pt[p3][:, c * 512:(c + 1) * 512], psum[:, sl * 2 + c, :], AF.Exp, [("pss", sl, c)], [("pt", p3, c)], scale=0.125)
                    for c in range(2):
                        for j in range(4):
                            ap_, bank, first = acc_ap(c, j)
                            self.mm(ap_, pt[p3][:, c * 512 + j * 128:c * 512 + (j + 1) * 128], VA[s][:, kt, 0:129],
                                    (kt == 0 and first), kt == nkt - 1, [("pt", p3, c), ("VA", s, (kt // 16) * 16), ("VAone", s)], [("acc", bank)], skip=True)
                        if kt + 2 < nkt:
                            ST(kt + 2, c)
                fs = fin[0] % 2
                fin[0] += 1
                for j in range(4):
                    a0, b0, _ = acc_ap(0, j)
                    a1, b1, _ = acc_ap(1, j)
                    u = j % 2
                    r0, r1, r1l, ss, mse, rstd = sm[u]
                    ku = ("smB", u)
                    P.op("dve", lambda e, r0=r0, a0=a0: e.reciprocal(out=r0, in_=a0[:, 128:129]), [("acc", b0)], [(ku, "r0")])
                    P.op("dve", lambda e, r1=r1, a1=a1: e.reciprocal(out=r1, in_=a1[:, 128:129]), [("acc", b1)], [(ku, "r1")])
                    self.tt("dve", r1l, r1, lam_b, ALU.mult, [(ku, "r1"), "lam_b"], [(ku, "r1l")])
                    self.ts(t32[u], a1[:, 0:128], r1l, None, ALU.mult, None, [("acc", b1), (ku, "r1l")], [("t32", u)])
                    self.stt(o32[u], a0[:, 0:128], r0, t32[u], ALU.mult, ALU.subtract, [("acc", b0), (ku, "r0"), ("t32", u)], [("o32", u)])
                    o_ = o32[u]
                    P.op("dve", lambda e, o_=o_, ss=ss: e.scalar_tensor_tensor(out=jk, in0=o_, scalar=1.0, in1=o_,
                                                                             op0=ALU.mult, op1=ALU.mult, accum_out=ss),
                         [("o32", u)], ["jk", (ku, "ss")])
                    self.rstd_from_ss(ss, mse, rstd, 1.0 / 128, 1e-5, (ku, "ss"), (ku, "mse"), (ku, "rstd"))
                    self.stt(onb[fs][:, j, :], o32[u], rstd, gsb, ALU.mult, ALU.mult, [("o32", u), (ku, "rstd"), "gsb"], [("onb", fs, j)])
                    pst = psum[:, 7, :].bitcast(BF16).rearrange("p (a b) -> p a b", a=8)
                    self.tr(pst[:, j, :], onb[fs][:, j, :], [("onb", fs, j)], [("pstB", 0)])
                pst = psum[:, 7, :].bitcast(BF16)
                self.tcopy("dve", oTs[fs], pst[:, 0:512], [("pstB", 0)], [("oTs", fs)])
                self.dma(MIXT[1024 + h * 128:1024 + (h + 1) * 128, qsl], oTs[fs], "st%d" % fs, reads=[("oTs", fs)])

    def phase_C(self, A, psum, gi, S, T, Z, MIXT):
        P = self.P
        A.reset()
        nch = S // 128
        NCH = min(32, nch)
        nfill = nch // NCH
        nkt = T // 128
        Zs = A.alloc([128, nch, 1024], BF16)
        R = 3
        dcs = [A.alloc([128, NCH, 128], BF16) for _ in range(R)]
        dss = [A.alloc([128, NCH, 128], BF16) for _ in range(R)]
        ysb = [A.alloc([128, 512], BF16) for _ in range(2)]
        yT = [A.alloc([128, 4, 128], BF16) for _ in range(2)]
        for half in range(2):
            step = 16
            zk = []
            for k0 in range(0, nch, step):
                k1 = min(nch, k0 + step)
                zk.append(self.dma(Zs[:, k0:k1, :], Z[k0 * 128:k1 * 128, half * 1024:(half + 1) * 1024].rearrange("(k p) c -> p k c", p=128),
                                   "z0", writes=[("Zs", k0)]))
            self.seal(zk)
            fl = [(kt, f) for kt in range(nkt) for f in range(nfill)]
            issued = [0]

            def ensure(upto):
                while issued[0] <= min(upto, len(fl) - 1):
                    i = issued[0]
                    kt, f = fl[i]
                    s = i % R
                    self.seal([self.dma(dcs[s], self.dc[gi][kt, :, f * NCH:(f + 1) * NCH, :], "df%d" % s, writes=[("dcs", s)]),
                               self.dma(dss[s], self.ds[gi][kt, :, f * NCH:(f + 1) * NCH, :], "df%d" % s, writes=[("dss", s)])])
                    issued[0] += 1

            for i, (kt, f) in enumerate(fl):
                ensure(i + R - 1)
                s = i % R
                b = kt % 2
                for n in range(NCH):
                    ncg = f * NCH + n
                    zrow = Zs[:, ncg, :].rearrange("p (g c) -> p g c", g=4)
                    self.mm(psum[:, b, :], dcs[s][:, n, :], zrow[:, :, 0:128], (ncg == 0), False, [("dcs", s), ("Zs", (ncg // 16) * 16)], [("psC", b)])
                    self.mm(psum[:, b, :], dss[s][:, n, :], zrow[:, :, 128:256], False, (ncg == nch - 1), [("dss", s), ("Zs", (ncg // 16) * 16)], [("psC", b)])
                if f == nfill - 1:
                    self.evac(ysb[b], psum[:, b, :], [("psC", b)], [("ysb", b)])
                    pst = psum[:, 6 + b, :].bitcast(BF16).rearrange("p (a b) -> p a b", a=8)
                    for g in range(4):
                        self.tr(pst[:, g, :], ysb[b][:, g * 128:(g + 1) * 128], [("ysb", b)], [("pstC", b)])
                    self.evac(yT[b], pst[:, 0:4, :], [("pstC", b)], [("yT", b)])
                    self.dma(MIXT[half * 512:(half + 1) * 512, kt * 128:(kt + 1) * 128].rearrange("(g p) c -> p g c", p=128), yT[b],
                             "st%d" % b, reads=[("yT", b)])

    def phase_D(self, A, psum, gi, S, T, gvec, WB, MIXT, ones_c):
        P = self.P
        A.reset()
        ntile = T // 512
        xres = A.alloc([128, 4, D], F32)
        actT = A.alloc([128, 16, 512], BF16)
        hT = A.alloc([128, 16, 512], BF16)
        hid = A.alloc([128, 24, 512], BF16)
        qT = [A.alloc([128, 4, 512], BF16) for _ in range(2)]
        R = 6
        wblk = [A.alloc([128, 8, 512], BF16) for _ in range(R)]
        kTm = A.alloc([128, 16, 256], BF16)
        vm = A.alloc([128, 2, D], BF16)
        gb = A.alloc([128, D], F32)
        hbf = [A.alloc([128, D], BF16) for _ in range(2)]
        pT = [A.alloc([128, 2, 512], BF16) for _ in range(2)]
        ocn = [A.alloc([128, 512], BF16) for _ in range(2)]
        sg = [A.alloc([128, 512], BF16) for _ in range(2)]
        sms = [(A.alloc([128, 1], F32), A.alloc([128, 1], F32), A.alloc([128, 1], F32), ("smD", i)) for i in range(2)]
        rl = [A.alloc([128, 1], F32) for _ in range(2)]
        memT = A.alloc([128, 16, 256], BF16)

        def blockfills(name, blk, kc0=0, kc1=16):
            out = []
            k = kc0
            while k < kc1:
                n = min(8, kc1 - k)
                out.append((name, blk, k, n))
                k += n
            return out

        fills = []
        for blk in range(4):
            fills += blockfills("w_ck", blk)
        for blk in range(4):
            fills += blockfills("w_cv", blk)
        halves = [(0, 6), (6, 11)]
        dstop = getattr(self, "dstop", 99)
        for ti in range(ntile):
            for blk in range(4):
                fills += blockfills("w_out", blk)
            if dstop <= 2:
                continue
            for blk in range(4):
                fills += blockfills("w_cq", blk)
            for blk in range(4):
                fills += blockfills("w_co", blk)
            if dstop <= 5:
                continue
            for (b0, b1) in halves:
                for blk in range(b0, b1):
                    fills += blockfills("w_gate", blk)
                    fills += blockfills("w_up", blk)
                for cb in range(4):
                    fills += blockfills("w_down", cb, b0 * 4, b1 * 4)
        issued = [0]
        used = [0]

        def ensure(upto):
            while issued[0] <= min(upto, len(fills) - 1):
                i = issued[0]
                name, blk, k0, n = fills[i]
                s = i % R
                self.dma(wblk[s][:, 0:n, :], WB[name][blk, :, k0:k0 + n, :], "w%d" % s, writes=[("wblk", s)])
                issued[0] += 1

        def next_fill(expect):
            i = used[0]
            assert fills[i][0] == expect, (fills[i], expect)
            ensure(i + R - 4)
            used[0] += 1
            return i % R, fills[i]

        self.dma(gb, gvec[2, :].partition_broadcast(128), "misc", writes=["gb"])
        for mt in range(2):
            self.dma(xres[:, mt, :], self.mem[gi][mt * 128:(mt + 1) * 128, :], "x%d" % mt, writes=[("xres", mt)])
            self.norm_transpose(psum, xres[:, mt, :], ("xres", mt), gb, sms[mt], hbf[mt], ("hbf", mt), memT, "memT", mt * 128, (6, 7))
        mk = [("memT", h, c) for h in range(2) for c in (0, 128)]
        evc = 0
        for blk in range(4):
            f = [next_fill("w_ck"), next_fill("w_ck")]
            for oc in range(4):
                b = (blk * 4 + oc) % 4
                for kh in range(2):
                    s = f[kh][0]
                    for k8 in range(8):
                        kc = kh * 8 + k8
                        self.mm(psum[:, b, 0:256], wblk[s][:, k8, oc * 128:(oc + 1) * 128], memT[:, kc, :], kc == 0, kc == 15,
                                [("wblk", s)] + [("memT", kc // 8, c) for c in (0, 128)], [("ps", b)])
                self.evac(kTm[:, blk * 4 + oc, :], psum[:, b, 0:256], [("ps", b)], [("kTm", blk * 4 + oc)])
        for blk in range(4):
            f = [next_fill("w_cv"), next_fill("w_cv")]
            for mt in range(2):
                b = (blk * 2 + mt) % 4
                for kh in range(2):
                    s = f[kh][0]
                    for k8 in range(8):
                        kc = kh * 8 + k8
                        self.mm(psum[:, b, :], memT[:, kc, mt * 128:(mt + 1) * 128], wblk[s][:, k8, :], kc == 0, kc == 15,
                                [("wblk", s), ("memT", kc // 8, mt * 128)], [("ps", b)])
                self.evac(vm[:, mt, blk * 512:(blk + 1) * 512], psum[:, b, :], [("ps", b)], [("vm", mt, blk)])

        def tokmajor_accum(name, lhs, lhs_keys, nk0, nk1):
            for cb in range(4):
                k = nk0
                while k < nk1:
                    s, (nm, blk, k0, n) = next_fill(name)
                    assert blk == cb and k0 == k
                    for st in range(4):
                        for k8 in range(n):
                            kc = k + k8
                            self.mm(psum[:, st, :], lhs[:, kc - lhs_base[0], st * 128:(st + 1) * 128], wblk[s][:, k8, :], kc == nk0, kc == nk1 - 1,
                                    [("wblk", s)] + lhs_keys(kc, st), [("ps", st)])
                    k += n
                for st in range(4):
                    xs = xres[:, st, cb * 512:(cb + 1) * 512]
                    self.tt("dve", xs, psum[:, st, :], xs, ALU.add, [("ps", st), ("xres", st)], [("xres", st)])

        lhs_base = [0]

        def norm_all(grow):
            self.dma(gb, gvec[grow, :].partition_broadcast(128), "misc", writes=["gb"])
            for st in range(4):
                self.norm_transpose(psum, xres[:, st, :], ("xres", st), gb, sms[st % 2], hbf[st % 2], ("hbf", st % 2), hT, "hT", st * 128, (6, 7))

        hTk = lambda kc: [("hT", kc // 8, c) for c in (0, 128, 256, 384)]
        cross_scale = 512 ** -0.5

        for ti in range(ntile):
            r0 = ti * 512
            self.dma(xres, self.x[gi][r0:r0 + 512, :].rearrange("(s p) c -> p s c", p=128), "x0", writes=[("xres", st) for st in range(4)])
            self.dma(actT, MIXT[:, r0:r0 + 512].rearrange("(k p) c -> p k c", p=128), "x1", writes=[("actT", kc) for kc in range(16)])
            lhs_base[0] = 0
            tokmajor_accum("w_out", actT, lambda kc, st: [("actT", kc)], 0, 16)
            def dump():
                self.dma(self.y[gi][r0:r0 + 512, :].rearrange("(s p) c -> p s c", p=128), xres, "out%d" % (ti % 2),
                         reads=[("xres", st) for st in range(4)])
            if dstop <= 1:
                dump()
                continue
            norm_all(1)
            if dstop == 2:
                if ti == 0 and gi == 0:
                    dh = self.nc.dram_tensor("dbg_hT", [128, 16, 512], BF16, kind="ExternalOutput").ap()
                    self.dma(dh, hT, "misc", reads=[("hT", h_, c_) for h_ in range(2) for c_ in (0, 128, 256, 384)])
                    dm_ = self.nc.dram_tensor("dbg_memT", [128, 16, 256], BF16, kind="ExternalOutput").ap()
                    self.dma(dm_, memT, "misc", reads=[("memT", h_, c_) for h_ in range(2) for c_ in (0, 128)])
                    dk = self.nc.dram_tensor("dbg_kTm", [128, 16, 256], BF16, kind="ExternalOutput").ap()
                    self.dma(dk, kTm, "misc", reads=[("kTm", i_) for i_ in range(16)])
                    dv = self.nc.dram_tensor("dbg_vm", [128, 2, 2048], BF16, kind="ExternalOutput").ap()
                    self.dma(dv, vm, "misc", reads=[("vm", m_, b_) for m_ in range(2) for b_ in range(4)])
                dump()
                continue
            for hh in range(4):
                f = [next_fill("w_cq"), next_fill("w_cq")]
                qs = hh % 2
                for oc in range(4):
                    b = oc
                    for kh in range(2):
                        s = f[kh][0]
                        for k8 in range(8):
                            kc = kh * 8 + k8
                            self.mm(psum[:, b, :], wblk[s][:, k8, oc * 128:(oc + 1) * 128], hT[:, kc, :], kc == 0, kc == 15,
                                    [("wblk", s)] + hTk(kc), [("ps", b)])
                    self.evac(qT[qs][:, oc, :], psum[:, b, :], [("ps", b)], [("qT", qs, oc)])
                ps_ = hh % 2
                for mt in range(2):
                    b = 4 + mt
                    for dc_ in range(4):
                        self.mm(psum[:, b, :], kTm[:, hh * 4 + dc_, mt * 128:(mt + 1) * 128], qT[qs][:, dc_, :], dc_ == 0, dc_ == 3,
                                [("kTm", hh * 4 + dc_), ("qT", qs, dc_)], [("ps", b)])
                    self.act(pT[ps_][:, mt, :], psum[:, b, :], AF.Exp, [("ps", b)], [("pT", ps_, mt)], scale=cross_scale)
                for st in range(4):
                    b = st % 2
                    for mt in range(2):
                        self.mm(psum[:, b, :], pT[ps_][:, mt, st * 128:(st + 1) * 128], vm[:, mt, hh * 512:(hh + 1) * 512], mt == 0, mt == 1,
                                [("pT", ps_, mt), ("vm", mt, hh)], [("ps", b)])
                    lb = 2 + (st % 2)
                    for mt in range(2):
                        self.mm(psum[:, lb, 0:1], pT[ps_][:, mt, st * 128:(st + 1) * 128], ones_c, mt == 0, mt == 1,
                                [("pT", ps_, mt), "ones_c"], [("ps", lb)])
                    rl_ = rl[st % 2]
                    lsrc = psum[:, lb, 0:1]
                    P.op("dve", lambda e, rl_=rl_, lsrc=lsrc: e.reciprocal(out=rl_, in_=lsrc), [("ps", lb)], [("rl", st % 2)])
                    self.ts(ocn[st % 2], psum[:, b, :], rl_, None, ALU.mult, None, [("ps", b), ("rl", st % 2)], [("ocn", st % 2)])
                    tb = 6 + (st % 2)
                    pst = psum[:, tb, :].bitcast(BF16).rearrange("p (a b) -> p a b", a=8)
                    for dc_ in range(4):
                        self.tr(pst[:, dc_, :], ocn[st % 2][:, dc_ * 128:(dc_ + 1) * 128], [("ocn", st % 2)], [("ps", tb)])
                    self.evac(actT[:, hh * 4:(hh + 1) * 4, st * 128:(st + 1) * 128], pst[:, 0:4, :], [("ps", tb)],
                              [("actT", hh * 4 + d_) for d_ in range(4)])
            tokmajor_accum("w_co", actT, lambda kc, st: [("actT", kc)], 0, 16)
            if dstop <= 5:
                dump()
                continue
            norm_all(3)
            for (b0, b1) in halves:
                for blk in range(b0, b1):
                    fg = [next_fill("w_gate"), next_fill("w_gate")]
                    fu = [next_fill("w_up"), next_fill("w_up")]
                    for oc in range(4):
                        bg = (oc % 2) * 2
                        bu = bg + 1
                        for (ff, bb) in ((fg, bg), (fu, bu)):
                            for kh in range(2):
                                s = ff[kh][0]
                                for k8 in range(8):
                                    kc = kh * 8 + k8
                                    self.mm(psum[:, 4 + bb, :], wblk[s][:, k8, oc * 128:(oc + 1) * 128], hT[:, kc, :], kc == 0, kc == 15,
                                            [("wblk", s)] + hTk(kc), [("ps", 4 + bb)])
                        sgs = oc % 2
                        self.act(sg[sgs], psum[:, 4 + bg, :], AF.Silu, [("ps", 4 + bg)], [("sg", sgs)])
                        hc = (blk - b0) * 4 + oc
                        self.tt("dve", hid[:, hc, :], psum[:, 4 + bu, :], sg[sgs], ALU.mult, [("ps", 4 + bu), ("sg", sgs)], [("hid", hc)])
                lhs_base[0] = b0 * 4
                tokmajor_accum("w_down", hid, lambda kc, st: [("hid", kc - lhs_base[0])], b0 * 4, b1 * 4)
            if dstop <= 8:
                dump()
                continue
            self.dma(gb, gvec[4, :].partition_broadcast(128), "misc", writes=["gb"])
            for st in range(4):
                ss, mse, rstd, ksm = sms[st % 2]
                hb = hbf[st % 2]
                self.act(hb, xres[:, st, :], AF.Square, [("xres", st)], [("hbf", st % 2), (ksm, "ss")], accum=ss)
                self.rstd_from_ss(ss, mse, rstd, 1.0 / D, 1e-6, (ksm, "ss"), (ksm, "mse"), (ksm, "rstd"))
                self.stt(xres[:, st, :], xres[:, st, :], rstd, gb, ALU.mult, ALU.mult, [("xres", st), (ksm, "rstd"), "gb"], [("xres", st)])
            self.dma(self.y[gi][r0:r0 + 512, :].rearrange("(s p) c -> p s c", p=128), xres, "out%d" % (ti % 2),
                     reads=[("xres", st) for st in range(4)])
        assert used[0] == len(fills), (used[0], len(fills))


_TABLE_CACHE = {}


def _perm(S, own0, T):
    own = np.arange(own0, own0 + T)
    rest = np.concatenate([np.arange(0, own0), np.arange(own0 + T, S)])
    return np.concatenate([own, rest]).astype(np.int64)


def _rope_tables(S, perm):
    inv = (1.0 / (np.float32(10000.0) ** (np.arange(0, 64, 2, dtype=np.float32) / np.float32(64)))).astype(np.float32)
    pos = perm.astype(np.float32)
    ang = (pos[:, None] * inv[None, :]).astype(np.float32)
    ang = np.concatenate([ang, ang], axis=1)
    c = np.cos(ang).astype(np.float32)
    s = np.sin(ang).astype(np.float32)
    sgn = np.concatenate([-np.ones(32, np.float32), np.ones(32, np.float32)])
    s = s * sgn[None, :]
    cT = np.ascontiguousarray(np.concatenate([c, c], axis=1).T)
    sT = np.ascontiguousarray(np.concatenate([s, s], axis=1).T)
    return cT, sT


def _dft_tables(S, perm, own0, T):
    key = (S, own0, T)
    if key in _TABLE_CACHE:
        return _TABLE_CACHE[key]
    k = np.arange(own0, own0 + T, dtype=np.int64)
    scale = 1.0 / np.sqrt(S * 128.0)
    nchunk = S // 128
    dc = np.empty((T // 128, 128, nchunk, 128), dtype=bf16_np)
    ds = np.empty((T // 128, 128, nchunk, 128), dtype=bf16_np)
    tab_c = (np.cos(2 * np.pi * np.arange(S) / S) * scale).astype(np.float32)
    tab_s = (-np.sin(2 * np.pi * np.arange(S) / S) * scale).astype(np.float32)
    n2 = perm.reshape(nchunk, 128)
    for kt in range(T // 128):
        kk = k[kt * 128:(kt + 1) * 128]
        m = (n2[:, :, None] * kk[None, None, :]) % S
        dc[kt] = tab_c[m].transpose(1, 0, 2).astype(bf16_np)
        ds[kt] = tab_s[m].transpose(1, 0, 2).astype(bf16_np)
    _TABLE_CACHE[key] = (dc, ds)
    return dc, ds


def _shared_inputs(inp):
    f32 = np.float32
    w_in = np.ascontiguousarray(np.asarray(inp["w_in"], f32)[0])
    m = np.arange(128)
    pm = np.zeros((128, 128), np.float32)
    pm[(m // 64) * 64 + ((m % 64) + 32) % 64, m] = 1.0
    c = np.arange(128)
    ang = 2 * np.pi * np.outer(c, c) / 128.0
    ccsc = np.concatenate([np.cos(ang), np.sin(ang)], axis=1).astype(f32)
    lam = np.concatenate([np.asarray(inp[n], f32)[0] for n in ("lambda_q1", "lambda_k1", "lambda_q2", "lambda_k2")])[None, :]
    gvec = np.stack([np.asarray(inp["g_mix"], f32)[0], np.asarray(inp["g_cross"], f32)[0], np.asarray(inp["g_mem"], f32)[0],
                     np.asarray(inp["g_ffn"], f32)[0], np.asarray(inp["g_final"], f32)])
    sh = {
        "w_in": w_in, "pmat": pm.astype(bf16_np),
        "w_f": np.ascontiguousarray(np.asarray(inp["w_fourier"], f32)[0]),
        "ccsc": np.ascontiguousarray(ccsc), "lam": np.ascontiguousarray(lam), "gvec": np.ascontiguousarray(gvec),
        "g_sub": np.ascontiguousarray(np.asarray(inp["g_subln"], f32)[0]),
        "ident": np.eye(128, dtype=np.float32).astype(bf16_np),
    }
    for n in ("w_out", "w_cq", "w_ck", "w_cv", "w_co", "w_gate", "w_up", "w_down"):
        sh[n] = np.ascontiguousarray(np.asarray(inp[n], f32)[0])
    return sh


def _group_inputs(gi, xseq, memseq, own0, T):
    S = xseq.shape[0]
    perm = _perm(S, own0, T)
    cT, sT = _rope_tables(S, perm)
    dc, ds = _dft_tables(S, perm, own0, T)
    return {
        "x%d" % gi: np.ascontiguousarray(xseq[perm]),
        "mem%d" % gi: np.ascontiguousarray(memseq),
        "rc%d" % gi: cT, "rs%d" % gi: sT, "dc%d" % gi: dc, "ds%d" % gi: ds,
    }


_NC_CACHE = {}


def _get_nc(groups):
    key = tuple(groups)
    if key not in _NC_CACHE:
        _NC_CACHE[key] = Builder(list(groups)).build()
    return _NC_CACHE[key]


def kernel(**inputs):
    f32 = np.float32
    xp = np.asarray(inputs["x_prompt"], f32)
    xs = np.asarray(inputs["x_sample"], f32)
    mp = np.asarray(inputs["mem_prompt"], f32)
    ms = np.asarray(inputs["mem_sample"], f32)
    B, S0, _ = xp.shape
    B1, S1, _ = xs.shape
    n = 8
    T0 = S0 * B // n
    T1 = S1 * B1 // n
    cp = n // B
    cs = n // B1
    groups = ((S0, T0), (S1, T1))
    nc = _get_nc(groups)
    sh = _shared_inputs(inputs)
    in_maps = []
    for c in range(n):
        m = dict(sh)
        m.update(_group_inputs(0, xp[c // cp], mp[c // cp], (c % cp) * T0, T0))
        m.update(_group_inputs(1, xs[c // cs], ms[c // cs], (c % cs) * T1, T1))
        in_maps.append(m)
    res = run_bass_kernel_spmd(nc, in_maps, core_ids=list(range(n)))
    yp = np.empty((B, S0, D), f32)
    ys = np.empty((B1, S1, D), f32)
    for c in range(n):
        r = res.results[c]
        yp[c // cp, (c % cp) * T0:(c % cp + 1) * T0] = r["y0"]
        ys[c // cs, (c % cs) * T1:(c % cs + 1) * T1] = r["y1"]
    return yp, ys
```

```python
import numpy as np
import ml_dtypes
import concourse.bass as bass
import concourse.mybir as mybir
from concourse.bass_utils import run_bass_kernel_spmd
from concourse.alu_op_type import AluOpType as ALU

F32 = mybir.dt.float32
BF16 = mybir.dt.bfloat16
AF = mybir.ActivationFunctionType
bf16_np = ml_dtypes.bfloat16

D = 2048
DFF = 5632
NH = 8
NMEM = 256
ENGS = ("pe", "act", "dve", "pool", "sp")


class Instr:
    __slots__ = ("eng", "fn", "raw", "other", "dma_sem", "dma_val", "milestone", "tick", "extra_waits")

    def __init__(self, eng, fn):
        self.eng = eng
        self.fn = fn
        self.raw = []
        self.other = []
        self.dma_sem = None
        self.dma_val = 0
        self.milestone = False
        self.tick = 0
        self.extra_waits = []


class Prog:
    def __init__(self, nc):
        self.nc = nc
        self.streams = {e: [] for e in ENGS}
        self.last_writer = {}
        self.readers = {}
        self.dma_count = {}
        self.pending = {e: [] for e in ENGS}

    def op(self, eng, fn, reads=(), writes=(), dma_sem=None):
        ins = Instr(eng, fn)
        if dma_sem is not None:
            c = self.dma_count.get(dma_sem, 0) + 16
            self.dma_count[dma_sem] = c
            ins.dma_sem = dma_sem
            ins.dma_val = c
        raw = {}
        oth = {}
        for k in reads:
            w = self.last_writer.get(k)
            if w is not None:
                raw[id(w)] = w
        for k in writes:
            w = self.last_writer.get(k)
            if w is not None:
                oth[id(w)] = w
            rd = self.readers.get(k)
            if rd:
                for r in rd.values():
                    oth[id(r)] = r
        ins.raw = list(raw.values())
        ins.other = [d for i, d in oth.items() if i not in raw]
        if self.pending[eng]:
            ins.extra_waits = self.pending[eng]
            self.pending[eng] = []
        for k in writes:
            self.last_writer[k] = ins
            self.readers[k] = {}
        rk = (eng, dma_sem)
        for k in reads:
            self.readers.setdefault(k, {})[rk] = ins
        self.streams[eng].append(ins)
        return ins

    def barrier(self):
        lasts = []
        for e in ENGS:
            if e == "sp":
                continue
            if self.streams[e]:
                l = self.streams[e][-1]
                l.milestone = True
                lasts.append(l)
        dm = list(self.dma_count.items())
        for e in ENGS:
            self.pending[e] = self.pending[e] + [("ins", l) for l in lasts if l.eng != e] + [("dma", k, v) for k, v in dm]
        self.last_writer = {}
        self.readers = {}

    def emit(self, sems_eng, sems_dma):
        nc = self.nc
        for e in ENGS:
            for ins in self.streams[e]:
                for d in ins.raw:
                    if d.dma_sem is None and not (d.eng == e and e == "pe"):
                        d.milestone = True
                for d in ins.other:
                    if d.dma_sem is None and d.eng != e:
                        d.milestone = True
        for e in ENGS:
            t = 0
            for ins in self.streams[e]:
                if ins.milestone:
                    t += 1
                    ins.tick = t

        def run(e, eng):
            waited = {}
            for ins in self.streams[e]:
                need = {}
                for d in ins.raw:
                    if d.dma_sem is not None:
                        sk, v = ("d", d.dma_sem), d.dma_val
                    elif d.eng == e and e == "pe":
                        continue
                    else:
                        sk, v = ("e", d.eng), d.tick
                    if need.get(sk, 0) < v:
                        need[sk] = v
                for d in ins.other:
                    if d.dma_sem is not None:
                        sk, v = ("d", d.dma_sem), d.dma_val
                    elif d.eng != e:
                        sk, v = ("e", d.eng), d.tick
                    else:
                        continue
                    if need.get(sk, 0) < v:
                        need[sk] = v
                for w in ins.extra_waits:
                    if w[0] == "ins":
                        sk, v = ("e", w[1].eng), w[1].tick
                    else:
                        sk, v = ("d", w[1]), w[2]
                    if need.get(sk, 0) < v:
                        need[sk] = v
                for sk, val in need.items():
                    if waited.get(sk, 0) >= val:
                        continue
                    waited[sk] = val
                    sem = sems_eng[sk[1]] if sk[0] == "e" else sems_dma[sk[1]]
                    eng.wait_ge(sem, val)
                r = ins.fn(eng)
                if ins.dma_sem is not None:
                    r.then_inc(sems_dma[ins.dma_sem], 16)
                elif ins.milestone:
                    r.then_inc(sems_eng[e], 1)
            if e == "sp":
                for k, v in self.dma_count.items():
                    if waited.get(("d", k), 0) < v:
                        eng.wait_ge(sems_dma[k], v)

        with nc.Block() as block:
            @block.tensor
            def _(eng):
                run("pe", eng)

            @block.scalar
            def _(eng):
                run("act", eng)

            @block.vector
            def _(eng):
                run("dve", eng)

            @block.gpsimd
            def _(eng):
                run("pool", eng)

            @block.sync
            def _(eng):
                run("sp", eng)


class Arena:
    def __init__(self, ap, nbytes):
        self.ap = ap
        self.cap = nbytes
        self.off = 0

    def reset(self, base=0):
        self.off = base

    def alloc(self, shape, dt):
        n = 1
        for s in shape[1:]:
            n *= s
        nb = n * (4 if dt == F32 else 2)
        a = self.ap[:, self.off // 2:(self.off + nb) // 2]
        self.off += (nb + 63) // 64 * 64
        assert self.off <= self.cap, ("SBUF arena overflow", self.off, self.cap)
        if dt == F32:
            a = a.bitcast(F32)
        if len(shape) == 3:
            a = a.rearrange("p (a b) -> p a b", a=shape[1])
        return a


DMA_SEMS = (["w%d" % i for i in range(6)] + ["x0", "x1", "rt0", "rt1", "st0", "st1", "st2", "st3",
            "c0", "c1", "c2", "hd0", "hd1", "z0", "df0", "df1", "df2", "misc", "out0", "out1", "cv0", "cv1", "cs0", "cs1"])


class Builder:
    def __init__(self, groups, stop_after=None, debug=()):
        self.groups = groups
        self.stop_after = stop_after
        self.debug = debug
        self.nc = bass.Bass("TRN2", target_bir_lowering=False)
        self.evc = 0

    def dma(self, out, in_, sem, reads=(), writes=()):
        return self.P.op("sp", lambda e: e.dma_start(out=out, in_=in_), reads, writes, dma_sem=sem)

    @staticmethod
    def seal(instrs):
        v = max(i.dma_val for i in instrs)
        for i in instrs:
            i.dma_val = v

    def mm(self, out, lhsT, rhs, start, stop, reads, writes, skip=False):
        if skip:
            return self.P.op("pe", lambda e: e.matmul(out, lhsT=lhsT, rhs=rhs, start=start, stop=stop, skip_group_check=True), reads, writes)
        return self.P.op("pe", lambda e: e.matmul(out, lhsT=lhsT, rhs=rhs, start=start, stop=stop), reads, writes)

    def tr(self, out, in_, reads, writes):
        ident = self.ident
        return self.P.op("pe", lambda e: e.transpose(out, in_, ident), list(reads) + ["ident"], writes)

    def act(self, out, in_, func, reads, writes, scale=None, accum=None):
        kw = {}
        if scale is not None:
            kw["scale"] = scale
        if accum is not None:
            kw["accum_out"] = accum
        return self.P.op("act", lambda e: e.activation(out=out, in_=in_, func=func, **kw), reads, writes)

    def tcopy(self, eng, out, in_, reads, writes):
        if eng == "act":
            return self.P.op("act", lambda e: e.activation(out=out, in_=in_, func=AF.Copy), reads, writes)
        return self.P.op(eng, lambda e: e.tensor_copy(out=out, in_=in_), reads, writes)

    def evac(self, out, in_, reads, writes, act_share=1, of=2):
        self.evc += 1
        eng = "act" if (self.evc % of) < act_share else "dve"
        return self.tcopy(eng, out, in_, reads, writes)

    def tt(self, eng, out, in0, in1, op, reads, writes):
        return self.P.op(eng, lambda e: e.tensor_tensor(out=out, in0=in0, in1=in1, op=op), reads, writes)

    def ts(self, out, in0, s1, s2, op0, op1, reads, writes, eng="dve"):
        if op1 is None:
            return self.P.op(eng, lambda e: e.tensor_scalar(out=out, in0=in0, scalar1=s1, scalar2=None, op0=op0), reads, writes)
        return self.P.op(eng, lambda e: e.tensor_scalar(out=out, in0=in0, scalar1=s1, scalar2=s2, op0=op0, op1=op1), reads, writes)

    def stt(self, out, in0, scalar, in1, op0, op1, reads, writes):
        return self.P.op("dve", lambda e: e.scalar_tensor_tensor(out=out, in0=in0, scalar=scalar, in1=in1, op0=op0, op1=op1), reads, writes)

    def rstd_from_ss(self, ss, mse, rstd, inv_n, eps, kss, kmse, krstd):
        self.ts(mse, ss, inv_n, eps, ALU.mult, ALU.add, [kss], [kmse])
        nh = self.negh
        self.P.op("pool", lambda e: e.tensor_tensor(out=rstd, in0=mse, in1=nh, op=ALU.pow), [kmse, "negh"], [krstd])

    def build(self):
        nc = self.nc
        dt = nc.dram_tensor
        G = self.groups
        self.din = {}

        def inp(name, shape, dtype=F32):
            self.din[name] = dt(name, list(shape), dtype, kind="ExternalInput").ap()
            return self.din[name]

        def scr(name, shape, dtype=BF16):
            return dt(name, list(shape), dtype, kind="Internal").ap()

        self.x = [inp("x%d" % g, [S, D]) for g, (S, T) in enumerate(G)]
        self.mem = [inp("mem%d" % g, [NMEM, D]) for g in range(len(G))]
        self.rc = [inp("rc%d" % g, [128, S]) for g, (S, T) in enumerate(G)]
        self.rs = [inp("rs%d" % g, [128, S]) for g, (S, T) in enumerate(G)]
        self.dc = [inp("dc%d" % g, [T // 128, 128, S // 128, 128], BF16) for g, (S, T) in enumerate(G)]
        self.ds = [inp("ds%d" % g, [T // 128, 128, S // 128, 128], BF16) for g, (S, T) in enumerate(G)]
        self.y = [dt("y%d" % g, [T, D], F32, kind="ExternalOutput").ap() for g, (S, T) in enumerate(G)]
        w_in = inp("w_in", [D, 4096])
        w_f = inp("w_f", [8, 128, 128])
        ccsc = inp("ccsc", [128, 256])
        lam_in = inp("lam", [1, 256])
        gvec = inp("gvec", [5, D])
        g_sub = inp("g_sub", [128])
        identd = inp("ident", [128, 128], BF16)
        pmatd = inp("pmat", [128, 128], BF16)
        wsrc = {n: inp(n, [D, D]) for n in ("w_out", "w_cq", "w_ck", "w_cv", "w_co")}
        wsrc["w_gate"] = inp("w_gate", [D, DFF])
        wsrc["w_up"] = inp("w_up", [D, DFF])
        wsrc["w_down"] = inp("w_down", [DFF, D])
        WA = scr("WA", [8, 128, 16, 512])
        WB = {n: scr("S_" + n, [4, 128, 16, 512]) for n in ("w_out", "w_cq", "w_ck", "w_cv", "w_co")}
        WB["w_gate"] = scr("S_w_gate", [11, 128, 16, 512])
        WB["w_up"] = scr("S_w_up", [11, 128, 16, 512])
        WB["w_down"] = scr("S_w_down", [4, 128, 44, 512])
        lam_scr = scr("lam_scr", [1, 1], F32)
        KT = [scr("KT%d" % g, [NH, 128, S]) for g, (S, T) in enumerate(G)]
        QT = [scr("QT%d" % g, [NH, 128, T]) for g, (S, T) in enumerate(G)]
        V = [scr("V%d" % g, [S, 1024]) for g, (S, T) in enumerate(G)]
        Z = [scr("Z%d" % g, [S, 2048]) for g, (S, T) in enumerate(G)]
        MIXT = [scr("MIXT%d" % g, [D, T]) for g, (S, T) in enumerate(G)]

        ARENA_BYTES = 199 * 1024
        with (
            nc.sbuf_tensor("arena", [128, ARENA_BYTES // 2], BF16) as arena_t,
            nc.sbuf_tensor("consts", [128, 3072], BF16) as consts_t,
            nc.psum_tensor("psum", [128, 8, 512], F32) as psum,
        ):
            import contextlib
            with contextlib.ExitStack() as es:
                sems_eng = {e: es.enter_context(nc.semaphore("se_" + e)) for e in ENGS}
                sems_dma = {k: es.enter_context(nc.semaphore("sd_" + k)) for k in DMA_SEMS}
                self.P = P = Prog(nc)
                A = Arena(arena_t, ARENA_BYTES)
                C = Arena(consts_t, 6144)
                self.ident = C.alloc([128, 128], BF16)
                self.negh = C.alloc([128, 1], F32)
                lam_b = C.alloc([128, 1], F32)
                gsb = C.alloc([128, 128], F32)
                ones_c = C.alloc([128, 1], BF16)
                self.pmat = C.alloc([128, 128], BF16)
                self.AB = C.alloc([128, 8, 256], BF16)
                grp0 = [self.dma(self.ident, identd, "misc", writes=["ident"])]
                negh = self.negh
                P.op("pool", lambda e: e.memset(negh, -0.5), writes=["negh"])
                P.op("pool", lambda e: e.memset(ones_c, 1.0), writes=["ones_c"])

                def ps_bank(b):
                    return psum[:, b, :]

                def ps_bf(b):
                    return psum[:, b, :].bitcast(BF16).rearrange("p (a b) -> p a b", a=8)

                CONV_W = 3072
                A.reset()
                cw32 = [A.alloc([128, CONV_W], F32) for _ in range(2)]
                cw16 = [A.alloc([128, CONV_W], BF16) for _ in range(2)]
                self.conv_bytes = A.off
                lam_sb = A.alloc([128, 256], F32)
                gs_raw = A.alloc([128, 128], F32)
                cc32 = A.alloc([128, 256], F32)
                wf32 = A.alloc([128, 8, 128], F32)
                grp0.append(self.dma(self.pmat, pmatd, "misc", writes=["pmat"]))
                grp0.append(self.dma(lam_sb[0:1, :], lam_in, "misc", writes=["lam_sb"]))
                grp0.append(self.dma(gs_raw, g_sub.partition_broadcast(128), "misc", writes=["gs_raw"]))
                grp0.append(self.dma(cc32, ccsc, "misc", writes=["cc32"]))
                grp0.append(self.dma(wf32, w_f.rearrange("g c d -> c g d"), "misc", writes=["wf32"]))
                self.seal(grp0)
                tasks = []
                for kc in range(16):
                    rows = slice(kc * 128, (kc + 1) * 128)
                    tasks.append((w_in[rows, 0:2048], 2048,
                                  [(WA[0:2, :, kc, :].rearrange("b p c -> p b c"), 0, 1024),
                                   (WA[6:8, :, kc, :].rearrange("b p c -> p b c"), 1024, 1024)]))
                    tasks.append((w_in[rows, 2048:4096], 2048,
                                  [(WA[2:4, :, kc, :].rearrange("b p c -> p b c"), 0, 1024),
                                   (WA[4:6, :, kc, :].rearrange("b p c -> p b c"), 1024, 1024)]))
                n_fore = len(tasks)
                for kc in range(16):
                    rows = slice(kc * 128, (kc + 1) * 128)
                    for n in ("w_out", "w_cq", "w_ck", "w_cv", "w_co"):
                        tasks.append((wsrc[n][rows, :], 2048, [(WB[n][:, :, kc, :].rearrange("b p c -> p b c"), 0, 2048)]))
                    for n in ("w_gate", "w_up"):
                        tasks.append((wsrc[n][rows, 0:3072], 3072, [(WB[n][0:6, :, kc, :].rearrange("b p c -> p b c"), 0, 3072)]))
                        tasks.append((wsrc[n][rows, 3072:DFF], 2560, [(WB[n][6:11, :, kc, :].rearrange("b p c -> p b c"), 0, 2560)]))
                for kc in range(44):
                    rows = slice(kc * 128, (kc + 1) * 128)
                    tasks.append((wsrc["w_down"][rows, :], 2048, [(WB["w_down"][:, :, kc, :].rearrange("b p c -> p b c"), 0, 2048)]))
                cstate = {"i": 0, "loaded": 0, "stored": 0}

                def conv_store(i):
                    for (dst, c0, nsub) in tasks[i][2]:
                        self.dma(dst, cw16[i % 2][:, c0:c0 + nsub].rearrange("p (b c) -> p b c", c=512), "cs%d" % (i % 2),
                                 reads=[("cw16", i % 2)])

                def conv_step(n, engines):
                    for _ in range(n):
                        i = cstate["i"]
                        if i >= len(tasks):
                            break
                        while cstate["loaded"] <= min(i + 1, len(tasks) - 1):
                            l = cstate["loaded"]
                            self.dma(cw32[l % 2][:, 0:tasks[l][1]], tasks[l][0], "cv%d" % (l % 2), writes=[("cw32", l % 2)])
                            cstate["loaded"] += 1
                        ncols = tasks[i][1]
                        self.tcopy(engines[i % len(engines)], cw16[i % 2][:, 0:ncols], cw32[i % 2][:, 0:ncols], [("cw32", i % 2)], [("cw16", i % 2)])
                        while cstate["stored"] < i:
                            conv_store(cstate["stored"])
                            cstate["stored"] += 1
                        cstate["i"] += 1
                    if cstate["i"] >= len(tasks):
                        while cstate["stored"] < len(tasks):
                            conv_store(cstate["stored"])
                            cstate["stored"] += 1

                def conv_flush():
                    while cstate["stored"] < cstate["i"]:
                        conv_store(cstate["stored"])
                        cstate["stored"] += 1

                self.conv_step = conv_step
                self.conv_left = lambda: len(tasks) - cstate["i"]
                conv_step(n_fore, ("dve", "act", "pool"))
                conv_flush()

                lj = A.alloc([128, 64], F32)
                lsum = A.alloc([128, 2], F32)
                lexp = A.alloc([128, 2], F32)
                lval = A.alloc([128, 1], F32)
                for i in range(2):
                    a0 = lam_sb[0:1, i * 128:i * 128 + 64]
                    a1 = lam_sb[0:1, i * 128 + 64:i * 128 + 128]
                    acc_ = lsum[0:1, i:i + 1]
                    P.op("dve", lambda e, a0=a0, a1=a1, acc_=acc_: e.scalar_tensor_tensor(
                        out=lj[0:1, :], in0=a0, scalar=1.0, in1=a1, op0=ALU.mult, op1=ALU.mult, accum_out=acc_),
                        ["lam_sb"], ["lj", ("lsum", i)])
                self.act(lexp[0:1, :], lsum[0:1, :], AF.Exp, [("lsum", 0), ("lsum", 1)], ["lexp"])
                self.stt(lval[0:1, :], lexp[0:1, 0:1], 0.2, lexp[0:1, 1:2], ALU.add, ALU.subtract, ["lexp"], ["lval"])
                self.dma(lam_scr, lval[0:1, :], "c2", reads=["lval"], writes=["lam_scr"])
                self.dma(lam_b, lam_scr[0, :].partition_broadcast(128), "c2", reads=["lam_scr"], writes=["lam_b"])
                self.ts(gsb, gs_raw, 0.8, None, ALU.mult, None, ["gs_raw"], ["gsb"])
                AB = self.AB
                for g in range(8):
                    for t in range(2):
                        b = (g * 2 + t) % 2
                        self.mm(ps_bank(b)[:, 0:128], cc32[:, t * 128:(t + 1) * 128], wf32[:, g, :], True, True, ["cc32", "wf32"], [("ps", b)])
                        self.tcopy("dve", AB[:, g, t * 128:(t + 1) * 128], ps_bank(b)[:, 0:128], [("ps", b)], [("AB", g, t)])
                P.barrier()

                stop = self.stop_after
                for gi, (S, T) in enumerate(G):
                    if stop == "W":
                        break
                    self.phase_A(A, psum, gi, S, T, gvec, WA, KT[gi], QT[gi], V[gi], Z[gi])
                    self.conv_step(self.conv_left(), ("pool", "act", "dve"))
                    P.barrier()
                    if stop == "A":
                        break
                    self.phase_B(A, psum, gi, S, T, KT[gi], QT[gi], V[gi], MIXT[gi], lam_b, gsb)
                    P.barrier()
                    if stop == "B":
                        break
                    self.phase_C(A, psum, gi, S, T, Z[gi], MIXT[gi])
                    P.barrier()
                    if stop == "C":
                        break
                    self.phase_D(A, psum, gi, S, T, gvec, WB, MIXT[gi], ones_c)
                    P.barrier()
                if self.debug:
                    dbg_src = {"WA": WA, "KT": KT[0], "QT": QT[0], "V": V[0], "Z": Z[0], "MIXT": MIXT[0], "Wout": WB["w_out"], "Wdown": WB["w_down"]}
                    for i, nm in enumerate(self.debug):
                        src = dbg_src[nm]
                        shp = list(src.shape)
                        dst = dt("dbg_" + nm, shp, BF16, kind="ExternalOutput").ap()
                        if len(shp) == 4:
                            for b_ in range(shp[0]):
                                self.dma(dst[b_], src[b_], "misc")
                        elif len(shp) == 3:
                            for b_ in range(shp[0]):
                                self.dma(dst[b_], src[b_], "misc")
                        else:
                            self.dma(dst, src, "misc")
                P.emit(sems_eng, sems_dma)
        return nc

    def norm_transpose(self, psum, xin, kx, gb, sm, hbf, khbf, hT, khT, tcol, tbanks):
        ss, mse, rstd, ksm = sm
        self.act(hbf, xin, AF.Square, [kx], [khbf, (ksm, "ss")], accum=ss)
        self.rstd_from_ss(ss, mse, rstd, 1.0 / D, 1e-6, (ksm, "ss"), (ksm, "mse"), (ksm, "rstd"))
        self.stt(hbf, xin, rstd, gb, ALU.mult, ALU.mult, [kx, (ksm, "rstd"), "gb"], [khbf])
        for half in range(2):
            b = tbanks[half]
            pst = psum[:, b, :].bitcast(BF16).rearrange("p (a b) -> p a b", a=8)
            for j in range(8):
                kc = half * 8 + j
                self.tr(pst[:, j, :], hbf[:, kc * 128:(kc + 1) * 128], [khbf], [("ps", b)])
            self.evac(hT[:, half * 8:(half + 1) * 8, tcol:tcol + 128], pst, [("ps", b)], [(khT, half, tcol)])

    def norm_only(self, xin, kx, gb, sm, hbf, khbf):
        ss, mse, rstd, ksm = sm
        self.act(hbf, xin, AF.Square, [kx], [khbf, (ksm, "ss")], accum=ss)
        self.rstd_from_ss(ss, mse, rstd, 1.0 / D, 1e-6, (ksm, "ss"), (ksm, "mse"), (ksm, "rstd"))
        self.stt(hbf, xin, rstd, gb, ALU.mult, ALU.mult, [kx, (ksm, "rstd"), "gb"], [khbf])

    def transpose_only(self, psum, hbf, khbf, hT, khT, tcol, tbanks):
        for half in range(2):
            b = tbanks[half]
            pst = psum[:, b, :].bitcast(BF16).rearrange("p (a b) -> p a b", a=8)
            for j in range(8):
                kc = half * 8 + j
                self.tr(pst[:, j, :], hbf[:, kc * 128:(kc + 1) * 128], [khbf], [("ps", b)])
            self.evac(hT[:, half * 8:(half + 1) * 8, tcol:tcol + 128], pst, [("ps", b)], [(khT, half, tcol)])

    def phase_A(self, A, psum, gi, S, T, gvec, WA, KT, QT, V, Z):
        P = self.P
        A.reset(self.conv_bytes if self.conv_left() > 0 else 0)
        ntile = S // 512
        nown = T // 512
        gb = A.alloc([128, D], F32)
        self.dma(gb, gvec[0, :].partition_broadcast(128), "misc", writes=["gb"])
        xin = [A.alloc([128, D], F32) for _ in range(2)]
        hbf = [A.alloc([128, D], BF16) for _ in range(4)]
        sms = [(A.alloc([128, 1], F32), A.alloc([128, 1], F32), A.alloc([128, 1], F32), ("smA", i)) for i in range(4)]
        hT = [A.alloc([128, 16, 512], BF16) for _ in range(2)]
        R = 6
        wblk = [A.alloc([128, 8, 512], BF16) for _ in range(R)]
        rct = [A.alloc([128, 512], F32) for _ in range(2)]
        rst = [A.alloc([128, 512], F32) for _ in range(2)]
        stg = [A.alloc([128, 4, 512], BF16) for _ in range(2)]
        zstg = [A.alloc([128, 4, 1024], BF16) for _ in range(2)]
        fsb = [A.alloc([128, 512], BF16) for _ in range(2)]
        t1 = [A.alloc([128, 512], F32) for _ in range(2)]
        t2 = [A.alloc([128, 512], F32) for _ in range(2)]
        AB = self.AB
        pmat = self.pmat
        xcnt = [0]

        def prologue(ti):
            hs = ti % 2
            self.seal([self.dma(rct[hs], self.rc[gi][:, ti * 512:(ti + 1) * 512], "rt%d" % hs, writes=[("rct", hs)]),
                       self.dma(rst[hs], self.rs[gi][:, ti * 512:(ti + 1) * 512], "rt%d" % hs, writes=[("rst", hs)])])
            for st in range(4):
                xs = xcnt[0] % 2
                xcnt[0] += 1
                r0 = ti * 512 + st * 128
                self.dma(xin[xs], self.x[gi][r0:r0 + 128, :], "x%d" % xs, writes=[("xin", xs)])
                self.norm_only(xin[xs], ("xin", xs), gb, sms[st], hbf[st], ("hbf", st))

        def prologue_tr(ti):
            hs = ti % 2
            for st in range(4):
                self.transpose_only(psum, hbf[st], ("hbf", st), hT[hs], ("hT", hs), st * 128, (6, 7))

        jobs = []
        for ti in range(ntile):
            for b in range(2):
                jobs.append((ti, "four", b, ("Z", b)))
            for b in range(2):
                jobs.append((ti, "tok", 4 + b, ("V", b)))
            for b in range(2):
                jobs.append((ti, "rope", 2 + b, ("K", b)))
            if ti < nown:
                for b in range(2):
                    jobs.append((ti, "rope", 6 + b, ("Q", b)))
        akinds = getattr(self, "akinds", None)
        if akinds is not None:
            jobs = [j for j in jobs if j[1] in akinds]
        issued = [0]
        nfills = 2 * len(jobs)

        def ensure(upto):
            while issued[0] <= min(upto, nfills - 1):
                f = issued[0]
                blk = jobs[f // 2][2]
                kh = f % 2
                s = f % R
                self.dma(wblk[s], WA[blk, :, kh * 8:(kh + 1) * 8, :], "w%d" % s, writes=[("wblk", s)])
                issued[0] += 1

        psc = [0]

        def bank():
            b = psc[0] % 6
            psc[0] += 1
            return b

        pend = []
        ucnt = [0]

        def flush():
            while pend:
                pend.pop(0)()

        prologue(0)
        prologue_tr(0)
        cur_t = -1
        for ji, (ti, kind, blk, dest) in enumerate(jobs):
            if ti != cur_t:
                cur_t = ti
                flush()
                if ti + 1 < ntile:
                    prologue(ti + 1)
            if ti + 1 < ntile and (ji + 1 == len(jobs) or jobs[ji + 1][0] != ti):
                prologue_tr(ti + 1)
            hs = ti % 2
            f0 = 2 * ji
            ensure(f0 + R - 1)
            if self.conv_left() > 0:
                self.conv_step(-(-self.conv_left() // max(1, (len(jobs) - ji) - 8)) if len(jobs) - ji > 8 else self.conv_left(),
                               ("pool", "act", "pool", "dve"))
            ss_ = ji % 2
            r0 = ti * 512
            if kind == "tok":
                banks = [bank() for _ in range(4)]
                for kh in range(2):
                    s = (f0 + kh) % R
                    for st in range(4):
                        for k8 in range(8):
                            kc = kh * 8 + k8
                            self.mm(psum[:, banks[st], :], hT[hs][:, kc, st * 128:(st + 1) * 128], wblk[s][:, k8, :],
                                    kc == 0, kc == 15, [(("hT", hs), kc // 8, st * 128), ("wblk", s)], [("ps", banks[st])])
                flush()
                for st in range(4):
                    self.evac(stg[ss_][:, st, :], psum[:, banks[st], :], [("ps", banks[st])], [("stg", ss_, st)], act_share=2, of=3)
                dst = V[r0:r0 + 512, dest[1] * 512:(dest[1] + 1) * 512]
                self.dma(dst.rearrange("(s p) c -> p s c", p=128), stg[ss_], "st%d" % ss_, reads=[("stg", ss_, st) for st in range(4)])
                continue
            for oc in range(4):
                b = bank()
                for kh in range(2):
                    s = (f0 + kh) % R
                    for k8 in range(8):
                        kc = kh * 8 + k8
                        self.mm(psum[:, b, :], wblk[s][:, k8, oc * 128:(oc + 1) * 128], hT[hs][:, kc, :],
                                kc == 0, kc == 15, [(("hT", hs), kc // 8, c) for c in (0, 128, 256, 384)] + [("wblk", s)], [("ps", b)])
                flush()
                u = ucnt[0] % 2
                ucnt[0] += 1
                self.tcopy("act", fsb[u], psum[:, b, :], [("ps", b)], [("fsb", u), ("ps", b)])
                if kind == "rope":
                    self.tt("dve", t1[u], psum[:, b, :], rct[hs], ALU.mult, [("ps", b), ("rct", hs)], [("t1", u)])

                    def stage2(u=u, oc=oc, hs=hs, ss_=ss_, dest=dest, ti=ti):
                        b2 = bank()
                        self.mm(psum[:, b2, :], pmat, fsb[u], True, True, ["pmat", ("fsb", u)], [("ps", b2)])
                        self.tt("dve", t2[u], psum[:, b2, :], rst[hs], ALU.mult, [("ps", b2), ("rst", hs)], [("t2", u)])
                        self.tt("pool", stg[ss_][:, oc, :], t1[u], t2[u], ALU.add, [("t1", u), ("t2", u)], [("stg", ss_, oc)])
                        if oc == 3:
                            h0 = dest[1] * 4
                            dd = KT if dest[0] == "K" else QT
                            dst = dd[h0:h0 + 4, :, ti * 512:(ti + 1) * 512]
                            self.dma(dst.rearrange("h p c -> p h c"), stg[ss_], "st%d" % ss_, reads=[("stg", ss_, o_) for o_ in range(4)])
                    pend.append(stage2)
                else:
                    def stage2(u=u, oc=oc, ss_=ss_, dest=dest, r0=r0, blk=blk):
                        g = blk * 4 + oc
                        for sp in range(2):
                            b2 = bank()
                            for q_ in range(2):
                                st = sp * 2 + q_
                                self.mm(psum[:, b2, q_ * 256:(q_ + 1) * 256], fsb[u][:, st * 128:(st + 1) * 128], AB[:, g, :], True, True,
                                        [("fsb", u), ("AB", g)], [("ps", b2)])
                            self.evac(zstg[ss_][:, sp * 2:sp * 2 + 2, oc * 256:(oc + 1) * 256], psum[:, b2, :].rearrange("p (a b) -> p a b", a=2),
                                      [("ps", b2)], [("zstg", ss_, oc, sp)], act_share=1, of=3)
                        if oc == 3:
                            dst = Z[r0:r0 + 512, dest[1] * 1024:(dest[1] + 1) * 1024]
                            self.dma(dst.rearrange("(s p) c -> p s c", p=128), zstg[ss_], "c%d" % ss_,
                                     reads=[("zstg", ss_, o_, p_) for o_ in range(4) for p_ in range(2)])
                    pend.append(stage2)
        flush()

    def phase_B(self, A, psum, gi, S, T, KT, QT, V, MIXT, lam_b, gsb):
        P = self.P
        A.reset()
        nkt = S // 128
        nqb = T // 512
        KTs = [A.alloc([128, S], BF16) for _ in range(2)]
        QTz = [[A.alloc([128, T], BF16) for _ in range(2)] for _ in range(2)]
        VA = [A.alloc([128, nkt, 130], BF16) for _ in range(2)]
        pt = [A.alloc([128, 1024], BF16) for _ in range(3)]
        t32 = [A.alloc([128, 128], F32) for _ in range(2)]
        o32 = [A.alloc([128, 128], F32) for _ in range(2)]
        jk = A.alloc([128, 128], F32)
        onb = [A.alloc([128, 4, 128], BF16) for _ in range(2)]
        oTs = [A.alloc([128, 512], BF16) for _ in range(2)]
        sm = [[A.alloc([128, 1], F32) for _ in range(6)] for _ in range(2)]
        for s in range(2):
            va = VA[s]
            P.op("pool", lambda e, va=va: e.memset(va[:, :, 128:129], 1.0), writes=[("VAone", s)])
            z0 = QTz[0][s][64:128, :]
            z1 = QTz[1][s][0:64, :]
            P.op("pool", lambda e, z0=z0: e.memset(z0, 0.0), writes=[("QTzero", s, 0)])
            P.op("dve", lambda e, z1=z1: e.memset(z1, 0.0), writes=[("QTzero", s, 1)])

        def load_head(h):
            s = h % 2
            grp = [self.dma(KTs[s], KT[h], "hd%d" % s, writes=[("KTs", s)]),
                   self.dma(QTz[0][s][0:64, :], QT[h, 0:64, :], "hd%d" % s, writes=[("QTs", s, 0)]),
                   self.dma(QTz[1][s][64:128, :], QT[h, 64:128, :], "hd%d" % s, writes=[("QTs", s, 1)])]
            step = 16
            for k0 in range(0, nkt, step):
                k1 = min(nkt, k0 + step)
                grp.append(self.dma(VA[s][:, k0:k1, 0:128], V[k0 * 128:k1 * 128, h * 128:(h + 1) * 128].rearrange("(k p) c -> p k c", p=128),
                                    "hd%d" % s, writes=[("VA", s, k0)]))
            self.seal(grp)

        def acc_ap(c, j):
            a = c * 4 + j
            return psum[:, 4 + a // 3, (a % 3) * 129:(a % 3) * 129 + 129], 4 + a // 3, (a % 3 == 0)

        load_head(0)
        fin = [0]
        for h in range(NH):
            s = h % 2
            if h + 1 < NH:
                load_head(h + 1)
            vkeys = [("VA", s, k0) for k0 in range(0, nkt, 16)] + [("VAone", s)]
            for qb in range(nqb):
                qsl = slice(qb * 512, (qb + 1) * 512)

                def ST(kt):
                    sl = kt % 2
                    for c in range(2):
                        self.mm(psum[:, sl * 2 + c, :], KTs[s][:, kt * 128:(kt + 1) * 128], QTz[c][s][:, qsl],
                                True, True, [("KTs", s), ("QTs", s, c), ("QTzero", s, c)], [("pss", sl, c)])

                ST(0)
                for kt in range(nkt):
                    if kt + 1 < nkt:
                        ST(kt + 1)
                    sl = kt % 2
                    p3 = kt % 3
                    for c in range(2):
                        self.act(pt[p3][:, c * 512:(c + 1) * 512], psum[:, sl * 2 + c, :], AF.Exp, [("pss", sl, c)], [("pt", p3, c)], scale=0.125)
                    for c in range(2):
                        for j in range(4):
                            ap_, bank, first = acc_ap(c, j)
                            self.mm(ap_, pt[p3][:, c * 512 + j * 128:c * 512 + (j + 1) * 128], VA[s][:, kt, 0:129],
                                    (kt == 0 and first), kt == nkt - 1, [("pt", p3, c), ("VA", s, (kt // 16) * 16), ("VAone", s)], [("acc", bank)], skip=True)
                fs = fin[0] % 2
                fin[0] += 1
                for j in range(4):
                    a0, b0, _ = acc_ap(0, j)
                    a1, b1, _ = acc_ap(1, j)
                    u = j % 2
                    r0, r1, r1l, ss, mse, rstd = sm[u]
                    ku = ("smB", u)
                    P.op("dve", lambda e, r0=r0, a0=a0: e.reciprocal(out=r0, in_=a0[:, 128:129]), [("acc", b0)], [(ku, "r0")])
                    P.op("dve", lambda e, r1=r1, a1=a1: e.reciprocal(out=r1, in_=a1[:, 128:129]), [("acc", b1)], [(ku, "r1")])
                    self.tt("dve", r1l, r1, lam_b, ALU.mult, [(ku, "r1"), "lam_b"], [(ku, "r1l")])
                    self.ts(t32[u], a1[:, 0:128], r1l, None, ALU.mult, None, [("acc", b1), (ku, "r1l")], [("t32", u)])
                    self.stt(o32[u], a0[:, 0:128], r0, t32[u], ALU.mult, ALU.subtract, [("acc", b0), (ku, "r0"), ("t32", u)], [("o32", u)])
                    o_ = o32[u]
                    P.op("dve", lambda e, o_=o_, ss=ss: e.scalar_tensor_tensor(out=jk, in0=o_, scalar=1.0, in1=o_,
                                                                             op0=ALU.mult, op1=ALU.mult, accum_out=ss),
                         [("o32", u)], ["jk", (ku, "ss")])
                    self.rstd_from_ss(ss, mse, rstd, 1.0 / 128, 1e-5, (ku, "ss"), (ku, "mse"), (ku, "rstd"))
                    self.stt(onb[fs][:, j, :], o32[u], rstd, gsb, ALU.mult, ALU.mult, [("o32", u), (ku, "rstd"), "gsb"], [("onb", fs, j)])
                    pst = psum[:, 7, :].bitcast(BF16).rearrange("p (a b) -> p a b", a=8)
                    self.tr(pst[:, j, :], onb[fs][:, j, :], [("onb", fs, j)], [("pstB", 0)])
                pst = psum[:, 7, :].bitcast(BF16)
                self.tcopy("dve", oTs[fs], pst[:, 0:512], [("pstB", 0)], [("oTs", fs)])
                self.dma(MIXT[1024 + h * 128:1024 + (h + 1) * 128, qsl], oTs[fs], "st%d" % fs, reads=[("oTs", fs)])

    def phase_C(self, A, psum, gi, S, T, Z, MIXT):
        P = self.P
        A.reset()
        nch = S // 128
        NCH = min(32, nch)
        nfill = nch // NCH
        nkt = T // 128
        Zs = A.alloc([128, nch, 1024], BF16)
        R = 3
        dcs = [A.alloc([128, NCH, 128], BF16) for _ in range(R)]
        dss = [A.alloc([128, NCH, 128], BF16) for _ in range(R)]
        ysb = [A.alloc([128, 512], BF16) for _ in range(2)]
        yT = [A.alloc([128, 4, 128], BF16) for _ in range(2)]
        for half in range(2):
            step = 16
            zk = []
            for k0 in range(0, nch, step):
                k1 = min(nch, k0 + step)
                zk.append(self.dma(Zs[:, k0:k1, :], Z[k0 * 128:k1 * 128, half * 1024:(half + 1) * 1024].rearrange("(k p) c -> p k c", p=128),
                                   "z0", writes=[("Zs", k0)]))
            self.seal(zk)
            fl = [(kt, f) for kt in range(nkt) for f in range(nfill)]
            issued = [0]

            def ensure(upto):
                while issued[0] <= min(upto, len(fl) - 1):
                    i = issued[0]
                    kt, f = fl[i]
                    s = i % R
                    self.seal([self.dma(dcs[s], self.dc[gi][kt, :, f * NCH:(f + 1) * NCH, :], "df%d" % s, writes=[("dcs", s)]),
                               self.dma(dss[s], self.ds[gi][kt, :, f * NCH:(f + 1) * NCH, :], "df%d" % s, writes=[("dss", s)])])
                    issued[0] += 1

            for i, (kt, f) in enumerate(fl):
                ensure(i + R - 1)
                s = i % R
                b = kt % 2
                for n in range(NCH):
                    ncg = f * NCH + n
                    zrow = Zs[:, ncg, :].rearrange("p (g c) -> p g c", g=4)
                    self.mm(psum[:, b, :], dcs[s][:, n, :], zrow[:, :, 0:128], (ncg == 0), False, [("dcs", s), ("Zs", (ncg // 16) * 16)], [("psC", b)])
                    self.mm(psum[:, b, :], dss[s][:, n, :], zrow[:, :, 128:256], False, (ncg == nch - 1), [("dss", s), ("Zs", (ncg // 16) * 16)], [("psC", b)])
                if f == nfill - 1:
                    self.evac(ysb[b], psum[:, b, :], [("psC", b)], [("ysb", b)])
                    pst = psum[:, 6 + b, :].bitcast(BF16).rearrange("p (a b) -> p a b", a=8)
                    for g in range(4):
                        self.tr(pst[:, g, :], ysb[b][:, g * 128:(g + 1) * 128], [("ysb", b)], [("pstC", b)])
                    self.evac(yT[b], pst[:, 0:4, :], [("pstC", b)], [("yT", b)])
                    self.dma(MIXT[half * 512:(half + 1) * 512, kt * 128:(kt + 1) * 128].rearrange("(g p) c -> p g c", p=128), yT[b],
                             "st%d" % b, reads=[("yT", b)])

    def phase_D(self, A, psum, gi, S, T, gvec, WB, MIXT, ones_c):
        P = self.P
        A.reset()
        ntile = T // 512
        xres = A.alloc([128, 4, D], F32)
        actT = A.alloc([128, 16, 512], BF16)
        hT = A.alloc([128, 16, 512], BF16)
        hid = A.alloc([128, 24, 512], BF16)
        qT = [A.alloc([128, 4, 512], BF16) for _ in range(2)]
        R = 6
        wblk = [A.alloc([128, 8, 512], BF16) for _ in range(R)]
        kTm = A.alloc([128, 16, 256], BF16)
        vm = A.alloc([128, 2, D], BF16)
        gb = A.alloc([128, D], F32)
        hbf = [A.alloc([128, D], BF16) for _ in range(2)]
        pT = [A.alloc([128, 2, 512], BF16) for _ in range(2)]
        ocn = [A.alloc([128, 512], BF16) for _ in range(2)]
        sg = [A.alloc([128, 512], BF16) for _ in range(2)]
        sms = [(A.alloc([128, 1], F32), A.alloc([128, 1], F32), A.alloc([128, 1], F32), ("smD", i)) for i in range(2)]
        rl = [A.alloc([128, 1], F32) for _ in range(2)]
        memT = A.alloc([128, 16, 256], BF16)

        def blockfills(name, blk, kc0=0, kc1=16):
            out = []
            k = kc0
            while k < kc1:
                n = min(8, kc1 - k)
                out.append((name, blk, k, n))
                k += n
            return out

        fills = []
        for blk in range(4):
            fills += blockfills("w_ck", blk)
        for blk in range(4):
            fills += blockfills("w_cv", blk)
        halves = [(0, 6), (6, 11)]
        dstop = getattr(self, "dstop", 99)
        for ti in range(ntile):
            for blk in range(4):
                fills += blockfills("w_out", blk)
            if dstop <= 2:
                continue
            for blk in range(4):
                fills += blockfills("w_cq", blk)
            for blk in range(4):
                fills += blockfills("w_co", blk)
            if dstop <= 5:
                continue
            for (b0, b1) in halves:
                for blk in range(b0, b1):
                    fills += blockfills("w_gate", blk)
                    fills += blockfills("w_up", blk)
                for cb in range(4):
                    fills += blockfills("w_down", cb, b0 * 4, b1 * 4)
        issued = [0]
        used = [0]

        def ensure(upto):
            while issued[0] <= min(upto, len(fills) - 1):
                i = issued[0]
                name, blk, k0, n = fills[i]
                s = i % R
                self.dma(wblk[s][:, 0:n, :], WB[name][blk, :, k0:k0 + n, :], "w%d" % s, writes=[("wblk", s)])
                issued[0] += 1

        def next_fill(expect):
            i = used[0]
            assert fills[i][0] == expect, (fills[i], expect)
            ensure(i + R - 4)
            used[0] += 1
            return i % R, fills[i]

        self.dma(gb, gvec[2, :].partition_broadcast(128), "misc", writes=["gb"])
        for mt in range(2):
            self.dma(xres[:, mt, :], self.mem[gi][mt * 128:(mt + 1) * 128, :], "x%d" % mt, writes=[("xres", mt)])
            self.norm_transpose(psum, xres[:, mt, :], ("xres", mt), gb, sms[mt], hbf[mt], ("hbf", mt), memT, "memT", mt * 128, (6, 7))
        mk = [("memT", h, c) for h in range(2) for c in (0, 128)]
        evc = 0
        for blk in range(4):
            f = [next_fill("w_ck"), next_fill("w_ck")]
            for oc in range(4):
                b = (blk * 4 + oc) % 4
                for kh in range(2):
                    s = f[kh][0]
                    for k8 in range(8):
                        kc = kh * 8 + k8
                        self.mm(psum[:, b, 0:256], wblk[s][:, k8, oc * 128:(oc + 1) * 128], memT[:, kc, :], kc == 0, kc == 15,
                                [("wblk", s)] + [("memT", kc // 8, c) for c in (0, 128)], [("ps", b)])
                self.evac(kTm[:, blk * 4 + oc, :], psum[:, b, 0:256], [("ps", b)], [("kTm", blk * 4 + oc)])
        for blk in range(4):
            f = [next_fill("w_cv"), next_fill("w_cv")]
            for mt in range(2):
                b = (blk * 2 + mt) % 4
                for kh in range(2):
                    s = f[kh][0]
                    for k8 in range(8):
                        kc = kh * 8 + k8
                        self.mm(psum[:, b, :], memT[:, kc, mt * 128:(mt + 1) * 128], wblk[s][:, k8, :], kc == 0, kc == 15,
                                [("wblk", s), ("memT", kc // 8, mt * 128)], [("ps", b)])
                self.evac(vm[:, mt, blk * 512:(blk + 1) * 512], psum[:, b, :], [("ps", b)], [("vm", mt, blk)])

        def tokmajor_accum(name, lhs, lhs_keys, nk0, nk1):
            for cb in range(4):
                k = nk0
                while k < nk1:
                    s, (nm, blk, k0, n) = next_fill(name)
                    assert blk == cb and k0 == k
                    for st in range(4):
                        for k8 in range(n):
                            kc = k + k8
                            self.mm(psum[:, st, :], lhs[:, kc - lhs_base[0], st * 128:(st + 1) * 128], wblk[s][:, k8, :], kc == nk0, kc == nk1 - 1,
                                    [("wblk", s)] + lhs_keys(kc, st), [("ps", st)])
                    k += n
                for st in range(4):
                    xs = xres[:, st, cb * 512:(cb + 1) * 512]
                    self.tt("dve", xs, psum[:, st, :], xs, ALU.add, [("ps", st), ("xres", st)], [("xres", st)])

        lhs_base = [0]

        def norm_all(grow):
            self.dma(gb, gvec[grow, :].partition_broadcast(128), "misc", writes=["gb"])
            for st in range(4):
                self.norm_transpose(psum, xres[:, st, :], ("xres", st), gb, sms[st % 2], hbf[st % 2], ("hbf", st % 2), hT, "hT", st * 128, (6, 7))

        hTk = lambda kc: [("hT", kc // 8, c) for c in (0, 128, 256, 384)]
        cross_scale = 512 ** -0.5

        for ti in range(ntile):
            r0 = ti * 512
            self.dma(xres, self.x[gi][r0:r0 + 512, :].rearrange("(s p) c -> p s c", p=128), "x0", writes=[("xres", st) for st in range(4)])
            self.dma(actT, MIXT[:, r0:r0 + 512].rearrange("(k p) c -> p k c", p=128), "x1", writes=[("actT", kc) for kc in range(16)])
            lhs_base[0] = 0
            tokmajor_accum("w_out", actT, lambda kc, st: [("actT", kc)], 0, 16)
            def dump():
                self.dma(self.y[gi][r0:r0 + 512, :].rearrange("(s p) c -> p s c", p=128), xres, "out%d" % (ti % 2),
                         reads=[("xres", st) for st in range(4)])
            if dstop <= 1:
                dump()
                continue
            norm_all(1)
            if dstop == 2:
                if ti == 0 and gi == 0:
                    dh = self.nc.dram_tensor("dbg_hT", [128, 16, 512], BF16, kind="ExternalOutput").ap()
                    self.dma(dh, hT, "misc", reads=[("hT", h_, c_) for h_ in range(2) for c_ in (0, 128, 256, 384)])
                    dm_ = self.nc.dram_tensor("dbg_memT", [128, 16, 256], BF16, kind="ExternalOutput").ap()
                    self.dma(dm_, memT, "misc", reads=[("memT", h_, c_) for h_ in range(2) for c_ in (0, 128)])
                    dk = self.nc.dram_tensor("dbg_kTm", [128, 16, 256], BF16, kind="ExternalOutput").ap()
                    self.dma(dk, kTm, "misc", reads=[("kTm", i_) for i_ in range(16)])
                    dv = self.nc.dram_tensor("dbg_vm", [128, 2, 2048], BF16, kind="ExternalOutput").ap()
                    self.dma(dv, vm, "misc", reads=[("vm", m_, b_) for m_ in range(2) for b_ in range(4)])
                dump()
                continue
            for hh in range(4):
                f = [next_fill("w_cq"), next_fill("w_cq")]
                qs = hh % 2
                for oc in range(4):
                    b = oc
                    for kh in range(2):
                        s = f[kh][0]
                        for k8 in range(8):
                            kc = kh * 8 + k8
                            self.mm(psum[:, b, :], wblk[s][:, k8, oc * 128:(oc + 1) * 128], hT[:, kc, :], kc == 0, kc == 15,
                                    [("wblk", s)] + hTk(kc), [("ps", b)])
                    self.evac(qT[qs][:, oc, :], psum[:, b, :], [("ps", b)], [("qT", qs, oc)])
                ps_ = hh % 2
                for mt in range(2):
                    b = 4 + mt
                    for dc_ in range(4):
                        self.mm(psum[:, b, :], kTm[:, hh * 4 + dc_, mt * 128:(mt + 1) * 128], qT[qs][:, dc_, :], dc_ == 0, dc_ == 3,
                                [("kTm", hh * 4 + dc_), ("qT", qs, dc_)], [("ps", b)])
                    self.act(pT[ps_][:, mt, :], psum[:, b, :], AF.Exp, [("ps", b)], [("pT", ps_, mt)], scale=cross_scale)
                for st in range(4):
                    b = st % 2
                    for mt in range(2):
                        self.mm(psum[:, b, :], pT[ps_][:, mt, st * 128:(st + 1) * 128], vm[:, mt, hh * 512:(hh + 1) * 512], mt == 0, mt == 1,
                                [("pT", ps_, mt), ("vm", mt, hh)], [("ps", b)])
                    lb = 2 + (st % 2)
                    for mt in range(2):
                        self.mm(psum[:, lb, 0:1], pT[ps_][:, mt, st * 128:(st + 1) * 128], ones_c, mt == 0, mt == 1,
                                [("pT", ps_, mt), "ones_c"], [("ps", lb)])
                    rl_ = rl[st % 2]
                    lsrc = psum[:, lb, 0:1]
                    P.op("dve", lambda e, rl_=rl_, lsrc=lsrc: e.reciprocal(out=rl_, in_=lsrc), [("ps", lb)], [("rl", st % 2)])
                    self.ts(ocn[st % 2], psum[:, b, :], rl_, None, ALU.mult, None, [("ps", b), ("rl", st % 2)], [("ocn", st % 2)])
                    tb = 6 + (st % 2)
                    pst = psum[:, tb, :].bitcast(BF16).rearrange("p (a b) -> p a b", a=8)
                    for dc_ in range(4):
                        self.tr(pst[:, dc_, :], ocn[st % 2][:, dc_ * 128:(dc_ + 1) * 128], [("ocn", st % 2)], [("ps", tb)])
                    self.evac(actT[:, hh * 4:(hh + 1) * 4, st * 128:(st + 1) * 128], pst[:, 0:4, :], [("ps", tb)],
                              [("actT", hh * 4 + d_) for d_ in range(4)])
            tokmajor_accum("w_co", actT, lambda kc, st: [("actT", kc)], 0, 16)
            if dstop <= 5:
                dump()
                continue
            norm_all(3)
            for (b0, b1) in halves:
                for blk in range(b0, b1):
                    fg = [next_fill("w_gate"), next_fill("w_gate")]
                    fu = [next_fill("w_up"), next_fill("w_up")]
                    for oc in range(4):
                        bg = (oc % 2) * 2
                        bu = bg + 1
                        for (ff, bb) in ((fg, bg), (fu, bu)):
                            for kh in range(2):
                                s = ff[kh][0]
                                for k8 in range(8):
                                    kc = kh * 8 + k8
                                    self.mm(psum[:, 4 + bb, :], wblk[s][:, k8, oc * 128:(oc + 1) * 128], hT[:, kc, :], kc == 0, kc == 15,
                                            [("wblk", s)] + hTk(kc), [("ps", 4 + bb)])
                        sgs = oc % 2
                        self.act(sg[sgs], psum[:, 4 + bg, :], AF.Silu, [("ps", 4 + bg)], [("sg", sgs)])
                        hc = (blk - b0) * 4 + oc
                        self.tt("dve", hid[:, hc, :], psum[:, 4 + bu, :], sg[sgs], ALU.mult, [("ps", 4 + bu), ("sg", sgs)], [("hid", hc)])
                lhs_base[0] = b0 * 4
                tokmajor_accum("w_down", hid, lambda kc, st: [("hid", kc - lhs_base[0])], b0 * 4, b1 * 4)
            if dstop <= 8:
                dump()
                continue
            self.dma(gb, gvec[4, :].partition_broadcast(128), "misc", writes=["gb"])
            for st in range(4):
                ss, mse, rstd, ksm = sms[st % 2]
                hb = hbf[st % 2]
                self.act(hb, xres[:, st, :], AF.Square, [("xres", st)], [("hbf", st % 2), (ksm, "ss")], accum=ss)
                self.rstd_from_ss(ss, mse, rstd, 1.0 / D, 1e-6, (ksm, "ss"), (ksm, "mse"), (ksm, "rstd"))
                self.stt(xres[:, st, :], xres[:, st, :], rstd, gb, ALU.mult, ALU.mult, [("xres", st), (ksm, "rstd"), "gb"], [("xres", st)])
            self.dma(self.y[gi][r0:r0 + 512, :].rearrange("(s p) c -> p s c", p=128), xres, "out%d" % (ti % 2),
                     reads=[("xres", st) for st in range(4)])
        assert used[0] == len(fills), (used[0], len(fills))


_TABLE_CACHE = {}


def _perm(S, own0, T):
    own = np.arange(own0, own0 + T)
    rest = np.concatenate([np.arange(0, own0), np.arange(own0 + T, S)])
    return np.concatenate([own, rest]).astype(np.int64)


def _rope_tables(S, perm):
    inv = (1.0 / (np.float32(10000.0) ** (np.arange(0, 64, 2, dtype=np.float32) / np.float32(64)))).astype(np.float32)
    pos = perm.astype(np.float32)
    ang = (pos[:, None] * inv[None, :]).astype(np.float32)
    ang = np.concatenate([ang, ang], axis=1)
    c = np.cos(ang).astype(np.float32)
    s = np.sin(ang).astype(np.float32)
    sgn = np.concatenate([-np.ones(32, np.float32), np.ones(32, np.float32)])
    s = s * sgn[None, :]
    cT = np.ascontiguousarray(np.concatenate([c, c], axis=1).T)
    sT = np.ascontiguousarray(np.concatenate([s, s], axis=1).T)
    return cT, sT


def _dft_tables(S, perm, own0, T):
    key = (S, own0, T)
    if key in _TABLE_CACHE:
        return _TABLE_CACHE[key]
    k = np.arange(own0, own0 + T, dtype=np.int64)
    scale = 1.0 / np.sqrt(S * 128.0)
    nchunk = S // 128
    dc = np.empty((T // 128, 128, nchunk, 128), dtype=bf16_np)
    ds = np.empty((T // 128, 128, nchunk, 128), dtype=bf16_np)
    tab_c = (np.cos(2 * np.pi * np.arange(S) / S) * scale).astype(np.float32)
    tab_s = (-np.sin(2 * np.pi * np.arange(S) / S) * scale).astype(np.float32)
    n2 = perm.reshape(nchunk, 128)
    for kt in range(T // 128):
        kk = k[kt * 128:(kt + 1) * 128]
        m = (n2[:, :, None] * kk[None, None, :]) % S
        dc[kt] = tab_c[m].transpose(1, 0, 2).astype(bf16_np)
        ds[kt] = tab_s[m].transpose(1, 0, 2).astype(bf16_np)
    _TABLE_CACHE[key] = (dc, ds)
    return dc, ds


def _shared_inputs(inp):
    f32 = np.float32
    w_in = np.ascontiguousarray(np.asarray(inp["w_in"], f32)[0])
    m = np.arange(128)
    pm = np.zeros((128, 128), np.float32)
    pm[(m // 64) * 64 + ((m % 64) + 32) % 64, m] = 1.0
    c = np.arange(128)
    ang = 2 * np.pi * np.outer(c, c) / 128.0
    ccsc = np.concatenate([np.cos(ang), np.sin(ang)], axis=1).astype(f32)
    lam = np.concatenate([np.asarray(inp[n], f32)[0] for n in ("lambda_q1", "lambda_k1", "lambda_q2", "lambda_k2")])[None, :]
    gvec = np.stack([np.asarray(inp["g_mix"], f32)[0], np.asarray(inp["g_cross"], f32)[0], np.asarray(inp["g_mem"], f32)[0],
                     np.asarray(inp["g_ffn"], f32)[0], np.asarray(inp["g_final"], f32)])
    sh = {
        "w_in": w_in, "pmat": pm.astype(bf16_np),
        "w_f": np.ascontiguousarray(np.asarray(inp["w_fourier"], f32)[0]),
        "ccsc": np.ascontiguousarray(ccsc), "lam": np.ascontiguousarray(lam), "gvec": np.ascontiguousarray(gvec),
        "g_sub": np.ascontiguousarray(np.asarray(inp["g_subln"], f32)[0]),
        "ident": np.eye(128, dtype=np.float32).astype(bf16_np),
    }
    for n in ("w_out", "w_cq", "w_ck", "w_cv", "w_co", "w_gate", "w_up", "w_down"):
        sh[n] = np.ascontiguousarray(np.asarray(inp[n], f32)[0])
    return sh


def _group_inputs(gi, xseq, memseq, own0, T):
    S = xseq.shape[0]
    perm = _perm(S, own0, T)
    cT, sT = _rope_tables(S, perm)
    dc, ds = _dft_tables(S, perm, own0, T)
    return {
        "x%d" % gi: np.ascontiguousarray(xseq[perm]),
        "mem%d" % gi: np.ascontiguousarray(memseq),
        "rc%d" % gi: cT, "rs%d" % gi: sT, "dc%d" % gi: dc, "ds%d" % gi: ds,
    }


_NC_CACHE = {}


def _get_nc(groups):
    key = tuple(groups)
    if key not in _NC_CACHE:
        _NC_CACHE[key] = Builder(list(groups)).build()
    return _NC_CACHE[key]


def kernel(**inputs):
    f32 = np.float32
    xp = np.asarray(inputs["x_prompt"], f32)
    xs = np.asarray(inputs["x_sample"], f32)
    mp = np.asarray(inputs["mem_prompt"], f32)
    ms = np.asarray(inputs["mem_sample"], f32)
    B, S0, _ = xp.shape
    B1, S1, _ = xs.shape
    n = 8
    T0 = S0 * B // n
    T1 = S1 * B1 // n
    cp = n // B
    cs = n // B1
    groups = ((S0, T0), (S1, T1))
    nc = _get_nc(groups)
    sh = _shared_inputs(inputs)
    in_maps = []
    for c in range(n):
        m = dict(sh)
        m.update(_group_inputs(0, xp[c // cp], mp[c // cp], (c % cp) * T0, T0))
        m.update(_group_inputs(1, xs[c // cs], ms[c // cs], (c % cs) * T1, T1))
        in_maps.append(m)
    res = run_bass_kernel_spmd(nc, in_maps, core_ids=list(range(n)))
    yp = np.empty((B, S0, D), f32)
    ys = np.empty((B1, S1, D), f32)
    for c in range(n):
        r = res.results[c]
        yp[c // cp, (c % cp) * T0:(c % cp + 1) * T0] = r["y0"]
        ys[c // cs, (c % cs) * T1:(c % cs + 1) * T1] = r["y1"]
    return yp, ys
```

```python
import numpy as np
import ml_dtypes
import concourse.bass as bass
import concourse.mybir as mybir
from concourse.bass_utils import run_bass_kernel_spmd
from concourse.alu_op_type import AluOpType as ALU

F32 = mybir.dt.float32
BF16 = mybir.dt.bfloat16
AF = mybir.ActivationFunctionType
bf16_np = ml_dtypes.bfloat16

D = 2048
DFF = 5632
NH = 8
NMEM = 256
ENGS = ("pe", "act", "dve", "pool", "sp")


class Instr:
    __slots__ = ("eng", "fn", "raw", "other", "dma_sem", "dma_val", "milestone", "tick", "extra_waits")

    def __init__(self, eng, fn):
        self.eng = eng
        self.fn = fn
        self.raw = []
        self.other = []
        self.dma_sem = None
        self.dma_val = 0
        self.milestone = False
        self.tick = 0
        self.extra_waits = []


class Prog:
    def __init__(self, nc):
        self.nc = nc
        self.streams = {e: [] for e in ENGS}
        self.last_writer = {}
        self.readers = {}
        self.dma_count = {}
        self.pending = {e: [] for e in ENGS}

    def op(self, eng, fn, reads=(), writes=(), dma_sem=None):
        ins = Instr(eng, fn)
        if dma_sem is not None:
            c = self.dma_count.get(dma_sem, 0) + 16
            self.dma_count[dma_sem] = c
            ins.dma_sem = dma_sem
            ins.dma_val = c
        raw = {}
        oth = {}
        for k in reads:
            w = self.last_writer.get(k)
            if w is not None:
                raw[id(w)] = w
        for k in writes:
            w = self.last_writer.get(k)
            if w is not None:
                oth[id(w)] = w
            rd = self.readers.get(k)
            if rd:
                for r in rd.values():
                    oth[id(r)] = r
        ins.raw = list(raw.values())
        ins.other = [d for i, d in oth.items() if i not in raw]
        if self.pending[eng]:
            ins.extra_waits = self.pending[eng]
            self.pending[eng] = []
        for k in writes:
            self.last_writer[k] = ins
            self.readers[k] = {}
        rk = (eng, dma_sem)
        for k in reads:
            self.readers.setdefault(k, {})[rk] = ins
        self.streams[eng].append(ins)
        return ins

    def barrier(self):
        lasts = []
        for e in ENGS:
            if e == "sp":
                continue
            if self.streams[e]:
                l = self.streams[e][-1]
                l.milestone = True
                lasts.append(l)
        dm = list(self.dma_count.items())
        for e in ENGS:
            self.pending[e] = self.pending[e] + [("ins", l) for l in lasts if l.eng != e] + [("dma", k, v) for k, v in dm]
        self.last_writer = {}
        self.readers = {}

    def emit(self, sems_eng, sems_dma):
        nc = self.nc
        for e in ENGS:
            for ins in self.streams[e]:
                for d in ins.raw:
                    if d.dma_sem is None and not (d.eng == e and e == "pe"):
                        d.milestone = True
                for d in ins.other:
                    if d.dma_sem is None and d.eng != e:
                        d.milestone = True
        for e in ENGS:
            t = 0
            for ins in self.streams[e]:
                if ins.milestone:
                    t += 1
                    ins.tick = t

        def run(e, eng):
            waited = {}
            for ins in self.streams[e]:
                need = {}
                for d in ins.raw:
                    if d.dma_sem is not None:
                        sk, v = ("d", d.dma_sem), d.dma_val
                    elif d.eng == e and e == "pe":
                        continue
                    else:
                        sk, v = ("e", d.eng), d.tick
                    if need.get(sk, 0) < v:
                        need[sk] = v
                for d in ins.other:
                    if d.dma_sem is not None:
                        sk, v = ("d", d.dma_sem), d.dma_val
                    elif d.eng != e:
                        sk, v = ("e", d.eng), d.tick
                    else:
                        continue
                    if need.get(sk, 0) < v:
                        need[sk] = v
                for w in ins.extra_waits:
                    if w[0] == "ins":
                        sk, v = ("e", w[1].eng), w[1].tick
                    else:
                        sk, v = ("d", w[1]), w[2]
                    if need.get(sk, 0) < v:
                        need[sk] = v
                for sk, val in need.items():
                    if waited.get(sk, 0) >= val:
                        continue
                    waited[sk] = val
                    sem = sems_eng[sk[1]] if sk[0] == "e" else sems_dma[sk[1]]
                    eng.wait_ge(sem, val)
                r = ins.fn(eng)
                if ins.dma_sem is not None:
                    r.then_inc(sems_dma[ins.dma_sem], 16)
                elif ins.milestone:
                    r.then_inc(sems_eng[e], 1)
            if e == "sp":
                for k, v in self.dma_count.items():
                    if waited.get(("d", k), 0) < v:
                        eng.wait_ge(sems_dma[k], v)

        with nc.Block() as block:
            @block.tensor
            def _(eng):
                run("pe", eng)

            @block.scalar
            def _(eng):
                run("act", eng)

            @block.vector
            def _(eng):
                run("dve", eng)

            @block.gpsimd
            def _(eng):
                run("pool", eng)

            @block.sync
            def _(eng):
                run("sp", eng)


class Arena:
    def __init__(self, ap, nbytes):
        self.ap = ap
        self.cap = nbytes
        self.off = 0

    def reset(self, base=0):
        self.off = base

    def alloc(self, shape, dt):
        n = 1
        for s in shape[1:]:
            n *= s
        nb = n * (4 if dt == F32 else 2)
        a = self.ap[:, self.off // 2:(self.off + nb) // 2]
        self.off += (nb + 63) // 64 * 64
        assert self.off <= self.cap, ("SBUF arena overflow", self.off, self.cap)
        if dt == F32:
            a = a.bitcast(F32)
        if len(shape) == 3:
            a = a.rearrange("p (a b) -> p a b", a=shape[1])
        return a


DMA_SEMS = (["w%d" % i for i in range(6)] + ["x0", "x1", "rt0", "rt1", "st0", "st1", "st2", "st3",
            "c0", "c1", "c2", "hd0", "hd1", "z0", "df0", "df1", "df2", "misc", "out0", "out1", "cv0", "cv1", "cs0", "cs1"])


class Builder:
    def __init__(self, groups, stop_after=None, debug=()):
        self.groups = groups
        self.stop_after = stop_after
        self.debug = debug
        self.nc = bass.Bass("TRN2", target_bir_lowering=False)
        self.evc = 0

    def dma(self, out, in_, sem, reads=(), writes=()):
        return self.P.op("sp", lambda e: e.dma_start(out=out, in_=in_), reads, writes, dma_sem=sem)

    @staticmethod
    def seal(instrs):
        v = max(i.dma_val for i in instrs)
        for i in instrs:
            i.dma_val = v

    def mm(self, out, lhsT, rhs, start, stop, reads, writes, skip=False):
        if skip:
            return self.P.op("pe", lambda e: e.matmul(out, lhsT=lhsT, rhs=rhs, start=start, stop=stop, skip_group_check=True), reads, writes)
        return self.P.op("pe", lambda e: e.matmul(out, lhsT=lhsT, rhs=rhs, start=start, stop=stop), reads, writes)

    def tr(self, out, in_, reads, writes):
        ident = self.ident
        return self.P.op("pe", lambda e: e.transpose(out, in_, ident), list(reads) + ["ident"], writes)

    def act(self, out, in_, func, reads, writes, scale=None, accum=None):
        kw = {}
        if scale is not None:
            kw["scale"] = scale
        if accum is not None:
            kw["accum_out"] = accum
        return self.P.op("act", lambda e: e.activation(out=out, in_=in_, func=func, **kw), reads, writes)

    def tcopy(self, eng, out, in_, reads, writes):
        if eng == "act":
            return self.P.op("act", lambda e: e.activation(out=out, in_=in_, func=AF.Copy), reads, writes)
        return self.P.op(eng, lambda e: e.tensor_copy(out=out, in_=in_), reads, writes)

    def evac(self, out, in_, reads, writes, act_share=1, of=2):
        self.evc += 1
        eng = "act" if (self.evc % of) < act_share else "dve"
        return self.tcopy(eng, out, in_, reads, writes)

    def tt(self, eng, out, in0, in1, op, reads, writes):
        return self.P.op(eng, lambda e: e.tensor_tensor(out=out, in0=in0, in1=in1, op=op), reads, writes)

    def ts(self, out, in0, s1, s2, op0, op1, reads, writes, eng="dve"):
        if op1 is None:
            return self.P.op(eng, lambda e: e.tensor_scalar(out=out, in0=in0, scalar1=s1, scalar2=None, op0=op0), reads, writes)
        return self.P.op(eng, lambda e: e.tensor_scalar(out=out, in0=in0, scalar1=s1, scalar2=s2, op0=op0, op1=op1), reads, writes)

    def stt(self, out, in0, scalar, in1, op0, op1, reads, writes):
        return self.P.op("dve", lambda e: e.scalar_tensor_tensor(out=out, in0=in0, scalar=scalar, in1=in1, op0=op0, op1=op1), reads, writes)

    def rstd_from_ss(self, ss, mse, rstd, inv_n, eps, kss, kmse, krstd):
        self.ts(mse, ss, inv_n, eps, ALU.mult, ALU.add, [kss], [kmse])
        nh = self.negh
        self.P.op("pool", lambda e: e.tensor_tensor(out=rstd, in0=mse, in1=nh, op=ALU.pow), [kmse, "negh"], [krstd])

    def build(self):
        nc = self.nc
        dt = nc.dram_tensor
        G = self.groups
        self.din = {}

        def inp(name, shape, dtype=F32):
            self.din[name] = dt(name, list(shape), dtype, kind="ExternalInput").ap()
            return self.din[name]

        def scr(name, shape, dtype=BF16):
            return dt(name, list(shape), dtype, kind="Internal").ap()

        self.x = [inp("x%d" % g, [S, D]) for g, (S, T) in enumerate(G)]
        self.mem = [inp("mem%d" % g, [NMEM, D]) for g in range(len(G))]
        self.rc = [inp("rc%d" % g, [128, S]) for g, (S, T) in enumerate(G)]
        self.rs = [inp("rs%d" % g, [128, S]) for g, (S, T) in enumerate(G)]
        self.dc = [inp("dc%d" % g, [T // 128, 128, S // 128, 128], BF16) for g, (S, T) in enumerate(G)]
        self.ds = [inp("ds%d" % g, [T // 128, 128, S // 128, 128], BF16) for g, (S, T) in enumerate(G)]
        self.y = [dt("y%d" % g, [T, D], F32, kind="ExternalOutput").ap() for g, (S, T) in enumerate(G)]
        w_in = inp("w_in", [D, 4096])
        w_f = inp("w_f", [8, 128, 128])
        ccsc = inp("ccsc", [128, 256])
        lam_in = inp("lam", [1, 256])
        gvec = inp("gvec", [5, D])
        g_sub = inp("g_sub", [128])
        identd = inp("ident", [128, 128], BF16)
        pmatd = inp("pmat", [128, 128], BF16)
        wsrc = {n: inp(n, [D, D]) for n in ("w_out", "w_cq", "w_ck", "w_cv", "w_co")}
        wsrc["w_gate"] = inp("w_gate", [D, DFF])
        wsrc["w_up"] = inp("w_up", [D, DFF])
        wsrc["w_down"] = inp("w_down", [DFF, D])
        WA = scr("WA", [8, 128, 16, 512])
        WB = {n: scr("S_" + n, [4, 128, 16, 512]) for n in ("w_out", "w_cq", "w_ck", "w_cv", "w_co")}
        WB["w_gate"] = scr("S_w_gate", [11, 128, 16, 512])
        WB["w_up"] = scr("S_w_up", [11, 128, 16, 512])
        WB["w_down"] = scr("S_w_down", [4, 128, 44, 512])
        lam_scr = scr("lam_scr", [1, 1], F32)
        KT = [scr("KT%d" % g, [NH, 128, S]) for g, (S, T) in enumerate(G)]
        QT = [scr("QT%d" % g, [NH, 128, T]) for g, (S, T) in enumerate(G)]
        V = [scr("V%d" % g, [S, 1024]) for g, (S, T) in enumerate(G)]
        Z = [scr("Z%d" % g, [S, 2048]) for g, (S, T) in enumerate(G)]
        MIXT = [scr("MIXT%d" % g, [D, T]) for g, (S, T) in enumerate(G)]

        ARENA_BYTES = 199 * 1024
        with (
            nc.sbuf_tensor("arena", [128, ARENA_BYTES // 2], BF16) as arena_t,
            nc.sbuf_tensor("consts", [128, 3072], BF16) as consts_t,
            nc.psum_tensor("psum", [128, 8, 512], F32) as psum,
        ):
            import contextlib
            with contextlib.ExitStack() as es:
                sems_eng = {e: es.enter_context(nc.semaphore("se_" + e)) for e in ENGS}
                sems_dma = {k: es.enter_context(nc.semaphore("sd_" + k)) for k in DMA_SEMS}
                self.P = P = Prog(nc)
                A = Arena(arena_t, ARENA_BYTES)
                C = Arena(consts_t, 6144)
                self.ident = C.alloc([128, 128], BF16)
                self.negh = C.alloc([128, 1], F32)
                lam_b = C.alloc([128, 1], F32)
                gsb = C.alloc([128, 128], F32)
                ones_c = C.alloc([128, 1], BF16)
                self.pmat = C.alloc([128, 128], BF16)
                self.AB = C.alloc([128, 8, 256], BF16)
                grp0 = [self.dma(self.ident, identd, "misc", writes=["ident"])]
                negh = self.negh
                P.op("pool", lambda e: e.memset(negh, -0.5), writes=["negh"])
                P.op("pool", lambda e: e.memset(ones_c, 1.0), writes=["ones_c"])

                def ps_bank(b):
                    return psum[:, b, :]

                def ps_bf(b):
                    return psum[:, b, :].bitcast(BF16).rearrange("p (a b) -> p a b", a=8)

                CONV_W = 3072
                A.reset()
                cw32 = [A.alloc([128, CONV_W], F32) for _ in range(2)]
                cw16 = [A.alloc([128, CONV_W], BF16) for _ in range(2)]
                self.conv_bytes = A.off
                lam_sb = A.alloc([128, 256], F32)
                gs_raw = A.alloc([128, 128], F32)
                cc32 = A.alloc([128, 256], F32)
                wf32 = A.alloc([128, 8, 128], F32)
                grp0.append(self.dma(self.pmat, pmatd, "misc", writes=["pmat"]))
                grp0.append(self.dma(lam_sb[0:1, :], lam_in, "misc", writes=["lam_sb"]))
                grp0.append(self.dma(gs_raw, g_sub.partition_broadcast(128), "misc", writes=["gs_raw"]))
                grp0.append(self.dma(cc32, ccsc, "misc", writes=["cc32"]))
                grp0.append(self.dma(wf32, w_f.rearrange("g c d -> c g d"), "misc", writes=["wf32"]))
                self.seal(grp0)
                tasks = []
                for kc in range(16):
                    rows = slice(kc * 128, (kc + 1) * 128)
                    tasks.append((w_in[rows, 0:2048], 2048,
                                  [(WA[0:2, :, kc, :].rearrange("b p c -> p b c"), 0, 1024),
                                   (WA[6:8, :, kc, :].rearrange("b p c -> p b c"), 1024, 1024)]))
                    tasks.append((w_in[rows, 2048:4096], 2048,
                                  [(WA[2:4, :, kc, :].rearrange("b p c -> p b c"), 0, 1024),
                                   (WA[4:6, :, kc, :].rearrange("b p c -> p b c"), 1024, 1024)]))
                n_fore = len(tasks)
                for kc in range(16):
                    rows = slice(kc * 128, (kc + 1) * 128)
                    for n in ("w_out", "w_cq", "w_ck", "w_cv", "w_co"):
                        tasks.append((wsrc[n][rows, :], 2048, [(WB[n][:, :, kc, :].rearrange("b p c -> p b c"), 0, 2048)]))
                    for n in ("w_gate", "w_up"):
                        tasks.append((wsrc[n][rows, 0:3072], 3072, [(WB[n][0:6, :, kc, :].rearrange("b p c -> p b c"), 0, 3072)]))
                        tasks.append((wsrc[n][rows, 3072:DFF], 2560, [(WB[n][6:11, :, kc, :].rearrange("b p c -> p b c"), 0, 2560)]))
                for kc in range(44):
                    rows = slice(kc * 128, (kc + 1) * 128)
                    tasks.append((wsrc["w_down"][rows, :], 2048, [(WB["w_down"][:, :, kc, :].rearrange("b p c -> p b c"), 0, 2048)]))
                cstate = {"i": 0, "loaded": 0, "stored": 0}

                def conv_store(i):
                    for (dst, c0, nsub) in tasks[i][2]:
                        self.dma(dst, cw16[i % 2][:, c0:c0 + nsub].rearrange("p (b c) -> p b c", c=512), "cs%d" % (i % 2),
                                 reads=[("cw16", i % 2)])

                def conv_step(n, engines):
                    for _ in range(n):
                        i = cstate["i"]
                        if i >= len(tasks):
                            break
                        while cstate["loaded"] <= min(i + 1, len(tasks) - 1):
                            l = cstate["loaded"]
                            self.dma(cw32[l % 2][:, 0:tasks[l][1]], tasks[l][0], "cv%d" % (l % 2), writes=[("cw32", l % 2)])
                            cstate["loaded"] += 1
                        ncols = tasks[i][1]
                        self.tcopy(engines[i % len(engines)], cw16[i % 2][:, 0:ncols], cw32[i % 2][:, 0:ncols], [("cw32", i % 2)], [("cw16", i % 2)])
                        while cstate["stored"] < i:
                            conv_store(cstate["stored"])
                            cstate["stored"] += 1
                        cstate["i"] += 1
                    if cstate["i"] >= len(tasks):
                        while cstate["stored"] < len(tasks):
                            conv_store(cstate["stored"])
                            cstate["stored"] += 1

                def conv_flush():
                    while cstate["stored"] < cstate["i"]:
                        conv_store(cstate["stored"])
                        cstate["stored"] += 1

                self.conv_step = conv_step
                self.conv_flush = conv_flush
                self.conv_left = lambda: len(tasks) - cstate["i"]
                conv_step(n_fore, ("dve", "act", "pool"))
                conv_flush()

                lj = A.alloc([128, 64], F32)
                lsum = A.alloc([128, 2], F32)
                lexp = A.alloc([128, 2], F32)
                lval = A.alloc([128, 1], F32)
                for i in range(2):
                    a0 = lam_sb[0:1, i * 128:i * 128 + 64]
                    a1 = lam_sb[0:1, i * 128 + 64:i * 128 + 128]
                    acc_ = lsum[0:1, i:i + 1]
                    P.op("dve", lambda e, a0=a0, a1=a1, acc_=acc_: e.scalar_tensor_tensor(
                        out=lj[0:1, :], in0=a0, scalar=1.0, in1=a1, op0=ALU.mult, op1=ALU.mult, accum_out=acc_),
                        ["lam_sb"], ["lj", ("lsum", i)])
                self.act(lexp[0:1, :], lsum[0:1, :], AF.Exp, [("lsum", 0), ("lsum", 1)], ["lexp"])
                self.stt(lval[0:1, :], lexp[0:1, 0:1], 0.2, lexp[0:1, 1:2], ALU.add, ALU.subtract, ["lexp"], ["lval"])
                self.dma(lam_scr, lval[0:1, :], "c2", reads=["lval"], writes=["lam_scr"])
                self.dma(lam_b, lam_scr[0, :].partition_broadcast(128), "c2", reads=["lam_scr"], writes=["lam_b"])
                self.ts(gsb, gs_raw, 0.8, None, ALU.mult, None, ["gs_raw"], ["gsb"])
                AB = self.AB
                for g in range(8):
                    for t in range(2):
                        b = (g * 2 + t) % 2
                        self.mm(ps_bank(b)[:, 0:128], cc32[:, t * 128:(t + 1) * 128], wf32[:, g, :], True, True, ["cc32", "wf32"], [("ps", b)])
                        self.tcopy("dve", AB[:, g, t * 128:(t + 1) * 128], ps_bank(b)[:, 0:128], [("ps", b)], [("AB", g, t)])
                P.barrier()

                stop = self.stop_after
                for gi, (S, T) in enumerate(G):
                    if stop == "W":
                        break
                    self.phase_A(A, psum, gi, S, T, gvec, WA, KT[gi], QT[gi], V[gi], Z[gi])
                    self.conv_flush()
                    P.barrier()
                    if stop == "A":
                        break
                    self.phase_B(A, psum, gi, S, T, KT[gi], QT[gi], V[gi], MIXT[gi], lam_b, gsb)
                    self.conv_step(self.conv_left(), ("pool", "act", "dve"))
                    self.conv_flush()
                    P.barrier()
                    if stop == "B":
                        break
                    self.phase_C(A, psum, gi, S, T, Z[gi], MIXT[gi])
                    P.barrier()
                    if stop == "C":
                        break
                    self.phase_D(A, psum, gi, S, T, gvec, WB, MIXT[gi], ones_c)
                    P.barrier()
                if self.debug:
                    dbg_src = {"WA": WA, "KT": KT[0], "QT": QT[0], "V": V[0], "Z": Z[0], "MIXT": MIXT[0], "Wout": WB["w_out"], "Wdown": WB["w_down"]}
                    for i, nm in enumerate(self.debug):
                        src = dbg_src[nm]
                        shp = list(src.shape)
                        dst = dt("dbg_" + nm, shp, BF16, kind="ExternalOutput").ap()
                        if len(shp) == 4:
                            for b_ in range(shp[0]):
                                self.dma(dst[b_], src[b_], "misc")
                        elif len(shp) == 3:
                            for b_ in range(shp[0]):
                                self.dma(dst[b_], src[b_], "misc")
                        else:
                            self.dma(dst, src, "misc")
                P.emit(sems_eng, sems_dma)
        return nc

    def norm_transpose(self, psum, xin, kx, gb, sm, hbf, khbf, hT, khT, tcol, tbanks):
        ss, mse, rstd, ksm = sm
        self.act(hbf, xin, AF.Square, [kx], [khbf, (ksm, "ss")], accum=ss)
        self.rstd_from_ss(ss, mse, rstd, 1.0 / D, 1e-6, (ksm, "ss"), (ksm, "mse"), (ksm, "rstd"))
        self.stt(hbf, xin, rstd, gb, ALU.mult, ALU.mult, [kx, (ksm, "rstd"), "gb"], [khbf])
        for half in range(2):
            b = tbanks[half]
            pst = psum[:, b, :].bitcast(BF16).rearrange("p (a b) -> p a b", a=8)
            for j in range(8):
                kc = half * 8 + j
                self.tr(pst[:, j, :], hbf[:, kc * 128:(kc + 1) * 128], [khbf], [("ps", b)])
            self.evac(hT[:, half * 8:(half + 1) * 8, tcol:tcol + 128], pst, [("ps", b)], [(khT, half, tcol)])

    def norm_only(self, xin, kx, gb, sm, hbf, khbf):
        ss, mse, rstd, ksm = sm
        self.act(hbf, xin, AF.Square, [kx], [khbf, (ksm, "ss")], accum=ss)
        self.rstd_from_ss(ss, mse, rstd, 1.0 / D, 1e-6, (ksm, "ss"), (ksm, "mse"), (ksm, "rstd"))
        self.stt(hbf, xin, rstd, gb, ALU.mult, ALU.mult, [kx, (ksm, "rstd"), "gb"], [khbf])

    def transpose_only(self, psum, hbf, khbf, hT, khT, tcol, tbanks):
        for half in range(2):
            b = tbanks[half]
            pst = psum[:, b, :].bitcast(BF16).rearrange("p (a b) -> p a b", a=8)
            for j in range(8):
                kc = half * 8 + j
                self.tr(pst[:, j, :], hbf[:, kc * 128:(kc + 1) * 128], [khbf], [("ps", b)])
            self.evac(hT[:, half * 8:(half + 1) * 8, tcol:tcol + 128], pst, [("ps", b)], [(khT, half, tcol)])

    def phase_A(self, A, psum, gi, S, T, gvec, WA, KT, QT, V, Z):
        P = self.P
        A.reset(self.conv_bytes if self.conv_left() > 0 else 0)
        ntile = S // 512
        nown = T // 512
        gb = A.alloc([128, D], F32)
        self.dma(gb, gvec[0, :].partition_broadcast(128), "misc", writes=["gb"])
        xin = [A.alloc([128, D], F32) for _ in range(2)]
        hbf = [A.alloc([128, D], BF16) for _ in range(4)]
        sms = [(A.alloc([128, 1], F32), A.alloc([128, 1], F32), A.alloc([128, 1], F32), ("smA", i)) for i in range(4)]
        hT = [A.alloc([128, 16, 512], BF16) for _ in range(2)]
        R = 6
        wblk = [A.alloc([128, 8, 512], BF16) for _ in range(R)]
        rct = [A.alloc([128, 512], F32) for _ in range(2)]
        rst = [A.alloc([128, 512], F32) for _ in range(2)]
        stg = [A.alloc([128, 4, 512], BF16) for _ in range(2)]
        zstg = [A.alloc([128, 4, 1024], BF16) for _ in range(2)]
        fsb = [A.alloc([128, 512], BF16) for _ in range(2)]
        t1 = [A.alloc([128, 512], F32) for _ in range(2)]
        t2 = [A.alloc([128, 512], F32) for _ in range(2)]
        AB = self.AB
        pmat = self.pmat
        xcnt = [0]

        def prologue(ti):
            hs = ti % 2
            self.seal([self.dma(rct[hs], self.rc[gi][:, ti * 512:(ti + 1) * 512], "rt%d" % hs, writes=[("rct", hs)]),
                       self.dma(rst[hs], self.rs[gi][:, ti * 512:(ti + 1) * 512], "rt%d" % hs, writes=[("rst", hs)])])
            for st in range(4):
                xs = xcnt[0] % 2
                xcnt[0] += 1
                r0 = ti * 512 + st * 128
                self.dma(xin[xs], self.x[gi][r0:r0 + 128, :], "x%d" % xs, writes=[("xin", xs)])
                self.norm_only(xin[xs], ("xin", xs), gb, sms[st], hbf[st], ("hbf", st))

        def prologue_tr(ti):
            hs = ti % 2
            for st in range(4):
                self.transpose_only(psum, hbf[st], ("hbf", st), hT[hs], ("hT", hs), st * 128, (6, 7))

        jobs = []
        for ti in range(ntile):
            for b in range(2):
                jobs.append((ti, "four", b, ("Z", b)))
            for b in range(2):
                jobs.append((ti, "tok", 4 + b, ("V", b)))
            for b in range(2):
                jobs.append((ti, "rope", 2 + b, ("K", b)))
            if ti < nown:
                for b in range(2):
                    jobs.append((ti, "rope", 6 + b, ("Q", b)))
        akinds = getattr(self, "akinds", None)
        if akinds is not None:
            jobs = [j for j in jobs if j[1] in akinds]
        issued = [0]
        nfills = 2 * len(jobs)

        def ensure(upto):
            while issued[0] <= min(upto, nfills - 1):
                f = issued[0]
                blk = jobs[f // 2][2]
                kh = f % 2
                s = f % R
                self.dma(wblk[s], WA[blk, :, kh * 8:(kh + 1) * 8, :], "w%d" % s, writes=[("wblk", s)])
                issued[0] += 1

        psc = [0]

        def bank():
            b = psc[0] % 6
            psc[0] += 1
            return b

        pend = []
        ucnt = [0]

        def flush():
            while pend:
                pend.pop(0)()

        prologue(0)
        prologue_tr(0)
        cur_t = -1
        for ji, (ti, kind, blk, dest) in enumerate(jobs):
            if ti != cur_t:
                cur_t = ti
                flush()
                if ti + 1 < ntile:
                    prologue(ti + 1)
            if ti + 1 < ntile and (ji + 1 == len(jobs) or jobs[ji + 1][0] != ti):
                prologue_tr(ti + 1)
            hs = ti % 2
            f0 = 2 * ji
            ensure(f0 + R - 1)
            if self.conv_left() > 0:
                self.conv_step(1, ("pool", "act", "pool", "dve"))
            ss_ = ji % 2
            r0 = ti * 512
            if kind == "tok":
                banks = [bank() for _ in range(4)]
                for kh in range(2):
                    s = (f0 + kh) % R
                    for st in range(4):
                        for k8 in range(8):
                            kc = kh * 8 + k8
                            self.mm(psum[:, banks[st], :], hT[hs][:, kc, st * 128:(st + 1) * 128], wblk[s][:, k8, :],
                                    kc == 0, kc == 15, [(("hT", hs), kc // 8, st * 128), ("wblk", s)], [("ps", banks[st])])
                flush()
                for st in range(4):
                    self.evac(stg[ss_][:, st, :], psum[:, banks[st], :], [("ps", banks[st])], [("stg", ss_, st)], act_share=2, of=3)
                dst = V[r0:r0 + 512, dest[1] * 512:(dest[1] + 1) * 512]
                self.dma(dst.rearrange("(s p) c -> p s c", p=128), stg[ss_], "st%d" % ss_, reads=[("stg", ss_, st) for st in range(4)])
                continue
            for oc in range(4):
                b = bank()
                for kh in range(2):
                    s = (f0 + kh) % R
                    for k8 in range(8):
                        kc = kh * 8 + k8
                        self.mm(psum[:, b, :], wblk[s][:, k8, oc * 128:(oc + 1) * 128], hT[hs][:, kc, :],
                                kc == 0, kc == 15, [(("hT", hs), kc // 8, c) for c in (0, 128, 256, 384)] + [("wblk", s)], [("ps", b)])
                flush()
                u = ucnt[0] % 2
                ucnt[0] += 1
                self.tcopy("act", fsb[u], psum[:, b, :], [("ps", b)], [("fsb", u), ("ps", b)])
                if kind == "rope":
                    self.tt("dve", t1[u], psum[:, b, :], rct[hs], ALU.mult, [("ps", b), ("rct", hs)], [("t1", u)])

                    def stage2(u=u, oc=oc, hs=hs, ss_=ss_, dest=dest, ti=ti):
                        b2 = bank()
                        self.mm(psum[:, b2, :], pmat, fsb[u], True, True, ["pmat", ("fsb", u)], [("ps", b2)])
                        self.tt("dve", t2[u], psum[:, b2, :], rst[hs], ALU.mult, [("ps", b2), ("rst", hs)], [("t2", u)])
                        self.tt("pool", stg[ss_][:, oc, :], t1[u], t2[u], ALU.add, [("t1", u), ("t2", u)], [("stg", ss_, oc)])
                        if oc == 3:
                            h0 = dest[1] * 4
                            dd = KT if dest[0] == "K" else QT
                            dst = dd[h0:h0 + 4, :, ti * 512:(ti + 1) * 512]
                            self.dma(dst.rearrange("h p c -> p h c"), stg[ss_], "st%d" % ss_, reads=[("stg", ss_, o_) for o_ in range(4)])
                    pend.append(stage2)
                else:
                    def stage2(u=u, oc=oc, ss_=ss_, dest=dest, r0=r0, blk=blk):
                        g = blk * 4 + oc
                        for sp in range(2):
                            b2 = bank()
                            for q_ in range(2):
                                st = sp * 2 + q_
                                self.mm(psum[:, b2, q_ * 256:(q_ + 1) * 256], fsb[u][:, st * 128:(st + 1) * 128], AB[:, g, :], True, True,
                                        [("fsb", u), ("AB", g)], [("ps", b2)])
                            self.evac(zstg[ss_][:, sp * 2:sp * 2 + 2, oc * 256:(oc + 1) * 256], psum[:, b2, :].rearrange("p (a b) -> p a b", a=2),
                                      [("ps", b2)], [("zstg", ss_, oc, sp)], act_share=1, of=3)
                        if oc == 3:
                            dst = Z[r0:r0 + 512, dest[1] * 1024:(dest[1] + 1) * 1024]
                            self.dma(dst.rearrange("(s p) c -> p s c", p=128), zstg[ss_], "c%d" % ss_,
                                     reads=[("zstg", ss_, o_, p_) for o_ in range(4) for p_ in range(2)])
                    pend.append(stage2)
        flush()

    def phase_B(self, A, psum, gi, S, T, KT, QT, V, MIXT, lam_b, gsb):
        P = self.P
        A.reset(self.conv_bytes if self.conv_left() > 0 else 0)
        nkt = S // 128
        nqb = T // 512
        KTs = [A.alloc([128, S], BF16) for _ in range(2)]
        QTz = [[A.alloc([128, T], BF16) for _ in range(2)] for _ in range(2)]
        VA = [A.alloc([128, nkt, 130], BF16) for _ in range(2)]
        pt = [A.alloc([128, 1024], BF16) for _ in range(3)]
        t32 = [A.alloc([128, 128], F32) for _ in range(2)]
        o32 = [A.alloc([128, 128], F32) for _ in range(2)]
        jk = A.alloc([128, 128], F32)
        onb = [A.alloc([128, 4, 128], BF16) for _ in range(2)]
        oTs = [A.alloc([128, 512], BF16) for _ in range(2)]
        sm = [[A.alloc([128, 1], F32) for _ in range(6)] for _ in range(2)]
        for s in range(2):
            va = VA[s]
            P.op("pool", lambda e, va=va: e.memset(va[:, :, 128:129], 1.0), writes=[("VAone", s)])
            z0 = QTz[0][s][64:128, :]
            z1 = QTz[1][s][0:64, :]
            P.op("pool", lambda e, z0=z0: e.memset(z0, 0.0), writes=[("QTzero", s, 0)])
            P.op("dve", lambda e, z1=z1: e.memset(z1, 0.0), writes=[("QTzero", s, 1)])

        def load_head(h):
            s = h % 2
            grp = [self.dma(KTs[s], KT[h], "hd%d" % s, writes=[("KTs", s)]),
                   self.dma(QTz[0][s][0:64, :], QT[h, 0:64, :], "hd%d" % s, writes=[("QTs", s, 0)]),
                   self.dma(QTz[1][s][64:128, :], QT[h, 64:128, :], "hd%d" % s, writes=[("QTs", s, 1)])]
            step = 16
            for k0 in range(0, nkt, step):
                k1 = min(nkt, k0 + step)
                grp.append(self.dma(VA[s][:, k0:k1, 0:128], V[k0 * 128:k1 * 128, h * 128:(h + 1) * 128].rearrange("(k p) c -> p k c", p=128),
                                    "hd%d" % s, writes=[("VA", s, k0)]))
            self.seal(grp)

        def acc_ap(c, j):
            a = c * 4 + j
            return psum[:, 4 + a // 3, (a % 3) * 129:(a % 3) * 129 + 129], 4 + a // 3, (a % 3 == 0)

        load_head(0)
        fin = [0]
        for h in range(NH):
            s = h % 2
            if h + 1 < NH:
                load_head(h + 1)
            vkeys = [("VA", s, k0) for k0 in range(0, nkt, 16)] + [("VAone", s)]
            for qb in range(nqb):
                qsl = slice(qb * 512, (qb + 1) * 512)
                if self.conv_left() > 0:
                    it_left = (NH - h) * nqb - qb
                    self.conv_step(-(-self.conv_left() // max(1, it_left - 2)) if it_left > 2 else self.conv_left(), ("pool",))

                def ST(kt):
                    sl = kt % 2
                    for c in range(2):
                        self.mm(psum[:, sl * 2 + c, :], KTs[s][:, kt * 128:(kt + 1) * 128], QTz[c][s][:, qsl],
                                True, True, [("KTs", s), ("QTs", s, c), ("QTzero", s, c)], [("pss", sl, c)])

                ST(0)
                for kt in range(nkt):
                    if kt + 1 < nkt:
                        ST(kt + 1)
                    sl = kt % 2
                    p3 = kt % 3
                    for c in range(2):
                        self.act(pt[p3][:, c * 512:(c + 1) * 512], psum[:, sl * 2 + c, :], AF.Exp, [("pss", sl, c)], [("pt", p3, c)], scale=0.125)
                    for c in range(2):
                        for j in range(4):
                            ap_, bank, first = acc_ap(c, j)
                            self.mm(ap_, pt[p3][:, c * 512 + j * 128:c * 512 + (j + 1) * 128], VA[s][:, kt, 0:129],
                                    (kt == 0 and first), kt == nkt - 1, [("pt", p3, c), ("VA", s, (kt // 16) * 16), ("VAone", s)], [("acc", bank)], skip=True)
                fs = fin[0] % 2
                fin[0] += 1
                for j in range(4):
                    a0, b0, _ = acc_ap(0, j)
                    a1, b1, _ = acc_ap(1, j)
                    u = j % 2
                    r0, r1, r1l, ss, mse, rstd = sm[u]
                    ku = ("smB", u)
                    P.op("dve", lambda e, r0=r0, a0=a0: e.reciprocal(out=r0, in_=a0[:, 128:129]), [("acc", b0)], [(ku, "r0")])
                    P.op("dve", lambda e, r1=r1, a1=a1: e.reciprocal(out=r1, in_=a1[:, 128:129]), [("acc", b1)], [(ku, "r1")])
                    self.tt("dve", r1l, r1, lam_b, ALU.mult, [(ku, "r1"), "lam_b"], [(ku, "r1l")])
                    self.ts(t32[u], a1[:, 0:128], r1l, None, ALU.mult, None, [("acc", b1), (ku, "r1l")], [("t32", u)])
                    self.stt(o32[u], a0[:, 0:128], r0, t32[u], ALU.mult, ALU.subtract, [("acc", b0), (ku, "r0"), ("t32", u)], [("o32", u)])
                    o_ = o32[u]
                    P.op("dve", lambda e, o_=o_, ss=ss: e.scalar_tensor_tensor(out=jk, in0=o_, scalar=1.0, in1=o_,
                                                                             op0=ALU.mult, op1=ALU.mult, accum_out=ss),
                         [("o32", u)], ["jk", (ku, "ss")])
                    self.rstd_from_ss(ss, mse, rstd, 1.0 / 128, 1e-5, (ku, "ss"), (ku, "mse"), (ku, "rstd"))
                    self.stt(onb[fs][:, j, :], o32[u], rstd, gsb, ALU.mult, ALU.mult, [("o32", u), (ku, "rstd"), "gsb"], [("onb", fs, j)])
                    pst = psum[:, 7, :].bitcast(BF16).rearrange("p (a b) -> p a b", a=8)
                    self.tr(pst[:, j, :], onb[fs][:, j, :], [("onb", fs, j)], [("pstB", 0)])
                pst = psum[:, 7, :].bitcast(BF16)
                self.tcopy("dve", oTs[fs], pst[:, 0:512], [("pstB", 0)], [("oTs", fs)])
                self.dma(MIXT[1024 + h * 128:1024 + (h + 1) * 128, qsl], oTs[fs], "st%d" % fs, reads=[("oTs", fs)])

    def phase_C(self, A, psum, gi, S, T, Z, MIXT):
        P = self.P
        A.reset()
        nch = S // 128
        NCH = min(32, nch)
        nfill = nch // NCH
        nkt = T // 128
        Zs = A.alloc([128, nch, 1024], BF16)
        R = 3
        dcs = [A.alloc([128, NCH, 128], BF16) for _ in range(R)]
        dss = [A.alloc([128, NCH, 128], BF16) for _ in range(R)]
        ysb = [A.alloc([128, 512], BF16) for _ in range(2)]
        yT = [A.alloc([128, 4, 128], BF16) for _ in range(2)]
        for half in range(2):
            step = 16
            zk = []
            for k0 in range(0, nch, step):
                k1 = min(nch, k0 + step)
                zk.append(self.dma(Zs[:, k0:k1, :], Z[k0 * 128:k1 * 128, half * 1024:(half + 1) * 1024].rearrange("(k p) c -> p k c", p=128),
                                   "z0", writes=[("Zs", k0)]))
            self.seal(zk)
            fl = [(kt, f) for kt in range(nkt) for f in range(nfill)]
            issued = [0]

            def ensure(upto):
                while issued[0] <= min(upto, len(fl) - 1):
                    i = issued[0]
                    kt, f = fl[i]
                    s = i % R
                    self.seal([self.dma(dcs[s], self.dc[gi][kt, :, f * NCH:(f + 1) * NCH, :], "df%d" % s, writes=[("dcs", s)]),
                               self.dma(dss[s], self.ds[gi][kt, :, f * NCH:(f + 1) * NCH, :], "df%d" % s, writes=[("dss", s)])])
                    issued[0] += 1

            for i, (kt, f) in enumerate(fl):
                ensure(i + R - 1)
                s = i % R
                b = kt % 2
                for n in range(NCH):
                    ncg = f * NCH + n
                    zrow = Zs[:, ncg, :].rearrange("p (g c) -> p g c", g=4)
                    self.mm(psum[:, b, :], dcs[s][:, n, :], zrow[:, :, 0:128], (ncg == 0), False, [("dcs", s), ("Zs", (ncg // 16) * 16)], [("psC", b)])
                    self.mm(psum[:, b, :], dss[s][:, n, :], zrow[:, :, 128:256], False, (ncg == nch - 1), [("dss", s), ("Zs", (ncg // 16) * 16)], [("psC", b)])
                if f == nfill - 1:
                    self.evac(ysb[b], psum[:, b, :], [("psC", b)], [("ysb", b)])
                    pst = psum[:, 6 + b, :].bitcast(BF16).rearrange("p (a b) -> p a b", a=8)
                    for g in range(4):
                        self.tr(pst[:, g, :], ysb[b][:, g * 128:(g + 1) * 128], [("ysb", b)], [("pstC", b)])
                    self.evac(yT[b], pst[:, 0:4, :], [("pstC", b)], [("yT", b)])
                    self.dma(MIXT[half * 512:(half + 1) * 512, kt * 128:(kt + 1) * 128].rearrange("(g p) c -> p g c", p=128), yT[b],
                             "st%d" % b, reads=[("yT", b)])

    def phase_D(self, A, psum, gi, S, T, gvec, WB, MIXT, ones_c):
        P = self.P
        A.reset()
        ntile = T // 512
        xres = A.alloc([128, 4, D], F32)
        actT = A.alloc([128, 16, 512], BF16)
        hT = A.alloc([128, 16, 512], BF16)
        hid = A.alloc([128, 24, 512], BF16)
        qT = [A.alloc([128, 4, 512], BF16) for _ in range(2)]
        R = 6
        wblk = [A.alloc([128, 8, 512], BF16) for _ in range(R)]
        kTm = A.alloc([128, 16, 256], BF16)
        vm = A.alloc([128, 2, D], BF16)
        gb = A.alloc([128, D], F32)
        hbf = [A.alloc([128, D], BF16) for _ in range(2)]
        pT = [A.alloc([128, 2, 512], BF16) for _ in range(2)]
        ocn = [A.alloc([128, 512], BF16) for _ in range(2)]
        sg = [A.alloc([128, 512], BF16) for _ in range(2)]
        sms = [(A.alloc([128, 1], F32), A.alloc([128, 1], F32), A.alloc([128, 1], F32), ("smD", i)) for i in range(2)]
        rl = [A.alloc([128, 1], F32) for _ in range(2)]
        memT = A.alloc([128, 16, 256], BF16)

        def blockfills(name, blk, kc0=0, kc1=16):
            out = []
            k = kc0
            while k < kc1:
                n = min(8, kc1 - k)
                out.append((name, blk, k, n))
                k += n
            return out

        fills = []
        for blk in range(4):
            fills += blockfills("w_ck", blk)
        for blk in range(4):
            fills += blockfills("w_cv", blk)
        halves = [(0, 6), (6, 11)]
        dstop = getattr(self, "dstop", 99)
        for ti in range(ntile):
            for blk in range(4):
                fills += blockfills("w_out", blk)
            if dstop <= 2:
                continue
            for blk in range(4):
                fills += blockfills("w_cq", blk)
            for blk in range(4):
                fills += blockfills("w_co", blk)
            if dstop <= 5:
                continue
            for (b0, b1) in halves:
                for blk in range(b0, b1):
                    fills += blockfills("w_gate", blk)
                    fills += blockfills("w_up", blk)
                for cb in range(4):
                    fills += blockfills("w_down", cb, b0 * 4, b1 * 4)
        issued = [0]
        used = [0]

        def ensure(upto):
            while issued[0] <= min(upto, len(fills) - 1):
                i = issued[0]
                name, blk, k0, n = fills[i]
                s = i % R
                self.dma(wblk[s][:, 0:n, :], WB[name][blk, :, k0:k0 + n, :], "w%d" % s, writes=[("wblk", s)])
                issued[0] += 1

        def next_fill(expect):
            i = used[0]
            assert fills[i][0] == expect, (fills[i], expect)
            ensure(i + R - 4)
            used[0] += 1
            return i % R, fills[i]

        self.dma(gb, gvec[2, :].partition_broadcast(128), "misc", writes=["gb"])
        for mt in range(2):
            self.dma(xres[:, mt, :], self.mem[gi][mt * 128:(mt + 1) * 128, :], "x%d" % mt, writes=[("xres", mt)])
            self.norm_transpose(psum, xres[:, mt, :], ("xres", mt), gb, sms[mt], hbf[mt], ("hbf", mt), memT, "memT", mt * 128, (6, 7))
        mk = [("memT", h, c) for h in range(2) for c in (0, 128)]
        evc = 0
        for blk in range(4):
            f = [next_fill("w_ck"), next_fill("w_ck")]
            for oc in range(4):
                b = (blk * 4 + oc) % 4
                for kh in range(2):
                    s = f[kh][0]
                    for k8 in range(8):
                        kc = kh * 8 + k8
                        self.mm(psum[:, b, 0:256], wblk[s][:, k8, oc * 128:(oc + 1) * 128], memT[:, kc, :], kc == 0, kc == 15,
                                [("wblk", s)] + [("memT", kc // 8, c) for c in (0, 128)], [("ps", b)])
                self.evac(kTm[:, blk * 4 + oc, :], psum[:, b, 0:256], [("ps", b)], [("kTm", blk * 4 + oc)])
        for blk in range(4):
            f = [next_fill("w_cv"), next_fill("w_cv")]
            for mt in range(2):
                b = (blk * 2 + mt) % 4
                for kh in range(2):
                    s = f[kh][0]
                    for k8 in range(8):
                        kc = kh * 8 + k8
                        self.mm(psum[:, b, :], memT[:, kc, mt * 128:(mt + 1) * 128], wblk[s][:, k8, :], kc == 0, kc == 15,
                                [("wblk", s), ("memT", kc // 8, mt * 128)], [("ps", b)])
                self.evac(vm[:, mt, blk * 512:(blk + 1) * 512], psum[:, b, :], [("ps", b)], [("vm", mt, blk)])

        def tokmajor_accum(name, lhs, lhs_keys, nk0, nk1):
            for cb in range(4):
                k = nk0
                while k < nk1:
                    s, (nm, blk, k0, n) = next_fill(name)
                    assert blk == cb and k0 == k
                    for st in range(4):
                        for k8 in range(n):
                            kc = k + k8
                            self.mm(psum[:, st, :], lhs[:, kc - lhs_base[0], st * 128:(st + 1) * 128], wblk[s][:, k8, :], kc == nk0, kc == nk1 - 1,
                                    [("wblk", s)] + lhs_keys(kc, st), [("ps", st)])
                    k += n
                for st in range(4):
                    xs = xres[:, st, cb * 512:(cb + 1) * 512]
                    self.tt("dve", xs, psum[:, st, :], xs, ALU.add, [("ps", st), ("xres", st)], [("xres", st)])

        lhs_base = [0]

        def norm_all(grow):
            self.dma(gb, gvec[grow, :].partition_broadcast(128), "misc", writes=["gb"])
            for st in range(4):
                self.norm_transpose(psum, xres[:, st, :], ("xres", st), gb, sms[st % 2], hbf[st % 2], ("hbf", st % 2), hT, "hT", st * 128, (6, 7))

        hTk = lambda kc: [("hT", kc // 8, c) for c in (0, 128, 256, 384)]
        cross_scale = 512 ** -0.5

        for ti in range(ntile):
            r0 = ti * 512
            self.dma(xres, self.x[gi][r0:r0 + 512, :].rearrange("(s p) c -> p s c", p=128), "x0", writes=[("xres", st) for st in range(4)])
            self.dma(actT, MIXT[:, r0:r0 + 512].rearrange("(k p) c -> p k c", p=128), "x1", writes=[("actT", kc) for kc in range(16)])
            lhs_base[0] = 0
            tokmajor_accum("w_out", actT, lambda kc, st: [("actT", kc)], 0, 16)
            def dump():
                self.dma(self.y[gi][r0:r0 + 512, :].rearrange("(s p) c -> p s c", p=128), xres, "out%d" % (ti % 2),
                         reads=[("xres", st) for st in range(4)])
            if dstop <= 1:
                dump()
                continue
            norm_all(1)
            if dstop == 2:
                if ti == 0 and gi == 0:
                    dh = self.nc.dram_tensor("dbg_hT", [128, 16, 512], BF16, kind="ExternalOutput").ap()
                    self.dma(dh, hT, "misc", reads=[("hT", h_, c_) for h_ in range(2) for c_ in (0, 128, 256, 384)])
                    dm_ = self.nc.dram_tensor("dbg_memT", [128, 16, 256], BF16, kind="ExternalOutput").ap()
                    self.dma(dm_, memT, "misc", reads=[("memT", h_, c_) for h_ in range(2) for c_ in (0, 128)])
                    dk = self.nc.dram_tensor("dbg_kTm", [128, 16, 256], BF16, kind="ExternalOutput").ap()
                    self.dma(dk, kTm, "misc", reads=[("kTm", i_) for i_ in range(16)])
                    dv = self.nc.dram_tensor("dbg_vm", [128, 2, 2048], BF16, kind="ExternalOutput").ap()
                    self.dma(dv, vm, "misc", reads=[("vm", m_, b_) for m_ in range(2) for b_ in range(4)])
                dump()
                continue
            for hh in range(4):
                f = [next_fill("w_cq"), next_fill("w_cq")]
                qs = hh % 2
                for oc in range(4):
                    b = oc
                    for kh in range(2):
                        s = f[kh][0]
                        for k8 in range(8):
                            kc = kh * 8 + k8
                            self.mm(psum[:, b, :], wblk[s][:, k8, oc * 128:(oc + 1) * 128], hT[:, kc, :], kc == 0, kc == 15,
                                    [("wblk", s)] + hTk(kc), [("ps", b)])
                    self.evac(qT[qs][:, oc, :], psum[:, b, :], [("ps", b)], [("qT", qs, oc)])
                ps_ = hh % 2
                for mt in range(2):
                    b = 4 + mt
                    for dc_ in range(4):
                        self.mm(psum[:, b, :], kTm[:, hh * 4 + dc_, mt * 128:(mt + 1) * 128], qT[qs][:, dc_, :], dc_ == 0, dc_ == 3,
                                [("kTm", hh * 4 + dc_), ("qT", qs, dc_)], [("ps", b)])
                    self.act(pT[ps_][:, mt, :], psum[:, b, :], AF.Exp, [("ps", b)], [("pT", ps_, mt)], scale=cross_scale)
                for st in range(4):
                    b = st % 2
                    for mt in range(2):
                        self.mm(psum[:, b, :], pT[ps_][:, mt, st * 128:(st + 1) * 128], vm[:, mt, hh * 512:(hh + 1) * 512], mt == 0, mt == 1,
                                [("pT", ps_, mt), ("vm", mt, hh)], [("ps", b)])
                    lb = 2 + (st % 2)
                    for mt in range(2):
                        self.mm(psum[:, lb, 0:1], pT[ps_][:, mt, st * 128:(st + 1) * 128], ones_c, mt == 0, mt == 1,
                                [("pT", ps_, mt), "ones_c"], [("ps", lb)])
                    rl_ = rl[st % 2]
                    lsrc = psum[:, lb, 0:1]
                    P.op("dve", lambda e, rl_=rl_, lsrc=lsrc: e.reciprocal(out=rl_, in_=lsrc), [("ps", lb)], [("rl", st % 2)])
                    self.ts(ocn[st % 2], psum[:, b, :], rl_, None, ALU.mult, None, [("ps", b), ("rl", st % 2)], [("ocn", st % 2)])
                    tb = 6 + (st % 2)
                    pst = psum[:, tb, :].bitcast(BF16).rearrange("p (a b) -> p a b", a=8)
                    for dc_ in range(4):
                        self.tr(pst[:, dc_, :], ocn[st % 2][:, dc_ * 128:(dc_ + 1) * 128], [("ocn", st % 2)], [("ps", tb)])
                    self.evac(actT[:, hh * 4:(hh + 1) * 4, st * 128:(st + 1) * 128], pst[:, 0:4, :], [("ps", tb)],
                              [("actT", hh * 4 + d_) for d_ in range(4)])
            tokmajor_accum("w_co", actT, lambda kc, st: [("actT", kc)], 0, 16)
            if dstop <= 5:
                dump()
                continue
            norm_all(3)
            for (b0, b1) in halves:
                for blk in range(b0, b1):
                    fg = [next_fill("w_gate"), next_fill("w_gate")]
                    fu = [next_fill("w_up"), next_fill("w_up")]
                    for oc in range(4):
                        bg = (oc % 2) * 2
                        bu = bg + 1
                        for (ff, bb) in ((fg, bg), (fu, bu)):
                            for kh in range(2):
                                s = ff[kh][0]
                                for k8 in range(8):
                                    kc = kh * 8 + k8
                                    self.mm(psum[:, 4 + bb, :], wblk[s][:, k8, oc * 128:(oc + 1) * 128], hT[:, kc, :], kc == 0, kc == 15,
                                            [("wblk", s)] + hTk(kc), [("ps", 4 + bb)])
                        sgs = oc % 2
                        self.act(sg[sgs], psum[:, 4 + bg, :], AF.Silu, [("ps", 4 + bg)], [("sg", sgs)])
                        hc = (blk - b0) * 4 + oc
                        self.tt("dve", hid[:, hc, :], psum[:, 4 + bu, :], sg[sgs], ALU.mult, [("ps", 4 + bu), ("sg", sgs)], [("hid", hc)])
                lhs_base[0] = b0 * 4
                tokmajor_accum("w_down", hid, lambda kc, st: [("hid", kc - lhs_base[0])], b0 * 4, b1 * 4)
            if dstop <= 8:
                dump()
                continue
            self.dma(gb, gvec[4, :].partition_broadcast(128), "misc", writes=["gb"])
            for st in range(4):
                ss, mse, rstd, ksm = sms[st % 2]
                hb = hbf[st % 2]
                self.act(hb, xres[:, st, :], AF.Square, [("xres", st)], [("hbf", st % 2), (ksm, "ss")], accum=ss)
                self.rstd_from_ss(ss, mse, rstd, 1.0 / D, 1e-6, (ksm, "ss"), (ksm, "mse"), (ksm, "rstd"))
                self.stt(xres[:, st, :], xres[:, st, :], rstd, gb, ALU.mult, ALU.mult, [("xres", st), (ksm, "rstd"), "gb"], [("xres", st)])
            self.dma(self.y[gi][r0:r0 + 512, :].rearrange("(s p) c -> p s c", p=128), xres, "out%d" % (ti % 2),
                     reads=[("xres", st) for st in range(4)])
        assert used[0] == len(fills), (used[0], len(fills))


_TABLE_CACHE = {}


def _perm(S, own0, T):
    own = np.arange(own0, own0 + T)
    rest = np.concatenate([np.arange(0, own0), np.arange(own0 + T, S)])
    return np.concatenate([own, rest]).astype(np.int64)


def _rope_tables(S, perm):
    inv = (1.0 / (np.float32(10000.0) ** (np.arange(0, 64, 2, dtype=np.float32) / np.float32(64)))).astype(np.float32)
    pos = perm.astype(np.float32)
    ang = (pos[:, None] * inv[None, :]).astype(np.float32)
    ang = np.concatenate([ang, ang], axis=1)
    c = np.cos(ang).astype(np.float32)
    s = np.sin(ang).astype(np.float32)
    sgn = np.concatenate([-np.ones(32, np.float32), np.ones(32, np.float32)])
    s = s * sgn[None, :]
    cT = np.ascontiguousarray(np.concatenate([c, c], axis=1).T)
    sT = np.ascontiguousarray(np.concatenate([s, s], axis=1).T)
    return cT, sT


def _dft_tables(S, perm, own0, T):
    key = (S, own0, T)
    if key in _TABLE_CACHE:
        return _TABLE_CACHE[key]
    k = np.arange(own0, own0 + T, dtype=np.int64)
    scale = 1.0 / np.sqrt(S * 128.0)
    nchunk = S // 128
    dc = np.empty((T // 128, 128, nchunk, 128), dtype=bf16_np)
    ds = np.empty((T // 128, 128, nchunk, 128), dtype=bf16_np)
    tab_c = (np.cos(2 * np.pi * np.arange(S) / S) * scale).astype(np.float32)
    tab_s = (-np.sin(2 * np.pi * np.arange(S) / S) * scale).astype(np.float32)
    n2 = perm.reshape(nchunk, 128)
    for kt in range(T // 128):
        kk = k[kt * 128:(kt + 1) * 128]
        m = (n2[:, :, None] * kk[None, None, :]) % S
        dc[kt] = tab_c[m].transpose(1, 0, 2).astype(bf16_np)
        ds[kt] = tab_s[m].transpose(1, 0, 2).astype(bf16_np)
    _TABLE_CACHE[key] = (dc, ds)
    return dc, ds


def _shared_inputs(inp):
    f32 = np.float32
    w_in = np.ascontiguousarray(np.asarray(inp["w_in"], f32)[0])
    m = np.arange(128)
    pm = np.zeros((128, 128), np.float32)
    pm[(m // 64) * 64 + ((m % 64) + 32) % 64, m] = 1.0
    c = np.arange(128)
    ang = 2 * np.pi * np.outer(c, c) / 128.0
    ccsc = np.concatenate([np.cos(ang), np.sin(ang)], axis=1).astype(f32)
    lam = np.concatenate([np.asarray(inp[n], f32)[0] for n in ("lambda_q1", "lambda_k1", "lambda_q2", "lambda_k2")])[None, :]
    gvec = np.stack([np.asarray(inp["g_mix"], f32)[0], np.asarray(inp["g_cross"], f32)[0], np.asarray(inp["g_mem"], f32)[0],
                     np.asarray(inp["g_ffn"], f32)[0], np.asarray(inp["g_final"], f32)])
    sh = {
        "w_in": w_in, "pmat": pm.astype(bf16_np),
        "w_f": np.ascontiguousarray(np.asarray(inp["w_fourier"], f32)[0]),
        "ccsc": np.ascontiguousarray(ccsc), "lam": np.ascontiguousarray(lam), "gvec": np.ascontiguousarray(gvec),
        "g_sub": np.ascontiguousarray(np.asarray(inp["g_subln"], f32)[0]),
        "ident": np.eye(128, dtype=np.float32).astype(bf16_np),
    }
    for n in ("w_out", "w_cq", "w_ck", "w_cv", "w_co", "w_gate", "w_up", "w_down"):
        sh[n] = np.ascontiguousarray(np.asarray(inp[n], f32)[0])
    return sh


def _group_inputs(gi, xseq, memseq, own0, T):
    S = xseq.shape[0]
    perm = _perm(S, own0, T)
    cT, sT = _rope_tables(S, perm)
    dc, ds = _dft_tables(S, perm, own0, T)
    return {
        "x%d" % gi: np.ascontiguousarray(xseq[perm]),
        "mem%d" % gi: np.ascontiguousarray(memseq),
        "rc%d" % gi: cT, "rs%d" % gi: sT, "dc%d" % gi: dc, "ds%d" % gi: ds,
    }


_NC_CACHE = {}


def _get_nc(groups):
    key = tuple(groups)
    if key not in _NC_CACHE:
        _NC_CACHE[key] = Builder(list(groups)).build()
    return _NC_CACHE[key]


def kernel(**inputs):
    f32 = np.float32
    xp = np.asarray(inputs["x_prompt"], f32)
    xs = np.asarray(inputs["x_sample"], f32)
    mp = np.asarray(inputs["mem_prompt"], f32)
    ms = np.asarray(inputs["mem_sample"], f32)
    B, S0, _ = xp.shape
    B1, S1, _ = xs.shape
    n = 8
    T0 = S0 * B // n
    T1 = S1 * B1 // n
    cp = n // B
    cs = n // B1
    groups = ((S0, T0), (S1, T1))
    nc = _get_nc(groups)
    sh = _shared_inputs(inputs)
    in_maps = []
    for c in range(n):
        m = dict(sh)
        m.update(_group_inputs(0, xp[c // cp], mp[c // cp], (c % cp) * T0, T0))
        m.update(_group_inputs(1, xs[c // cs], ms[c // cs], (c % cs) * T1, T1))
        in_maps.append(m)
    res = run_bass_kernel_spmd(nc, in_maps, core_ids=list(range(n)))
    yp = np.empty((B, S0, D), f32)
    ys = np.empty((B1, S1, D), f32)
    for c in range(n):
        r = res.results[c]
        yp[c // cp, (c % cp) * T0:(c % cp + 1) * T0] = r["y0"]
        ys[c // cs, (c % cs) * T1:(c % cs + 1) * T1] = r["y1"]
    return yp, ys
```

```python
import numpy as np
import ml_dtypes
import concourse.bass as bass
import concourse.mybir as mybir
from concourse.bass_utils import run_bass_kernel_spmd
from concourse.alu_op_type import AluOpType as ALU

F32 = mybir.dt.float32
BF16 = mybir.dt.bfloat16
AF = mybir.ActivationFunctionType
bf16_np = ml_dtypes.bfloat16

D = 2048
DFF = 5632
NH = 8
NMEM = 256
ENGS = ("pe", "act", "dve", "pool", "sp")


class Instr:
    __slots__ = ("eng", "fn", "raw", "other", "dma_sem", "dma_val", "milestone", "tick", "extra_waits")

    def __init__(self, eng, fn):
        self.eng = eng
        self.fn = fn
        self.raw = []
        self.other = []
        self.dma_sem = None
        self.dma_val = 0
        self.milestone = False
        self.tick = 0
        self.extra_waits = []


class Prog:
    def __init__(self, nc):
        self.nc = nc
        self.streams = {e: [] for e in ENGS}
        self.last_writer = {}
        self.readers = {}
        self.dma_count = {}
        self.pending = {e: [] for e in ENGS}

    def op(self, eng, fn, reads=(), writes=(), dma_sem=None):
        ins = Instr(eng, fn)
        if dma_sem is not None:
            c = self.dma_count.get(dma_sem, 0) + 16
            self.dma_count[dma_sem] = c
            ins.dma_sem = dma_sem
            ins.dma_val = c
        raw = {}
        oth = {}
        for k in reads:
            w = self.last_writer.get(k)
            if w is not None:
                raw[id(w)] = w
        for k in writes:
            w = self.last_writer.get(k)
            if w is not None:
                oth[id(w)] = w
            rd = self.readers.get(k)
            if rd:
                for r in rd.values():
                    oth[id(r)] = r
        ins.raw = list(raw.values())
        ins.other = [d for i, d in oth.items() if i not in raw]
        if self.pending[eng]:
            ins.extra_waits = self.pending[eng]
            self.pending[eng] = []
        for k in writes:
            self.last_writer[k] = ins
            self.readers[k] = {}
        rk = (eng, dma_sem)
        for k in reads:
            self.readers.setdefault(k, {})[rk] = ins
        self.streams[eng].append(ins)
        return ins

    def barrier(self):
        lasts = []
        for e in ENGS:
            if e == "sp":
                continue
            if self.streams[e]:
                l = self.streams[e][-1]
                l.milestone = True
                lasts.append(l)
        dm = list(self.dma_count.items())
        for e in ENGS:
            self.pending[e] = self.pending[e] + [("ins", l) for l in lasts if l.eng != e] + [("dma", k, v) for k, v in dm]
        self.last_writer = {}
        self.readers = {}

    def emit(self, sems_eng, sems_dma):
        nc = self.nc
        for e in ENGS:
            for ins in self.streams[e]:
                for d in ins.raw:
                    if d.dma_sem is None and not (d.eng == e and e == "pe"):
                        d.milestone = True
                for d in ins.other:
                    if d.dma_sem is None and d.eng != e:
                        d.milestone = True
        for e in ENGS:
            t = 0
            for ins in self.streams[e]:
                if ins.milestone:
                    t += 1
                    ins.tick = t

        def run(e, eng):
            waited = {}
            for ins in self.streams[e]:
                need = {}
                for d in ins.raw:
                    if d.dma_sem is not None:
                        sk, v = ("d", d.dma_sem), d.dma_val
                    elif d.eng == e and e == "pe":
                        continue
                    else:
                        sk, v = ("e", d.eng), d.tick
                    if need.get(sk, 0) < v:
                        need[sk] = v
                for d in ins.other:
                    if d.dma_sem is not None:
                        sk, v = ("d", d.dma_sem), d.dma_val
                    elif d.eng != e:
                        sk, v = ("e", d.eng), d.tick
                    else:
                        continue
                    if need.get(sk, 0) < v:
                        need[sk] = v
                for w in ins.extra_waits:
                    if w[0] == "ins":
                        sk, v = ("e", w[1].eng), w[1].tick
                    else:
                        sk, v = ("d", w[1]), w[2]
                    if need.get(sk, 0) < v:
                        need[sk] = v
                for sk, val in need.items():
                    if waited.get(sk, 0) >= val:
                        continue
                    waited[sk] = val
                    sem = sems_eng[sk[1]] if sk[0] == "e" else sems_dma[sk[1]]
                    eng.wait_ge(sem, val)
                r = ins.fn(eng)
                if ins.dma_sem is not None:
                    r.then_inc(sems_dma[ins.dma_sem], 16)
                elif ins.milestone:
                    r.then_inc(sems_eng[e], 1)
            if e == "sp":
                for k, v in self.dma_count.items():
                    if waited.get(("d", k), 0) < v:
                        eng.wait_ge(sems_dma[k], v)

        with nc.Block() as block:
            @block.tensor
            def _(eng):
                run("pe", eng)

            @block.scalar
            def _(eng):
                run("act", eng)

            @block.vector
            def _(eng):
                run("dve", eng)

            @block.gpsimd
            def _(eng):
                run("pool", eng)

            @block.sync
            def _(eng):
                run("sp", eng)


class Arena:
    def __init__(self, ap, nbytes):
        self.ap = ap
        self.cap = nbytes
        self.off = 0

    def reset(self, base=0):
        self.off = base

    def alloc(self, shape, dt):
        n = 1
        for s in shape[1:]:
            n *= s
        nb = n * (4 if dt == F32 else 2)
        a = self.ap[:, self.off // 2:(self.off + nb) // 2]
        self.off += (nb + 63) // 64 * 64
        assert self.off <= self.cap, ("SBUF arena overflow", self.off, self.cap)
        if dt == F32:
            a = a.bitcast(F32)
        if len(shape) == 3:
            a = a.rearrange("p (a b) -> p a b", a=shape[1])
        return a


DMA_SEMS = (["w%d" % i for i in range(6)] + ["x0", "x1", "rt0", "rt1", "st0", "st1", "st2", "st3",
            "c0", "c1", "c2", "hd0", "hd1", "z0", "df0", "df1", "df2", "misc", "out0", "out1", "cv0", "cv1", "cs0", "cs1"])


class Builder:
    def __init__(self, groups, stop_after=None, debug=()):
        self.groups = groups
        self.stop_after = stop_after
        self.debug = debug
        self.nc = bass.Bass("TRN2", target_bir_lowering=False)
        self.evc = 0

    def dma(self, out, in_, sem, reads=(), writes=()):
        return self.P.op("sp", lambda e: e.dma_start(out=out, in_=in_), reads, writes, dma_sem=sem)

    @staticmethod
    def seal(instrs):
        v = max(i.dma_val for i in instrs)
        for i in instrs:
            i.dma_val = v

    def mm(self, out, lhsT, rhs, start, stop, reads, writes, skip=False):
        if skip:
            return self.P.op("pe", lambda e: e.matmul(out, lhsT=lhsT, rhs=rhs, start=start, stop=stop, skip_group_check=True), reads, writes)
        return self.P.op("pe", lambda e: e.matmul(out, lhsT=lhsT, rhs=rhs, start=start, stop=stop), reads, writes)

    def tr(self, out, in_, reads, writes):
        ident = self.ident
        return self.P.op("pe", lambda e: e.transpose(out, in_, ident), list(reads) + ["ident"], writes)

    def act(self, out, in_, func, reads, writes, scale=None, accum=None):
        kw = {}
        if scale is not None:
            kw["scale"] = scale
        if accum is not None:
            kw["accum_out"] = accum
        return self.P.op("act", lambda e: e.activation(out=out, in_=in_, func=func, **kw), reads, writes)

    def tcopy(self, eng, out, in_, reads, writes):
        if eng == "act":
            return self.P.op("act", lambda e: e.activation(out=out, in_=in_, func=AF.Copy), reads, writes)
        return self.P.op(eng, lambda e: e.tensor_copy(out=out, in_=in_), reads, writes)

    def evac(self, out, in_, reads, writes, act_share=1, of=2):
        self.evc += 1
        eng = "act" if (self.evc % of) < act_share else "dve"
        return self.tcopy(eng, out, in_, reads, writes)

    def tt(self, eng, out, in0, in1, op, reads, writes):
        return self.P.op(eng, lambda e: e.tensor_tensor(out=out, in0=in0, in1=in1, op=op), reads, writes)

    def ts(self, out, in0, s1, s2, op0, op1, reads, writes, eng="dve"):
        if op1 is None:
            return self.P.op(eng, lambda e: e.tensor_scalar(out=out, in0=in0, scalar1=s1, scalar2=None, op0=op0), reads, writes)
        return self.P.op(eng, lambda e: e.tensor_scalar(out=out, in0=in0, scalar1=s1, scalar2=s2, op0=op0, op1=op1), reads, writes)

    def stt(self, out, in0, scalar, in1, op0, op1, reads, writes):
        return self.P.op("dve", lambda e: e.scalar_tensor_tensor(out=out, in0=in0, scalar=scalar, in1=in1, op0=op0, op1=op1), reads, writes)

    def rstd_from_ss(self, ss, mse, rstd, inv_n, eps, kss, kmse, krstd):
        self.ts(mse, ss, inv_n, eps, ALU.mult, ALU.add, [kss], [kmse])
        nh = self.negh
        self.P.op("pool", lambda e: e.tensor_tensor(out=rstd, in0=mse, in1=nh, op=ALU.pow), [kmse, "negh"], [krstd])

    def build(self):
        nc = self.nc
        dt = nc.dram_tensor
        G = self.groups
        self.din = {}

        def inp(name, shape, dtype=F32):
            self.din[name] = dt(name, list(shape), dtype, kind="ExternalInput").ap()
            return self.din[name]

        def scr(name, shape, dtype=BF16):
            return dt(name, list(shape), dtype, kind="Internal").ap()

        self.x = [inp("x%d" % g, [S, D]) for g, (S, T) in enumerate(G)]
        self.mem = [inp("mem%d" % g, [NMEM, D]) for g in range(len(G))]
        self.rc = [inp("rc%d" % g, [128, S]) for g, (S, T) in enumerate(G)]
        self.rs = [inp("rs%d" % g, [128, S]) for g, (S, T) in enumerate(G)]
        self.dc = [inp("dc%d" % g, [T // 128, 128, S // 128, 128], BF16) for g, (S, T) in enumerate(G)]
        self.ds = [inp("ds%d" % g, [T // 128, 128, S // 128, 128], BF16) for g, (S, T) in enumerate(G)]
        self.y = [dt("y%d" % g, [T, D], F32, kind="ExternalOutput").ap() for g, (S, T) in enumerate(G)]
        w_in = inp("w_in", [D, 4096])
        w_f = inp("w_f", [8, 128, 128])
        ccsc = inp("ccsc", [128, 256])
        lam_in = inp("lam", [1, 256])
        gvec = inp("gvec", [5, D])
        g_sub = inp("g_sub", [128])
        identd = inp("ident", [128, 128], BF16)
        pmatd = inp("pmat", [128, 128], BF16)
        wsrc = {n: inp(n, [D, D]) for n in ("w_out", "w_cq", "w_ck", "w_cv", "w_co")}
        wsrc["w_gate"] = inp("w_gate", [D, DFF])
        wsrc["w_up"] = inp("w_up", [D, DFF])
        wsrc["w_down"] = inp("w_down", [DFF, D])
        WA = scr("WA", [8, 128, 16, 512])
        WB = {n: scr("S_" + n, [4, 128, 16, 512]) for n in ("w_out", "w_cq", "w_ck", "w_cv", "w_co")}
        WB["w_gate"] = scr("S_w_gate", [11, 128, 16, 512])
        WB["w_up"] = scr("S_w_up", [11, 128, 16, 512])
        WB["w_down"] = scr("S_w_down", [4, 128, 44, 512])
        lam_scr = scr("lam_scr", [1, 1], F32)
        KT = [scr("KT%d" % g, [NH, 128, S]) for g, (S, T) in enumerate(G)]
        QT = [scr("QT%d" % g, [NH, 128, T]) for g, (S, T) in enumerate(G)]
        V = [scr("V%d" % g, [S, 1024]) for g, (S, T) in enumerate(G)]
        Z = [scr("Z%d" % g, [S, 2048]) for g, (S, T) in enumerate(G)]
        MIXT = [scr("MIXT%d" % g, [D, T]) for g, (S, T) in enumerate(G)]

        ARENA_BYTES = 199 * 1024
        with (
            nc.sbuf_tensor("arena", [128, ARENA_BYTES // 2], BF16) as arena_t,
            nc.sbuf_tensor("consts", [128, 3072], BF16) as consts_t,
            nc.psum_tensor("psum", [128, 8, 512], F32) as psum,
        ):
            import contextlib
            with contextlib.ExitStack() as es:
                sems_eng = {e: es.enter_context(nc.semaphore("se_" + e)) for e in ENGS}
                sems_dma = {k: es.enter_context(nc.semaphore("sd_" + k)) for k in DMA_SEMS}
                self.P = P = Prog(nc)
                A = Arena(arena_t, ARENA_BYTES)
                C = Arena(consts_t, 6144)
                self.ident = C.alloc([128, 128], BF16)
                self.negh = C.alloc([128, 1], F32)
                lam_b = C.alloc([128, 1], F32)
                gsb = C.alloc([128, 128], F32)
                ones_c = C.alloc([128, 1], BF16)
                self.pmat = C.alloc([128, 128], BF16)
                self.AB = C.alloc([128, 8, 256], BF16)
                grp0 = [self.dma(self.ident, identd, "misc", writes=["ident"])]
                negh = self.negh
                P.op("pool", lambda e: e.memset(negh, -0.5), writes=["negh"])
                P.op("pool", lambda e: e.memset(ones_c, 1.0), writes=["ones_c"])

                def ps_bank(b):
                    return psum[:, b, :]

                def ps_bf(b):
                    return psum[:, b, :].bitcast(BF16).rearrange("p (a b) -> p a b", a=8)

                CONV_W = 3072
                A.reset()
                cw32 = [A.alloc([128, CONV_W], F32) for _ in range(2)]
                cw16 = [A.alloc([128, CONV_W], BF16) for _ in range(2)]
                self.conv_bytes = A.off
                lam_sb = A.alloc([128, 256], F32)
                gs_raw = A.alloc([128, 128], F32)
                cc32 = A.alloc([128, 256], F32)
                wf32 = A.alloc([128, 8, 128], F32)
                grp0.append(self.dma(self.pmat, pmatd, "misc", writes=["pmat"]))
                grp0.append(self.dma(lam_sb[0:1, :], lam_in, "misc", writes=["lam_sb"]))
                grp0.append(self.dma(gs_raw, g_sub.partition_broadcast(128), "misc", writes=["gs_raw"]))
                grp0.append(self.dma(cc32, ccsc, "misc", writes=["cc32"]))
                grp0.append(self.dma(wf32, w_f.rearrange("g c d -> c g d"), "misc", writes=["wf32"]))
                self.seal(grp0)
                tasks = []
                for kc in range(16):
                    rows = slice(kc * 128, (kc + 1) * 128)
                    tasks.append((w_in[rows, 0:2048], 2048,
                                  [(WA[0:2, :, kc, :].rearrange("b p c -> p b c"), 0, 1024),
                                   (WA[6:8, :, kc, :].rearrange("b p c -> p b c"), 1024, 1024)]))
                    tasks.append((w_in[rows, 2048:4096], 2048,
                                  [(WA[2:4, :, kc, :].rearrange("b p c -> p b c"), 0, 1024),
                                   (WA[4:6, :, kc, :].rearrange("b p c -> p b c"), 1024, 1024)]))
                n_fore = len(tasks)
                for kc in range(16):
                    rows = slice(kc * 128, (kc + 1) * 128)
                    for n in ("w_out", "w_cq", "w_ck", "w_cv", "w_co"):
                        tasks.append((wsrc[n][rows, :], 2048, [(WB[n][:, :, kc, :].rearrange("b p c -> p b c"), 0, 2048)]))
                    for n in ("w_gate", "w_up"):
                        tasks.append((wsrc[n][rows, 0:3072], 3072, [(WB[n][0:6, :, kc, :].rearrange("b p c -> p b c"), 0, 3072)]))
                        tasks.append((wsrc[n][rows, 3072:DFF], 2560, [(WB[n][6:11, :, kc, :].rearrange("b p c -> p b c"), 0, 2560)]))
                for kc in range(44):
                    rows = slice(kc * 128, (kc + 1) * 128)
                    tasks.append((wsrc["w_down"][rows, :], 2048, [(WB["w_down"][:, :, kc, :].rearrange("b p c -> p b c"), 0, 2048)]))
                cstate = {"i": 0, "loaded": 0, "stored": 0}

                def conv_store(i):
                    for (dst, c0, nsub) in tasks[i][2]:
                        self.dma(dst, cw16[i % 2][:, c0:c0 + nsub].rearrange("p (b c) -> p b c", c=512), "cs%d" % (i % 2),
                                 reads=[("cw16", i % 2)])

                def conv_step(n, engines):
                    for _ in range(n):
                        i = cstate["i"]
                        if i >= len(tasks):
                            break
                        while cstate["loaded"] <= min(i + 1, len(tasks) - 1):
                            l = cstate["loaded"]
                            self.dma(cw32[l % 2][:, 0:tasks[l][1]], tasks[l][0], "cv%d" % (l % 2), writes=[("cw32", l % 2)])
                            cstate["loaded"] += 1
                        ncols = tasks[i][1]
                        self.tcopy(engines[i % len(engines)], cw16[i % 2][:, 0:ncols], cw32[i % 2][:, 0:ncols], [("cw32", i % 2)], [("cw16", i % 2)])
                        while cstate["stored"] < i:
                            conv_store(cstate["stored"])
                            cstate["stored"] += 1
                        cstate["i"] += 1
                    if cstate["i"] >= len(tasks):
                        while cstate["stored"] < len(tasks):
                            conv_store(cstate["stored"])
                            cstate["stored"] += 1

                def conv_flush():
                    while cstate["stored"] < cstate["i"]:
                        conv_store(cstate["stored"])
                        cstate["stored"] += 1

                self.conv_step = conv_step
                self.conv_flush = conv_flush
                self.conv_left = lambda: len(tasks) - cstate["i"]
                conv_step(n_fore, ("dve", "act", "pool"))
                conv_flush()

                lj = A.alloc([128, 64], F32)
                lsum = A.alloc([128, 2], F32)
                lexp = A.alloc([128, 2], F32)
                lval = A.alloc([128, 1], F32)
                for i in range(2):
                    a0 = lam_sb[0:1, i * 128:i * 128 + 64]
                    a1 = lam_sb[0:1, i * 128 + 64:i * 128 + 128]
                    acc_ = lsum[0:1, i:i + 1]
                    P.op("dve", lambda e, a0=a0, a1=a1, acc_=acc_: e.scalar_tensor_tensor(
                        out=lj[0:1, :], in0=a0, scalar=1.0, in1=a1, op0=ALU.mult, op1=ALU.mult, accum_out=acc_),
                        ["lam_sb"], ["lj", ("lsum", i)])
                self.act(lexp[0:1, :], lsum[0:1, :], AF.Exp, [("lsum", 0), ("lsum", 1)], ["lexp"])
                self.stt(lval[0:1, :], lexp[0:1, 0:1], 0.2, lexp[0:1, 1:2], ALU.add, ALU.subtract, ["lexp"], ["lval"])
                self.dma(lam_scr, lval[0:1, :], "c2", reads=["lval"], writes=["lam_scr"])
                self.dma(lam_b, lam_scr[0, :].partition_broadcast(128), "c2", reads=["lam_scr"], writes=["lam_b"])
                self.ts(gsb, gs_raw, 0.8, None, ALU.mult, None, ["gs_raw"], ["gsb"])
                AB = self.AB
                for g in range(8):
                    for t in range(2):
                        b = (g * 2 + t) % 2
                        self.mm(ps_bank(b)[:, 0:128], cc32[:, t * 128:(t + 1) * 128], wf32[:, g, :], True, True, ["cc32", "wf32"], [("ps", b)])
                        self.tcopy("dve", AB[:, g, t * 128:(t + 1) * 128], ps_bank(b)[:, 0:128], [("ps", b)], [("AB", g, t)])
                P.barrier()

                stop = self.stop_after
                for gi, (S, T) in enumerate(G):
                    if stop == "W":
                        break
                    self.phase_A(A, psum, gi, S, T, gvec, WA, KT[gi], QT[gi], V[gi], Z[gi])
                    self.conv_flush()
                    P.barrier()
                    if stop == "A":
                        break
                    self.phase_B(A, psum, gi, S, T, KT[gi], QT[gi], V[gi], MIXT[gi], lam_b, gsb)
                    self.conv_step(self.conv_left(), ("pool", "act", "dve"))
                    self.conv_flush()
                    P.barrier()
                    if stop == "B":
                        break
                    self.phase_C(A, psum, gi, S, T, Z[gi], MIXT[gi])
                    P.barrier()
                    if stop == "C":
                        break
                    self.phase_D(A, psum, gi, S, T, gvec, WB, MIXT[gi], ones_c)
                    P.barrier()
                if self.debug:
                    dbg_src = {"WA": WA, "KT": KT[0], "QT": QT[0], "V": V[0], "Z": Z[0], "MIXT": MIXT[0], "Wout": WB["w_out"], "Wdown": WB["w_down"]}
                    for i, nm in enumerate(self.debug):
                        src = dbg_src[nm]
                        shp = list(src.shape)
                        dst = dt("dbg_" + nm, shp, BF16, kind="ExternalOutput").ap()
                        if len(shp) == 4:
                            for b_ in range(shp[0]):
                                self.dma(dst[b_], src[b_], "misc")
                        elif len(shp) == 3:
                            for b_ in range(shp[0]):
                                self.dma(dst[b_], src[b_], "misc")
                        else:
                            self.dma(dst, src, "misc")
                P.emit(sems_eng, sems_dma)
        return nc

    def norm_transpose(self, psum, xin, kx, gb, sm, hbf, khbf, hT, khT, tcol, tbanks):
        ss, mse, rstd, ksm = sm
        self.act(hbf, xin, AF.Square, [kx], [khbf, (ksm, "ss")], accum=ss)
        self.rstd_from_ss(ss, mse, rstd, 1.0 / D, 1e-6, (ksm, "ss"), (ksm, "mse"), (ksm, "rstd"))
        self.stt(hbf, xin, rstd, gb, ALU.mult, ALU.mult, [kx, (ksm, "rstd"), "gb"], [khbf])
        for half in range(2):
            b = tbanks[half]
            pst = psum[:, b, :].bitcast(BF16).rearrange("p (a b) -> p a b", a=8)
            for j in range(8):
                kc = half * 8 + j
                self.tr(pst[:, j, :], hbf[:, kc * 128:(kc + 1) * 128], [khbf], [("ps", b)])
            self.evac(hT[:, half * 8:(half + 1) * 8, tcol:tcol + 128], pst, [("ps", b)], [(khT, half, tcol)])

    def norm_only(self, xin, kx, gb, sm, hbf, khbf):
        ss, mse, rstd, ksm = sm
        self.act(hbf, xin, AF.Square, [kx], [khbf, (ksm, "ss")], accum=ss)
        self.rstd_from_ss(ss, mse, rstd, 1.0 / D, 1e-6, (ksm, "ss"), (ksm, "mse"), (ksm, "rstd"))
        self.stt(hbf, xin, rstd, gb, ALU.mult, ALU.mult, [kx, (ksm, "rstd"), "gb"], [khbf])

    def transpose_only(self, psum, hbf, khbf, hT, khT, tcol, tbanks):
        for half in range(2):
            b = tbanks[half]
            pst = psum[:, b, :].bitcast(BF16).rearrange("p (a b) -> p a b", a=8)
            for j in range(8):
                kc = half * 8 + j
                self.tr(pst[:, j, :], hbf[:, kc * 128:(kc + 1) * 128], [khbf], [("ps", b)])
            self.evac(hT[:, half * 8:(half + 1) * 8, tcol:tcol + 128], pst, [("ps", b)], [(khT, half, tcol)])

    def phase_A(self, A, psum, gi, S, T, gvec, WA, KT, QT, V, Z):
        P = self.P
        A.reset(self.conv_bytes if self.conv_left() > 0 else 0)
        ntile = S // 512
        nown = T // 512
        gb = A.alloc([128, D], F32)
        self.dma(gb, gvec[0, :].partition_broadcast(128), "misc", writes=["gb"])
        xin = [A.alloc([128, D], F32) for _ in range(2)]
        hbf = [A.alloc([128, D], BF16) for _ in range(4)]
        sms = [(A.alloc([128, 1], F32), A.alloc([128, 1], F32), A.alloc([128, 1], F32), ("smA", i)) for i in range(4)]
        hT = [A.alloc([128, 16, 512], BF16) for _ in range(2)]
        R = 6
        wblk = [A.alloc([128, 8, 512], BF16) for _ in range(R)]
        rct = [A.alloc([128, 512], F32) for _ in range(2)]
        rst = [A.alloc([128, 512], F32) for _ in range(2)]
        stg = [A.alloc([128, 4, 512], BF16) for _ in range(2)]
        zstg = [A.alloc([128, 4, 1024], BF16) for _ in range(2)]
        fsb = [A.alloc([128, 512], BF16) for _ in range(2)]
        t1 = [A.alloc([128, 512], F32) for _ in range(2)]
        t2 = [A.alloc([128, 512], F32) for _ in range(2)]
        AB = self.AB
        pmat = self.pmat
        xcnt = [0]

        def prologue(ti):
            hs = ti % 2
            self.seal([self.dma(rct[hs], self.rc[gi][:, ti * 512:(ti + 1) * 512], "rt%d" % hs, writes=[("rct", hs)]),
                       self.dma(rst[hs], self.rs[gi][:, ti * 512:(ti + 1) * 512], "rt%d" % hs, writes=[("rst", hs)])])
            for st in range(4):
                xs = xcnt[0] % 2
                xcnt[0] += 1
                r0 = ti * 512 + st * 128
                self.dma(xin[xs], self.x[gi][r0:r0 + 128, :], "x%d" % xs, writes=[("xin", xs)])
                self.norm_only(xin[xs], ("xin", xs), gb, sms[st], hbf[st], ("hbf", st))

        def prologue_tr(ti):
            hs = ti % 2
            for st in range(4):
                self.transpose_only(psum, hbf[st], ("hbf", st), hT[hs], ("hT", hs), st * 128, (6, 7))

        jobs = []
        for ti in range(ntile):
            for b in range(2):
                jobs.append((ti, "four", b, ("Z", b)))
            for b in range(2):
                jobs.append((ti, "tok", 4 + b, ("V", b)))
            for b in range(2):
                jobs.append((ti, "rope", 2 + b, ("K", b)))
            if ti < nown:
                for b in range(2):
                    jobs.append((ti, "rope", 6 + b, ("Q", b)))
        akinds = getattr(self, "akinds", None)
        if akinds is not None:
            jobs = [j for j in jobs if j[1] in akinds]
        issued = [0]
        nfills = 2 * len(jobs)

        def ensure(upto):
            while issued[0] <= min(upto, nfills - 1):
                f = issued[0]
                blk = jobs[f // 2][2]
                kh = f % 2
                s = f % R
                self.dma(wblk[s], WA[blk, :, kh * 8:(kh + 1) * 8, :], "w%d" % s, writes=[("wblk", s)])
                issued[0] += 1

        psc = [0]

        def bank():
            b = psc[0] % 6
            psc[0] += 1
            return b

        pend = []
        ucnt = [0]

        def flush():
            while pend:
                pend.pop(0)()

        prologue(0)
        prologue_tr(0)
        cur_t = -1
        for ji, (ti, kind, blk, dest) in enumerate(jobs):
            if ti != cur_t:
                cur_t = ti
                flush()
                if ti + 1 < ntile:
                    prologue(ti + 1)
            if ti + 1 < ntile and (ji + 1 == len(jobs) or jobs[ji + 1][0] != ti):
                prologue_tr(ti + 1)
            hs = ti % 2
            f0 = 2 * ji
            ensure(f0 + R - 1)
            if self.conv_left() > 0:
                self.conv_step(1, ("pool", "act", "pool", "dve"))
            ss_ = ji % 2
            r0 = ti * 512
            if kind == "tok":
                banks = [bank() for _ in range(4)]
                for kh in range(2):
                    s = (f0 + kh) % R
                    for st in range(4):
                        for k8 in range(8):
                            kc = kh * 8 + k8
                            self.mm(psum[:, banks[st], :], hT[hs][:, kc, st * 128:(st + 1) * 128], wblk[s][:, k8, :],
                                    kc == 0, kc == 15, [(("hT", hs), kc // 8, st * 128), ("wblk", s)], [("ps", banks[st])])
                flush()
                for st in range(4):
                    self.evac(stg[ss_][:, st, :], psum[:, banks[st], :], [("ps", banks[st])], [("stg", ss_, st)], act_share=2, of=3)
                dst = V[r0:r0 + 512, dest[1] * 512:(dest[1] + 1) * 512]
                self.dma(dst.rearrange("(s p) c -> p s c", p=128), stg[ss_], "st%d" % ss_, reads=[("stg", ss_, st) for st in range(4)])
                continue
            for oc in range(4):
                b = bank()
                for kh in range(2):
                    s = (f0 + kh) % R
                    for k8 in range(8):
                        kc = kh * 8 + k8
                        self.mm(psum[:, b, :], wblk[s][:, k8, oc * 128:(oc + 1) * 128], hT[hs][:, kc, :],
                                kc == 0, kc == 15, [(("hT", hs), kc // 8, c) for c in (0, 128, 256, 384)] + [("wblk", s)], [("ps", b)])
                flush()
                u = ucnt[0] % 2
                ucnt[0] += 1
                self.tcopy("act", fsb[u], psum[:, b, :], [("ps", b)], [("fsb", u), ("ps", b)])
                if kind == "rope":
                    self.tt("dve", t1[u], psum[:, b, :], rct[hs], ALU.mult, [("ps", b), ("rct", hs)], [("t1", u)])

                    def stage2(u=u, oc=oc, hs=hs, ss_=ss_, dest=dest, ti=ti):
                        b2 = bank()
                        self.mm(psum[:, b2, :], pmat, fsb[u], True, True, ["pmat", ("fsb", u)], [("ps", b2)])
                        self.tt("dve", t2[u], psum[:, b2, :], rst[hs], ALU.mult, [("ps", b2), ("rst", hs)], [("t2", u)])
                        self.tt("pool", stg[ss_][:, oc, :], t1[u], t2[u], ALU.add, [("t1", u), ("t2", u)], [("stg", ss_, oc)])
                        if oc == 3:
                            h0 = dest[1] * 4
                            dd = KT if dest[0] == "K" else QT
                            dst = dd[h0:h0 + 4, :, ti * 512:(ti + 1) * 512]
                            self.dma(dst.rearrange("h p c -> p h c"), stg[ss_], "st%d" % ss_, reads=[("stg", ss_, o_) for o_ in range(4)])
                    pend.append(stage2)
                else:
                    def stage2(u=u, oc=oc, ss_=ss_, dest=dest, r0=r0, blk=blk):
                        g = blk * 4 + oc
                        for sp in range(2):
                            b2 = bank()
                            for q_ in range(2):
                                st = sp * 2 + q_
                                self.mm(psum[:, b2, q_ * 256:(q_ + 1) * 256], fsb[u][:, st * 128:(st + 1) * 128], AB[:, g, :], True, True,
                                        [("fsb", u), ("AB", g)], [("ps", b2)])
                            self.evac(zstg[ss_][:, sp * 2:sp * 2 + 2, oc * 256:(oc + 1) * 256], psum[:, b2, :].rearrange("p (a b) -> p a b", a=2),
                                      [("ps", b2)], [("zstg", ss_, oc, sp)], act_share=1, of=3)
                        if oc == 3:
                            dst = Z[r0:r0 + 512, dest[1] * 1024:(dest[1] + 1) * 1024]
                            self.dma(dst.rearrange("(s p) c -> p s c", p=128), zstg[ss_], "c%d" % ss_,
                                     reads=[("zstg", ss_, o_, p_) for o_ in range(4) for p_ in range(2)])
                    pend.append(stage2)
        flush()

    def phase_B(self, A, psum, gi, S, T, KT, QT, V, MIXT, lam_b, gsb):
        P = self.P
        A.reset(self.conv_bytes if self.conv_left() > 0 else 0)
        nkt = S // 128
        nqb = T // 512
        KTs = [A.alloc([128, S], BF16) for _ in range(2)]
        QTz = [[A.alloc([128, T], BF16) for _ in range(2)] for _ in range(2)]
        VA = [A.alloc([128, nkt, 130], BF16) for _ in range(2)]
        pt = [A.alloc([128, 1024], BF16) for _ in range(3)]
        t32 = [A.alloc([128, 128], F32) for _ in range(2)]
        o32 = [A.alloc([128, 128], F32) for _ in range(2)]
        jk = A.alloc([128, 128], F32)
        onb = [A.alloc([128, 4, 128], BF16) for _ in range(2)]
        oTs = [A.alloc([128, 512], BF16) for _ in range(2)]
        sm = [[A.alloc([128, 1], F32) for _ in range(6)] for _ in range(2)]
        for s in range(2):
            va = VA[s]
            P.op("pool", lambda e, va=va: e.memset(va[:, :, 128:129], 1.0), writes=[("VAone", s)])
            z0 = QTz[0][s][64:128, :]
            z1 = QTz[1][s][0:64, :]
            P.op("pool", lambda e, z0=z0: e.memset(z0, 0.0), writes=[("QTzero", s, 0)])
            P.op("dve", lambda e, z1=z1: e.memset(z1, 0.0), writes=[("QTzero", s, 1)])

        def load_head(h):
            s = h % 2
            grp = [self.dma(KTs[s], KT[h], "hd%d" % s, writes=[("KTs", s)]),
                   self.dma(QTz[0][s][0:64, :], QT[h, 0:64, :], "hd%d" % s, writes=[("QTs", s, 0)]),
                   self.dma(QTz[1][s][64:128, :], QT[h, 64:128, :], "hd%d" % s, writes=[("QTs", s, 1)])]
            step = 16
            for k0 in range(0, nkt, step):
                k1 = min(nkt, k0 + step)
                grp.append(self.dma(VA[s][:, k0:k1, 0:128], V[k0 * 128:k1 * 128, h * 128:(h + 1) * 128].rearrange("(k p) c -> p k c", p=128),
                                    "hd%d" % s, writes=[("VA", s, k0)]))
            self.seal(grp)

        def acc_ap(c, j):
            a = c * 4 + j
            return psum[:, 4 + a // 3, (a % 3) * 129:(a % 3) * 129 + 129], 4 + a // 3, (a % 3 == 0)

        load_head(0)
        fin = [0]
        for h in range(NH):
            s = h % 2
            if h + 1 < NH:
                load_head(h + 1)
            vkeys = [("VA", s, k0) for k0 in range(0, nkt, 16)] + [("VAone", s)]
            for qb in range(nqb):
                qsl = slice(qb * 512, (qb + 1) * 512)
                if self.conv_left() > 0:
                    it_left = (NH - h) * nqb - qb
                    self.conv_step(-(-self.conv_left() // max(1, it_left - 2)) if it_left > 2 else self.conv_left(), ("dve",))

                def ST(kt):
                    sl = kt % 2
                    for c in range(2):
                        self.mm(psum[:, sl * 2 + c, :], KTs[s][:, kt * 128:(kt + 1) * 128], QTz[c][s][:, qsl],
                                True, True, [("KTs", s), ("QTs", s, c), ("QTzero", s, c)], [("pss", sl, c)])

                ST(0)
                for kt in range(nkt):
                    if kt + 1 < nkt:
                        ST(kt + 1)
                    sl = kt % 2
                    p3 = kt % 3
                    for c in range(2):
                        self.act(pt[p3][:, c * 512:(c + 1) * 512], psum[:, sl * 2 + c, :], AF.Exp, [("pss", sl, c)], [("pt", p3, c)], scale=0.125)
                    for c in range(2):
                        for j in range(4):
                            ap_, bank, first = acc_ap(c, j)
                            self.mm(ap_, pt[p3][:, c * 512 + j * 128:c * 512 + (j + 1) * 128], VA[s][:, kt, 0:129],
                                    (kt == 0 and first), kt == nkt - 1, [("pt", p3, c), ("VA", s, (kt // 16) * 16), ("VAone", s)], [("acc", bank)], skip=True)
                fs = fin[0] % 2
                fin[0] += 1
                for j in range(4):
                    a0, b0, _ = acc_ap(0, j)
                    a1, b1, _ = acc_ap(1, j)
                    u = j % 2
                    r0, r1, r1l, ss, mse, rstd = sm[u]
                    ku = ("smB", u)
                    P.op("dve", lambda e, r0=r0, a0=a0: e.reciprocal(out=r0, in_=a0[:, 128:129]), [("acc", b0)], [(ku, "r0")])
                    P.op("dve", lambda e, r1=r1, a1=a1: e.reciprocal(out=r1, in_=a1[:, 128:129]), [("acc", b1)], [(ku, "r1")])
                    self.tt("dve", r1l, r1, lam_b, ALU.mult, [(ku, "r1"), "lam_b"], [(ku, "r1l")])
                    self.ts(t32[u], a1[:, 0:128], r1l, None, ALU.mult, None, [("acc", b1), (ku, "r1l")], [("t32", u)])
                    self.stt(o32[u], a0[:, 0:128], r0, t32[u], ALU.mult, ALU.subtract, [("acc", b0), (ku, "r0"), ("t32", u)], [("o32", u)])
                    o_ = o32[u]
                    P.op("dve", lambda e, o_=o_, ss=ss: e.scalar_tensor_tensor(out=jk, in0=o_, scalar=1.0, in1=o_,
                                                                             op0=ALU.mult, op1=ALU.mult, accum_out=ss),
                         [("o32", u)], ["jk", (ku, "ss")])
                    self.rstd_from_ss(ss, mse, rstd, 1.0 / 128, 1e-5, (ku, "ss"), (ku, "mse"), (ku, "rstd"))
                    self.stt(onb[fs][:, j, :], o32[u], rstd, gsb, ALU.mult, ALU.mult, [("o32", u), (ku, "rstd"), "gsb"], [("onb", fs, j)])
                    pst = psum[:, 7, :].bitcast(BF16).rearrange("p (a b) -> p a b", a=8)
                    self.tr(pst[:, j, :], onb[fs][:, j, :], [("onb", fs, j)], [("pstB", 0)])
                pst = psum[:, 7, :].bitcast(BF16)
                self.tcopy("dve", oTs[fs], pst[:, 0:512], [("pstB", 0)], [("oTs", fs)])
                self.dma(MIXT[1024 + h * 128:1024 + (h + 1) * 128, qsl], oTs[fs], "st%d" % fs, reads=[("oTs", fs)])

    def phase_C(self, A, psum, gi, S, T, Z, MIXT):
        P = self.P
        A.reset()
        nch = S // 128
        NCH = min(32, nch)
        nfill = nch // NCH
        nkt = T // 128
        Zs = A.alloc([128, nch, 1024], BF16)
        R = 3
        dcs = [A.alloc([128, NCH, 128], BF16) for _ in range(R)]
        dss = [A.alloc([128, NCH, 128], BF16) for _ in range(R)]
        ysb = [A.alloc([128, 512], BF16) for _ in range(2)]
        yT = [A.alloc([128, 4, 128], BF16) for _ in range(2)]
        for half in range(2):
            step = 16
            zk = []
            for k0 in range(0, nch, step):
                k1 = min(nch, k0 + step)
                zk.append(self.dma(Zs[:, k0:k1, :], Z[k0 * 128:k1 * 128, half * 1024:(half + 1) * 1024].rearrange("(k p) c -> p k c", p=128),
                                   "z0", writes=[("Zs", k0)]))
            self.seal(zk)
            fl = [(kt, f) for kt in range(nkt) for f in range(nfill)]
            issued = [0]

            def ensure(upto):
                while issued[0] <= min(upto, len(fl) - 1):
                    i = issued[0]
                    kt, f = fl[i]
                    s = i % R
                    self.seal([self.dma(dcs[s], self.dc[gi][kt, :, f * NCH:(f + 1) * NCH, :], "df%d" % s, writes=[("dcs", s)]),
                               self.dma(dss[s], self.ds[gi][kt, :, f * NCH:(f + 1) * NCH, :], "df%d" % s, writes=[("dss", s)])])
                    issued[0] += 1

            for i, (kt, f) in enumerate(fl):
                ensure(i + R - 1)
                s = i % R
                b = kt % 2
                for n in range(NCH):
                    ncg = f * NCH + n
                    zrow = Zs[:, ncg, :].rearrange("p (g c) -> p g c", g=4)
                    self.mm(psum[:, b, :], dcs[s][:, n, :], zrow[:, :, 0:128], (ncg == 0), False, [("dcs", s), ("Zs", (ncg // 16) * 16)], [("psC", b)])
                    self.mm(psum[:, b, :], dss[s][:, n, :], zrow[:, :, 128:256], False, (ncg == nch - 1), [("dss", s), ("Zs", (ncg // 16) * 16)], [("psC", b)])
                if f == nfill - 1:
                    self.evac(ysb[b], psum[:, b, :], [("psC", b)], [("ysb", b)])
                    pst = psum[:, 6 + b, :].bitcast(BF16).rearrange("p (a b) -> p a b", a=8)
                    for g in range(4):
                        self.tr(pst[:, g, :], ysb[b][:, g * 128:(g + 1) * 128], [("ysb", b)], [("pstC", b)])
                    self.evac(yT[b], pst[:, 0:4, :], [("pstC", b)], [("yT", b)])
                    self.dma(MIXT[half * 512:(half + 1) * 512, kt * 128:(kt + 1) * 128].rearrange("(g p) c -> p g c", p=128), yT[b],
                             "st%d" % b, reads=[("yT", b)])

    def phase_D(self, A, psum, gi, S, T, gvec, WB, MIXT, ones_c):
        P = self.P
        A.reset()
        ntile = T // 512
        xres = A.alloc([128, 4, D], F32)
        actT = A.alloc([128, 16, 512], BF16)
        hT = A.alloc([128, 16, 512], BF16)
        hid = A.alloc([128, 24, 512], BF16)
        qT = [A.alloc([128, 4, 512], BF16) for _ in range(2)]
        R = 6
        wblk = [A.alloc([128, 8, 512], BF16) for _ in range(R)]
        kTm = A.alloc([128, 16, 256], BF16)
        vm = A.alloc([128, 2, D], BF16)
        gb = A.alloc([128, D], F32)
        hbf = [A.alloc([128, D], BF16) for _ in range(2)]
        pT = [A.alloc([128, 2, 512], BF16) for _ in range(2)]
        ocn = [A.alloc([128, 512], BF16) for _ in range(2)]
        sg = [A.alloc([128, 512], BF16) for _ in range(2)]
        sms = [(A.alloc([128, 1], F32), A.alloc([128, 1], F32), A.alloc([128, 1], F32), ("smD", i)) for i in range(2)]
        rl = [A.alloc([128, 1], F32) for _ in range(2)]
        memT = A.alloc([128, 16, 256], BF16)

        def blockfills(name, blk, kc0=0, kc1=16):
            out = []
            k = kc0
            while k < kc1:
                n = min(8, kc1 - k)
                out.append((name, blk, k, n))
                k += n
            return out

        fills = []
        for blk in range(4):
            fills += blockfills("w_ck", blk)
        for blk in range(4):
            fills += blockfills("w_cv", blk)
        halves = [(0, 6), (6, 11)]
        dstop = getattr(self, "dstop", 99)
        for ti in range(ntile):
            for blk in range(4):
                fills += blockfills("w_out", blk)
            if dstop <= 2:
                continue
            for blk in range(4):
                fills += blockfills("w_cq", blk)
            for blk in range(4):
                fills += blockfills("w_co", blk)
            if dstop <= 5:
                continue
            for (b0, b1) in halves:
                for blk in range(b0, b1):
                    fills += blockfills("w_gate", blk)
                    fills += blockfills("w_up", blk)
                for cb in range(4):
                    fills += blockfills("w_down", cb, b0 * 4, b1 * 4)
        issued = [0]
        used = [0]

        def ensure(upto):
            while issued[0] <= min(upto, len(fills) - 1):
                i = issued[0]
                name, blk, k0, n = fills[i]
                s = i % R
                self.dma(wblk[s][:, 0:n, :], WB[name][blk, :, k0:k0 + n, :], "w%d" % s, writes=[("wblk", s)])
                issued[0] += 1

        def next_fill(expect):
            i = used[0]
            assert fills[i][0] == expect, (fills[i], expect)
            ensure(i + R - 4)
            used[0] += 1
            return i % R, fills[i]

        self.dma(gb, gvec[2, :].partition_broadcast(128), "misc", writes=["gb"])
        for mt in range(2):
            self.dma(xres[:, mt, :], self.mem[gi][mt * 128:(mt + 1) * 128, :], "x%d" % mt, writes=[("xres", mt)])
            self.norm_transpose(psum, xres[:, mt, :], ("xres", mt), gb, sms[mt], hbf[mt], ("hbf", mt), memT, "memT", mt * 128, (6, 7))
        mk = [("memT", h, c) for h in range(2) for c in (0, 128)]
        evc = 0
        for blk in range(4):
            f = [next_fill("w_ck"), next_fill("w_ck")]
            for oc in range(4):
                b = (blk * 4 + oc) % 4
                for kh in range(2):
                    s = f[kh][0]
                    for k8 in range(8):
                        kc = kh * 8 + k8
                        self.mm(psum[:, b, 0:256], wblk[s][:, k8, oc * 128:(oc + 1) * 128], memT[:, kc, :], kc == 0, kc == 15,
                                [("wblk", s)] + [("memT", kc // 8, c) for c in (0, 128)], [("ps", b)])
                self.evac(kTm[:, blk * 4 + oc, :], psum[:, b, 0:256], [("ps", b)], [("kTm", blk * 4 + oc)])
        for blk in range(4):
            f = [next_fill("w_cv"), next_fill("w_cv")]
            for mt in range(2):
                b = (blk * 2 + mt) % 4
                for kh in range(2):
                    s = f[kh][0]
                    for k8 in range(8):
                        kc = kh * 8 + k8
                        self.mm(psum[:, b, :], memT[:, kc, mt * 128:(mt + 1) * 128], wblk[s][:, k8, :], kc == 0, kc == 15,
                                [("wblk", s), ("memT", kc // 8, mt * 128)], [("ps", b)])
                self.evac(vm[:, mt, blk * 512:(blk + 1) * 512], psum[:, b, :], [("ps", b)], [("vm", mt, blk)])

        def tokmajor_accum(name, lhs, lhs_keys, nk0, nk1):
            for cb in range(4):
                k = nk0
                while k < nk1:
                    s, (nm, blk, k0, n) = next_fill(name)
                    assert blk == cb and k0 == k
                    for st in range(4):
                        for k8 in range(n):
                            kc = k + k8
                            self.mm(psum[:, st, :], lhs[:, kc - lhs_base[0], st * 128:(st + 1) * 128], wblk[s][:, k8, :], kc == nk0, kc == nk1 - 1,
                                    [("wblk", s)] + lhs_keys(kc, st), [("ps", st)])
                    k += n
                for st in range(4):
                    xs = xres[:, st, cb * 512:(cb + 1) * 512]
                    self.tt("dve", xs, psum[:, st, :], xs, ALU.add, [("ps", st), ("xres", st)], [("xres", st)])

        lhs_base = [0]

        def norm_all(grow):
            self.dma(gb, gvec[grow, :].partition_broadcast(128), "misc", writes=["gb"])
            for st in range(4):
                self.norm_transpose(psum, xres[:, st, :], ("xres", st), gb, sms[st % 2], hbf[st % 2], ("hbf", st % 2), hT, "hT", st * 128, (6, 7))

        hTk = lambda kc: [("hT", kc // 8, c) for c in (0, 128, 256, 384)]
        cross_scale = 512 ** -0.5

        for ti in range(ntile):
            r0 = ti * 512
            self.dma(xres, self.x[gi][r0:r0 + 512, :].rearrange("(s p) c -> p s c", p=128), "x0", writes=[("xres", st) for st in range(4)])
            self.dma(actT, MIXT[:, r0:r0 + 512].rearrange("(k p) c -> p k c", p=128), "x1", writes=[("actT", kc) for kc in range(16)])
            lhs_base[0] = 0
            tokmajor_accum("w_out", actT, lambda kc, st: [("actT", kc)], 0, 16)
            def dump():
                self.dma(self.y[gi][r0:r0 + 512, :].rearrange("(s p) c -> p s c", p=128), xres, "out%d" % (ti % 2),
                         reads=[("xres", st) for st in range(4)])
            if dstop <= 1:
                dump()
                continue
            norm_all(1)
            if dstop == 2:
                if ti == 0 and gi == 0:
                    dh = self.nc.dram_tensor("dbg_hT", [128, 16, 512], BF16, kind="ExternalOutput").ap()
                    self.dma(dh, hT, "misc", reads=[("hT", h_, c_) for h_ in range(2) for c_ in (0, 128, 256, 384)])
                    dm_ = self.nc.dram_tensor("dbg_memT", [128, 16, 256], BF16, kind="ExternalOutput").ap()
                    self.dma(dm_, memT, "misc", reads=[("memT", h_, c_) for h_ in range(2) for c_ in (0, 128)])
                    dk = self.nc.dram_tensor("dbg_kTm", [128, 16, 256], BF16, kind="ExternalOutput").ap()
                    self.dma(dk, kTm, "misc", reads=[("kTm", i_) for i_ in range(16)])
                    dv = self.nc.dram_tensor("dbg_vm", [128, 2, 2048], BF16, kind="ExternalOutput").ap()
                    self.dma(dv, vm, "misc", reads=[("vm", m_, b_) for m_ in range(2) for b_ in range(4)])
                dump()
                continue
            for hh in range(4):
                f = [next_fill("w_cq"), next_fill("w_cq")]
                qs = hh % 2
                for oc in range(4):
                    b = oc
                    for kh in range(2):
                        s = f[kh][0]
                        for k8 in range(8):
                            kc = kh * 8 + k8
                            self.mm(psum[:, b, :], wblk[s][:, k8, oc * 128:(oc + 1) * 128], hT[:, kc, :], kc == 0, kc == 15,
                                    [("wblk", s)] + hTk(kc), [("ps", b)])
                    self.evac(qT[qs][:, oc, :], psum[:, b, :], [("ps", b)], [("qT", qs, oc)])
                ps_ = hh % 2
                for mt in range(2):
                    b = 4 + mt
                    for dc_ in range(4):
                        self.mm(psum[:, b, :], kTm[:, hh * 4 + dc_, mt * 128:(mt + 1) * 128], qT[qs][:, dc_, :], dc_ == 0, dc_ == 3,
                                [("kTm", hh * 4 + dc_), ("qT", qs, dc_)], [("ps", b)])
                    self.act(pT[ps_][:, mt, :], psum[:, b, :], AF.Exp, [("ps", b)], [("pT", ps_, mt)], scale=cross_scale)
                for st in range(4):
                    b = st % 2
                    for mt in range(2):
                        self.mm(psum[:, b, :], pT[ps_][:, mt, st * 128:(st + 1) * 128], vm[:, mt, hh * 512:(hh + 1) * 512], mt == 0, mt == 1,
                                [("pT", ps_, mt), ("vm", mt, hh)], [("ps", b)])
                    lb = 2 + (st % 2)
                    for mt in range(2):
                        self.mm(psum[:, lb, 0:1], pT[ps_][:, mt, st * 128:(st + 1) * 128], ones_c, mt == 0, mt == 1,
                                [("pT", ps_, mt), "ones_c"], [("ps", lb)])
                    rl_ = rl[st % 2]
                    lsrc = psum[:, lb, 0:1]
                    P.op("dve", lambda e, rl_=rl_, lsrc=lsrc: e.reciprocal(out=rl_, in_=lsrc), [("ps", lb)], [("rl", st % 2)])
                    self.ts(ocn[st % 2], psum[:, b, :], rl_, None, ALU.mult, None, [("ps", b), ("rl", st % 2)], [("ocn", st % 2)])
                    tb = 6 + (st % 2)
                    pst = psum[:, tb, :].bitcast(BF16).rearrange("p (a b) -> p a b", a=8)
                    for dc_ in range(4):
                        self.tr(pst[:, dc_, :], ocn[st % 2][:, dc_ * 128:(dc_ + 1) * 128], [("ocn", st % 2)], [("ps", tb)])
                    self.evac(actT[:, hh * 4:(hh + 1) * 4, st * 128:(st + 1) * 128], pst[:, 0:4, :], [("ps", tb)],
                              [("actT", hh * 4 + d_) for d_ in range(4)])
            tokmajor_accum("w_co", actT, lambda kc, st: [("actT", kc)], 0, 16)
            if dstop <= 5:
                dump()
                continue
            norm_all(3)
            for (b0, b1) in halves:
                for blk in range(b0, b1):
                    fg = [next_fill("w_gate"), next_fill("w_gate")]
                    fu = [next_fill("w_up"), next_fill("w_up")]
                    for oc in range(4):
                        bg = (oc % 2) * 2
                        bu = bg + 1
                        for (ff, bb) in ((fg, bg), (fu, bu)):
                            for kh in range(2):
                                s = ff[kh][0]
                                for k8 in range(8):
                                    kc = kh * 8 + k8
                                    self.mm(psum[:, 4 + bb, :], wblk[s][:, k8, oc * 128:(oc + 1) * 128], hT[:, kc, :], kc == 0, kc == 15,
                                            [("wblk", s)] + hTk(kc), [("ps", 4 + bb)])
                        sgs = oc % 2
                        self.act(sg[sgs], psum[:, 4 + bg, :], AF.Silu, [("ps", 4 + bg)], [("sg", sgs)])
                        hc = (blk - b0) * 4 + oc
                        self.tt("dve", hid[:, hc, :], psum[:, 4 + bu, :], sg[sgs], ALU.mult, [("ps", 4 + bu), ("sg", sgs)], [("hid", hc)])
                lhs_base[0] = b0 * 4
                tokmajor_accum("w_down", hid, lambda kc, st: [("hid", kc - lhs_base[0])], b0 * 4, b1 * 4)
            if dstop <= 8:
                dump()
                continue
            self.dma(gb, gvec[4, :].partition_broadcast(128), "misc", writes=["gb"])
            for st in range(4):
                ss, mse, rstd, ksm = sms[st % 2]
                hb = hbf[st % 2]
                self.act(hb, xres[:, st, :], AF.Square, [("xres", st)], [("hbf", st % 2), (ksm, "ss")], accum=ss)
                self.rstd_from_ss(ss, mse, rstd, 1.0 / D, 1e-6, (ksm, "ss"), (ksm, "mse"), (ksm, "rstd"))
                self.stt(xres[:, st, :], xres[:, st, :], rstd, gb, ALU.mult, ALU.mult, [("xres", st), (ksm, "rstd"), "gb"], [("xres", st)])
            self.dma(self.y[gi][r0:r0 + 512, :].rearrange("(s p) c -> p s c", p=128), xres, "out%d" % (ti % 2),
                     reads=[("xres", st) for st in range(4)])
        assert used[0] == len(fills), (used[0], len(fills))


_TABLE_CACHE = {}


def _perm(S, own0, T):
    own = np.arange(own0, own0 + T)
    rest = np.concatenate([np.arange(0, own0), np.arange(own0 + T, S)])
    return np.concatenate([own, rest]).astype(np.int64)


def _rope_tables(S, perm):
    inv = (1.0 / (np.float32(10000.0) ** (np.arange(0, 64, 2, dtype=np.float32) / np.float32(64)))).astype(np.float32)
    pos = perm.astype(np.float32)
    ang = (pos[:, None] * inv[None, :]).astype(np.float32)
    ang = np.concatenate([ang, ang], axis=1)
    c = np.cos(ang).astype(np.float32)
    s = np.sin(ang).astype(np.float32)
    sgn = np.concatenate([-np.ones(32, np.float32), np.ones(32, np.float32)])
    s = s * sgn[None, :]
    cT = np.ascontiguousarray(np.concatenate([c, c], axis=1).T)
    sT = np.ascontiguousarray(np.concatenate([s, s], axis=1).T)
    return cT, sT


def _dft_tables(S, perm, own0, T):
    key = (S, own0, T)
    if key in _TABLE_CACHE:
        return _TABLE_CACHE[key]
    k = np.arange(own0, own0 + T, dtype=np.int64)
    scale = 1.0 / np.sqrt(S * 128.0)
    nchunk = S // 128
    dc = np.empty((T // 128, 128, nchunk, 128), dtype=bf16_np)
    ds = np.empty((T // 128, 128, nchunk, 128), dtype=bf16_np)
    tab_c = (np.cos(2 * np.pi * np.arange(S) / S) * scale).astype(np.float32)
    tab_s = (-np.sin(2 * np.pi * np.arange(S) / S) * scale).astype(np.float32)
    n2 = perm.reshape(nchunk, 128)
    for kt in range(T // 128):
        kk = k[kt * 128:(kt + 1) * 128]
        m = (n2[:, :, None] * kk[None, None, :]) % S
        dc[kt] = tab_c[m].transpose(1, 0, 2).astype(bf16_np)
        ds[kt] = tab_s[m].transpose(1, 0, 2).astype(bf16_np)
    _TABLE_CACHE[key] = (dc, ds)
    return dc, ds


def _shared_inputs(inp):
    f32 = np.float32
    w_in = np.ascontiguousarray(np.asarray(inp["w_in"], f32)[0])
    m = np.arange(128)
    pm = np.zeros((128, 128), np.float32)
    pm[(m // 64) * 64 + ((m % 64) + 32) % 64, m] = 1.0
    c = np.arange(128)
    ang = 2 * np.pi * np.outer(c, c) / 128.0
    ccsc = np.concatenate([np.cos(ang), np.sin(ang)], axis=1).astype(f32)
    lam = np.concatenate([np.asarray(inp[n], f32)[0] for n in ("lambda_q1", "lambda_k1", "lambda_q2", "lambda_k2")])[None, :]
    gvec = np.stack([np.asarray(inp["g_mix"], f32)[0], np.asarray(inp["g_cross"], f32)[0], np.asarray(inp["g_mem"], f32)[0],
                     np.asarray(inp["g_ffn"], f32)[0], np.asarray(inp["g_final"], f32)])
    sh = {
        "w_in": w_in, "pmat": pm.astype(bf16_np),
        "w_f": np.ascontiguousarray(np.asarray(inp["w_fourier"], f32)[0]),
        "ccsc": np.ascontiguousarray(ccsc), "lam": np.ascontiguousarray(lam), "gvec": np.ascontiguousarray(gvec),
        "g_sub": np.ascontiguousarray(np.asarray(inp["g_subln"], f32)[0]),
        "ident": np.eye(128, dtype=np.float32).astype(bf16_np),
    }
    for n in ("w_out", "w_cq", "w_ck", "w_cv", "w_co", "w_gate", "w_up", "w_down"):
        sh[n] = np.ascontiguousarray(np.asarray(inp[n], f32)[0])
    return sh


def _group_inputs(gi, xseq, memseq, own0, T):
    S = xseq.shape[0]
    perm = _perm(S, own0, T)
    cT, sT = _rope_tables(S, perm)
    dc, ds = _dft_tables(S, perm, own0, T)
    return {
        "x%d" % gi: np.ascontiguousarray(xseq[perm]),
        "mem%d" % gi: np.ascontiguousarray(memseq),
        "rc%d" % gi: cT, "rs%d" % gi: sT, "dc%d" % gi: dc, "ds%d" % gi: ds,
    }


_NC_CACHE = {}


def _get_nc(groups):
    key = tuple(groups)
    if key not in _NC_CACHE:
        _NC_CACHE[key] = Builder(list(groups)).build()
    return _NC_CACHE[key]


def kernel(**inputs):
    f32 = np.float32
    xp = np.asarray(inputs["x_prompt"], f32)
    xs = np.asarray(inputs["x_sample"], f32)
    mp = np.asarray(inputs["mem_prompt"], f32)
    ms = np.asarray(inputs["mem_sample"], f32)
    B, S0, _ = xp.shape
    B1, S1, _ = xs.shape
    n = 8
    T0 = S0 * B // n
    T1 = S1 * B1 // n
    cp = n // B
    cs = n // B1
    groups = ((S0, T0), (S1, T1))
    nc = _get_nc(groups)
    sh = _shared_inputs(inputs)
    in_maps = []
    for c in range(n):
        m = dict(sh)
        m.update(_group_inputs(0, xp[c // cp], mp[c // cp], (c % cp) * T0, T0))
        m.update(_group_inputs(1, xs[c // cs], ms[c // cs], (c % cs) * T1, T1))
        in_maps.append(m)
    res = run_bass_kernel_spmd(nc, in_maps, core_ids=list(range(n)))
    yp = np.empty((B, S0, D), f32)
    ys = np.empty((B1, S1, D), f32)
    for c in range(n):
        r = res.results[c]
        yp[c // cp, (c % cp) * T0:(c % cp + 1) * T0] = r["y0"]
        ys[c // cs, (c % cs) * T1:(c % cs + 1) * T1] = r["y1"]
    return yp, ys
```

```python
import numpy as np
import ml_dtypes
import concourse.bass as bass
import concourse.mybir as mybir
from concourse.bass_utils import run_bass_kernel_spmd
from concourse.alu_op_type import AluOpType as ALU

F32 = mybir.dt.float32
BF16 = mybir.dt.bfloat16
AF = mybir.ActivationFunctionType
bf16_np = ml_dtypes.bfloat16

D = 2048
DFF = 5632
NH = 8
NMEM = 256
ENGS = ("pe", "act", "dve", "pool", "sp")


class Instr:
    __slots__ = ("eng", "fn", "raw", "other", "dma_sem", "dma_val", "milestone", "tick", "extra_waits")

    def __init__(self, eng, fn):
        self.eng = eng
        self.fn = fn
        self.raw = []
        self.other = []
        self.dma_sem = None
        self.dma_val = 0
        self.milestone = False
        self.tick = 0
        self.extra_waits = []


class Prog:
    def __init__(self, nc):
        self.nc = nc
        self.streams = {e: [] for e in ENGS}
        self.last_writer = {}
        self.readers = {}
        self.dma_count = {}
        self.pending = {e: [] for e in ENGS}

    def op(self, eng, fn, reads=(), writes=(), dma_sem=None):
        ins = Instr(eng, fn)
        if dma_sem is not None:
            c = self.dma_count.get(dma_sem, 0) + 16
            self.dma_count[dma_sem] = c
            ins.dma_sem = dma_sem
            ins.dma_val = c
        raw = {}
        oth = {}
        for k in reads:
            w = self.last_writer.get(k)
            if w is not None:
                raw[id(w)] = w
        for k in writes:
            w = self.last_writer.get(k)
            if w is not None:
                oth[id(w)] = w
            rd = self.readers.get(k)
            if rd:
                for r in rd.values():
                    oth[id(r)] = r
        ins.raw = list(raw.values())
        ins.other = [d for i, d in oth.items() if i not in raw]
        if self.pending[eng]:
            ins.extra_waits = self.pending[eng]
            self.pending[eng] = []
        for k in writes:
            self.last_writer[k] = ins
            self.readers[k] = {}
        rk = (eng, dma_sem)
        for k in reads:
            self.readers.setdefault(k, {})[rk] = ins
        self.streams[eng].append(ins)
        return ins

    def barrier(self):
        lasts = []
        for e in ENGS:
            if e == "sp":
                continue
            if self.streams[e]:
                l = self.streams[e][-1]
                l.milestone = True
                lasts.append(l)
        dm = list(self.dma_count.items())
        for e in ENGS:
            self.pending[e] = self.pending[e] + [("ins", l) for l in lasts if l.eng != e] + [("dma", k, v) for k, v in dm]
        self.last_writer = {}
        self.readers = {}

    def emit(self, sems_eng, sems_dma):
        nc = self.nc
        for e in ENGS:
            for ins in self.streams[e]:
                for d in ins.raw:
                    if d.dma_sem is None and not (d.eng == e and e == "pe"):
                        d.milestone = True
                for d in ins.other:
                    if d.dma_sem is None and d.eng != e:
                        d.milestone = True
        for e in ENGS:
            t = 0
            for ins in self.streams[e]:
                if ins.milestone:
                    t += 1
                    ins.tick = t

        def run(e, eng):
            waited = {}
            for ins in self.streams[e]:
                need = {}
                for d in ins.raw:
                    if d.dma_sem is not None:
                        sk, v = ("d", d.dma_sem), d.dma_val
                    elif d.eng == e and e == "pe":
                        continue
                    else:
                        sk, v = ("e", d.eng), d.tick
                    if need.get(sk, 0) < v:
                        need[sk] = v
                for d in ins.other:
                    if d.dma_sem is not None:
                        sk, v = ("d", d.dma_sem), d.dma_val
                    elif d.eng != e:
                        sk, v = ("e", d.eng), d.tick
                    else:
                        continue
                    if need.get(sk, 0) < v:
                        need[sk] = v
                for w in ins.extra_waits:
                    if w[0] == "ins":
                        sk, v = ("e", w[1].eng), w[1].tick
                    else:
                        sk, v = ("d", w[1]), w[2]
                    if need.get(sk, 0) < v:
                        need[sk] = v
                for sk, val in need.items():
                    if waited.get(sk, 0) >= val:
                        continue
                    waited[sk] = val
                    sem = sems_eng[sk[1]] if sk[0] == "e" else sems_dma[sk[1]]
                    eng.wait_ge(sem, val)
                r = ins.fn(eng)
                if ins.dma_sem is not None:
                    r.then_inc(sems_dma[ins.dma_sem], 16)
                elif ins.milestone:
                    r.then_inc(sems_eng[e], 1)
            if e == "sp":
                for k, v in self.dma_count.items():
                    if waited.get(("d", k), 0) < v:
                        eng.wait_ge(sems_dma[k], v)

        with nc.Block() as block:
            @block.tensor
            def _(eng):
                run("pe", eng)

            @block.scalar
            def _(eng):
                run("act", eng)

            @block.vector
            def _(eng):
                run("dve", eng)

            @block.gpsimd
            def _(eng):
                run("pool", eng)

            @block.sync
            def _(eng):
                run("sp", eng)


class Arena:
    def __init__(self, ap, nbytes):
        self.ap = ap
        self.cap = nbytes
        self.off = 0

    def reset(self, base=0):
        self.off = base

    def alloc(self, shape, dt):
        n = 1
        for s in shape[1:]:
            n *= s
        nb = n * (4 if dt == F32 else 2)
        a = self.ap[:, self.off // 2:(self.off + nb) // 2]
        self.off += (nb + 63) // 64 * 64
        assert self.off <= self.cap, ("SBUF arena overflow", self.off, self.cap)
        if dt == F32:
            a = a.bitcast(F32)
        if len(shape) == 3:
            a = a.rearrange("p (a b) -> p a b", a=shape[1])
        return a


DMA_SEMS = (["w%d" % i for i in range(6)] + ["x0", "x1", "rt0", "rt1", "st0", "st1", "st2", "st3",
            "c0", "c1", "c2", "hd0", "hd1", "z0", "df0", "df1", "df2", "misc", "out0", "out1", "cv0", "cv1", "cs0", "cs1"])


class Builder:
    def __init__(self, groups, stop_after=None, debug=()):
        self.groups = groups
        self.stop_after = stop_after
        self.debug = debug
        self.nc = bass.Bass("TRN2", target_bir_lowering=False)
        self.evc = 0

    def dma(self, out, in_, sem, reads=(), writes=()):
        return self.P.op("sp", lambda e: e.dma_start(out=out, in_=in_), reads, writes, dma_sem=sem)

    @staticmethod
    def seal(instrs):
        v = max(i.dma_val for i in instrs)
        for i in instrs:
            i.dma_val = v

    def mm(self, out, lhsT, rhs, start, stop, reads, writes, skip=False):
        if skip:
            return self.P.op("pe", lambda e: e.matmul(out, lhsT=lhsT, rhs=rhs, start=start, stop=stop, skip_group_check=True), reads, writes)
        return self.P.op("pe", lambda e: e.matmul(out, lhsT=lhsT, rhs=rhs, start=start, stop=stop), reads, writes)

    def tr(self, out, in_, reads, writes):
        ident = self.ident
        return self.P.op("pe", lambda e: e.transpose(out, in_, ident), list(reads) + ["ident"], writes)

    def act(self, out, in_, func, reads, writes, scale=None, accum=None):
        kw = {}
        if scale is not None:
            kw["scale"] = scale
        if accum is not None:
            kw["accum_out"] = accum
        return self.P.op("act", lambda e: e.activation(out=out, in_=in_, func=func, **kw), reads, writes)

    def tcopy(self, eng, out, in_, reads, writes):
        if eng == "act":
            return self.P.op("act", lambda e: e.activation(out=out, in_=in_, func=AF.Copy), reads, writes)
        return self.P.op(eng, lambda e: e.tensor_copy(out=out, in_=in_), reads, writes)

    def evac(self, out, in_, reads, writes, act_share=1, of=2):
        self.evc += 1
        eng = "act" if (self.evc % of) < act_share else "dve"
        return self.tcopy(eng, out, in_, reads, writes)

    def tt(self, eng, out, in0, in1, op, reads, writes):
        return self.P.op(eng, lambda e: e.tensor_tensor(out=out, in0=in0, in1=in1, op=op), reads, writes)

    def ts(self, out, in0, s1, s2, op0, op1, reads, writes, eng="dve"):
        if op1 is None:
            return self.P.op(eng, lambda e: e.tensor_scalar(out=out, in0=in0, scalar1=s1, scalar2=None, op0=op0), reads, writes)
        return self.P.op(eng, lambda e: e.tensor_scalar(out=out, in0=in0, scalar1=s1, scalar2=s2, op0=op0, op1=op1), reads, writes)

    def stt(self, out, in0, scalar, in1, op0, op1, reads, writes):
        return self.P.op("dve", lambda e: e.scalar_tensor_tensor(out=out, in0=in0, scalar=scalar, in1=in1, op0=op0, op1=op1), reads, writes)

    def rstd_from_ss(self, ss, mse, rstd, inv_n, eps, kss, kmse, krstd):
        self.ts(mse, ss, inv_n, eps, ALU.mult, ALU.add, [kss], [kmse])
        nh = self.negh
        self.P.op("pool", lambda e: e.tensor_tensor(out=rstd, in0=mse, in1=nh, op=ALU.pow), [kmse, "negh"], [krstd])

    def build(self):
        nc = self.nc
        dt = nc.dram_tensor
        G = self.groups
        self.din = {}

        def inp(name, shape, dtype=F32):
            self.din[name] = dt(name, list(shape), dtype, kind="ExternalInput").ap()
            return self.din[name]

        def scr(name, shape, dtype=BF16):
            return dt(name, list(shape), dtype, kind="Internal").ap()

        self.x = [inp("x%d" % g, [S, D]) for g, (S, T) in enumerate(G)]
        self.mem = [inp("mem%d" % g, [NMEM, D]) for g in range(len(G))]
        self.rc = [inp("rc%d" % g, [128, S]) for g, (S, T) in enumerate(G)]
        self.rs = [inp("rs%d" % g, [128, S]) for g, (S, T) in enumerate(G)]
        self.dc = [inp("dc%d" % g, [T // 128, 128, S // 128, 128], BF16) for g, (S, T) in enumerate(G)]
        self.ds = [inp("ds%d" % g, [T // 128, 128, S // 128, 128], BF16) for g, (S, T) in enumerate(G)]
        self.y = [dt("y%d" % g, [T, D], F32, kind="ExternalOutput").ap() for g, (S, T) in enumerate(G)]
        w_in = inp("w_in", [D, 4096])
        w_f = inp("w_f", [8, 128, 128])
        ccsc = inp("ccsc", [128, 256])
        lam_in = inp("lam", [1, 256])
        gvec = inp("gvec", [5, D])
        g_sub = inp("g_sub", [128])
        identd = inp("ident", [128, 128], BF16)
        pmatd = inp("pmat", [128, 128], BF16)
        wsrc = {n: inp(n, [D, D]) for n in ("w_out", "w_cq", "w_ck", "w_cv", "w_co")}
        wsrc["w_gate"] = inp("w_gate", [D, DFF])
        wsrc["w_up"] = inp("w_up", [D, DFF])
        wsrc["w_down"] = inp("w_down", [DFF, D])
        WA = scr("WA", [8, 128, 16, 512])
        WB = {n: scr("S_" + n, [4, 128, 16, 512]) for n in ("w_out", "w_cq", "w_ck", "w_cv", "w_co")}
        WB["w_gate"] = scr("S_w_gate", [11, 128, 16, 512])
        WB["w_up"] = scr("S_w_up", [11, 128, 16, 512])
        WB["w_down"] = scr("S_w_down", [4, 128, 44, 512])
        lam_scr = scr("lam_scr", [1, 1], F32)
        KT = [scr("KT%d" % g, [NH, 128, S]) for g, (S, T) in enumerate(G)]
        QT = [scr("QT%d" % g, [NH, 128, T]) for g, (S, T) in enumerate(G)]
        V = [scr("V%d" % g, [S, 1024]) for g, (S, T) in enumerate(G)]
        Z = [scr("Z%d" % g, [S, 2048]) for g, (S, T) in enumerate(G)]
        MIXT = [scr("MIXT%d" % g, [D, T]) for g, (S, T) in enumerate(G)]

        ARENA_BYTES = 201 * 1024
        with (
            nc.sbuf_tensor("arena", [128, ARENA_BYTES // 2], BF16) as arena_t,
            nc.sbuf_tensor("consts", [128, 3072], BF16) as consts_t,
            nc.psum_tensor("psum", [128, 8, 512], F32) as psum,
        ):
            import contextlib
            with contextlib.ExitStack() as es:
                sems_eng = {e: es.enter_context(nc.semaphore("se_" + e)) for e in ENGS}
                sems_dma = {k: es.enter_context(nc.semaphore("sd_" + k)) for k in DMA_SEMS}
                self.P = P = Prog(nc)
                A = Arena(arena_t, ARENA_BYTES)
                C = Arena(consts_t, 6144)
                self.ident = C.alloc([128, 128], BF16)
                self.negh = C.alloc([128, 1], F32)
                lam_b = C.alloc([128, 1], F32)
                gsb = C.alloc([128, 128], F32)
                ones_c = C.alloc([128, 1], BF16)
                self.pmat = C.alloc([128, 128], BF16)
                self.AB = C.alloc([128, 8, 256], BF16)
                grp0 = [self.dma(self.ident, identd, "misc", writes=["ident"])]
                negh = self.negh
                P.op("pool", lambda e: e.memset(negh, -0.5), writes=["negh"])
                P.op("pool", lambda e: e.memset(ones_c, 1.0), writes=["ones_c"])

                def ps_bank(b):
                    return psum[:, b, :]

                def ps_bf(b):
                    return psum[:, b, :].bitcast(BF16).rearrange("p (a b) -> p a b", a=8)

                CONV_W = 3072
                A.reset()
                cw32 = [A.alloc([128, CONV_W], F32) for _ in range(2)]
                cw16 = [A.alloc([128, CONV_W], BF16) for _ in range(2)]
                self.conv_bytes = A.off
                lam_sb = A.alloc([128, 256], F32)
                gs_raw = A.alloc([128, 128], F32)
                cc32 = A.alloc([128, 256], F32)
                wf32 = A.alloc([128, 8, 128], F32)
                grp0.append(self.dma(self.pmat, pmatd, "misc", writes=["pmat"]))
                grp0.append(self.dma(lam_sb[0:1, :], lam_in, "misc", writes=["lam_sb"]))
                grp0.append(self.dma(gs_raw, g_sub.partition_broadcast(128), "misc", writes=["gs_raw"]))
                grp0.append(self.dma(cc32, ccsc, "misc", writes=["cc32"]))
                grp0.append(self.dma(wf32, w_f.rearrange("g c d -> c g d"), "misc", writes=["wf32"]))
                self.seal(grp0)
                tasks = []
                for kc in range(16):
                    rows = slice(kc * 128, (kc + 1) * 128)
                    tasks.append((w_in[rows, 0:2048], 2048,
                                  [(WA[0:2, :, kc, :].rearrange("b p c -> p b c"), 0, 1024),
                                   (WA[6:8, :, kc, :].rearrange("b p c -> p b c"), 1024, 1024)]))
                    tasks.append((w_in[rows, 2048:4096], 2048,
                                  [(WA[2:4, :, kc, :].rearrange("b p c -> p b c"), 0, 1024),
                                   (WA[4:6, :, kc, :].rearrange("b p c -> p b c"), 1024, 1024)]))
                n_fore = len(tasks)
                for kc in range(16):
                    rows = slice(kc * 128, (kc + 1) * 128)
                    for n in ("w_out", "w_cq", "w_ck", "w_cv", "w_co"):
                        tasks.append((wsrc[n][rows, :], 2048, [(WB[n][:, :, kc, :].rearrange("b p c -> p b c"), 0, 2048)]))
                    for n in ("w_gate", "w_up"):
                        tasks.append((wsrc[n][rows, 0:3072], 3072, [(WB[n][0:6, :, kc, :].rearrange("b p c -> p b c"), 0, 3072)]))
                        tasks.append((wsrc[n][rows, 3072:DFF], 2560, [(WB[n][6:11, :, kc, :].rearrange("b p c -> p b c"), 0, 2560)]))
                for kc in range(44):
                    rows = slice(kc * 128, (kc + 1) * 128)
                    tasks.append((wsrc["w_down"][rows, :], 2048, [(WB["w_down"][:, :, kc, :].rearrange("b p c -> p b c"), 0, 2048)]))
                cstate = {"i": 0, "loaded": 0, "stored": 0}

                def conv_store(i):
                    for (dst, c0, nsub) in tasks[i][2]:
                        self.dma(dst, cw16[i % 2][:, c0:c0 + nsub].rearrange("p (b c) -> p b c", c=512), "cs%d" % (i % 2),
                                 reads=[("cw16", i % 2)])

                def conv_step(n, engines):
                    for _ in range(n):
                        i = cstate["i"]
                        if i >= len(tasks):
                            break
                        while cstate["loaded"] <= min(i + 1, len(tasks) - 1):
                            l = cstate["loaded"]
                            self.dma(cw32[l % 2][:, 0:tasks[l][1]], tasks[l][0], "cv%d" % (l % 2), writes=[("cw32", l % 2)])
                            cstate["loaded"] += 1
                        ncols = tasks[i][1]
                        self.tcopy(engines[i % len(engines)], cw16[i % 2][:, 0:ncols], cw32[i % 2][:, 0:ncols], [("cw32", i % 2)], [("cw16", i % 2)])
                        while cstate["stored"] < i:
                            conv_store(cstate["stored"])
                            cstate["stored"] += 1
                        cstate["i"] += 1
                    if cstate["i"] >= len(tasks):
                        while cstate["stored"] < len(tasks):
                            conv_store(cstate["stored"])
                            cstate["stored"] += 1

                def conv_flush():
                    while cstate["stored"] < cstate["i"]:
                        conv_store(cstate["stored"])
                        cstate["stored"] += 1

                self.conv_step = conv_step
                self.conv_flush = conv_flush
                self.conv_left = lambda: len(tasks) - cstate["i"]
                conv_step(n_fore, ("dve", "act", "pool"))
                conv_flush()

                lj = A.alloc([128, 64], F32)
                lsum = A.alloc([128, 2], F32)
                lexp = A.alloc([128, 2], F32)
                lval = A.alloc([128, 1], F32)
                for i in range(2):
                    a0 = lam_sb[0:1, i * 128:i * 128 + 64]
                    a1 = lam_sb[0:1, i * 128 + 64:i * 128 + 128]
                    acc_ = lsum[0:1, i:i + 1]
                    P.op("dve", lambda e, a0=a0, a1=a1, acc_=acc_: e.scalar_tensor_tensor(
                        out=lj[0:1, :], in0=a0, scalar=1.0, in1=a1, op0=ALU.mult, op1=ALU.mult, accum_out=acc_),
                        ["lam_sb"], ["lj", ("lsum", i)])
                self.act(lexp[0:1, :], lsum[0:1, :], AF.Exp, [("lsum", 0), ("lsum", 1)], ["lexp"])
                self.stt(lval[0:1, :], lexp[0:1, 0:1], 0.2, lexp[0:1, 1:2], ALU.add, ALU.subtract, ["lexp"], ["lval"])
                self.dma(lam_scr, lval[0:1, :], "c2", reads=["lval"], writes=["lam_scr"])
                self.dma(lam_b, lam_scr[0, :].partition_broadcast(128), "c2", reads=["lam_scr"], writes=["lam_b"])
                self.ts(gsb, gs_raw, 0.8, None, ALU.mult, None, ["gs_raw"], ["gsb"])
                AB = self.AB
                for g in range(8):
                    for t in range(2):
                        b = (g * 2 + t) % 2
                        self.mm(ps_bank(b)[:, 0:128], cc32[:, t * 128:(t + 1) * 128], wf32[:, g, :], True, True, ["cc32", "wf32"], [("ps", b)])
                        self.tcopy("dve", AB[:, g, t * 128:(t + 1) * 128], ps_bank(b)[:, 0:128], [("ps", b)], [("AB", g, t)])
                P.barrier()

                stop = self.stop_after
                for gi, (S, T) in enumerate(G):
                    if stop == "W":
                        break
                    self.phase_A(A, psum, gi, S, T, gvec, WA, KT[gi], QT[gi], V[gi], Z[gi])
                    self.conv_flush()
                    P.barrier()
                    if stop == "A":
                        break
                    self.phase_B(A, psum, gi, S, T, KT[gi], QT[gi], V[gi], MIXT[gi], lam_b, gsb)
                    self.conv_step(self.conv_left(), ("pool", "act", "dve"))
                    self.conv_flush()
                    P.barrier()
                    if stop == "B":
                        break
                    self.phase_C(A, psum, gi, S, T, Z[gi], MIXT[gi])
                    P.barrier()
                    if stop == "C":
                        break
                    self.phase_D(A, psum, gi, S, T, gvec, WB, MIXT[gi], ones_c)
                    P.barrier()
                if self.debug:
                    dbg_src = {"WA": WA, "KT": KT[0], "QT": QT[0], "V": V[0], "Z": Z[0], "MIXT": MIXT[0], "Wout": WB["w_out"], "Wdown": WB["w_down"]}
                    for i, nm in enumerate(self.debug):
                        src = dbg_src[nm]
                        shp = list(src.shape)
                        dst = dt("dbg_" + nm, shp, BF16, kind="ExternalOutput").ap()
                        if len(shp) == 4:
                            for b_ in range(shp[0]):
                                self.dma(dst[b_], src[b_], "misc")
                        elif len(shp) == 3:
                            for b_ in range(shp[0]):
                                self.dma(dst[b_], src[b_], "misc")
                        else:
                            self.dma(dst, src, "misc")
                P.emit(sems_eng, sems_dma)
        return nc

    def norm_transpose(self, psum, xin, kx, gb, sm, hbf, khbf, hT, khT, tcol, tbanks):
        ss, mse, rstd, ksm = sm
        self.act(hbf, xin, AF.Square, [kx], [khbf, (ksm, "ss")], accum=ss)
        self.rstd_from_ss(ss, mse, rstd, 1.0 / D, 1e-6, (ksm, "ss"), (ksm, "mse"), (ksm, "rstd"))
        self.stt(hbf, xin, rstd, gb, ALU.mult, ALU.mult, [kx, (ksm, "rstd"), "gb"], [khbf])
        for half in range(2):
            b = tbanks[half]
            pst = psum[:, b, :].bitcast(BF16).rearrange("p (a b) -> p a b", a=8)
            for j in range(8):
                kc = half * 8 + j
                self.tr(pst[:, j, :], hbf[:, kc * 128:(kc + 1) * 128], [khbf], [("ps", b)])
            self.evac(hT[:, half * 8:(half + 1) * 8, tcol:tcol + 128], pst, [("ps", b)], [(khT, half, tcol)])

    def norm_only(self, xin, kx, gb, sm, hbf, khbf):
        ss, mse, rstd, ksm = sm
        self.act(hbf, xin, AF.Square, [kx], [khbf, (ksm, "ss")], accum=ss)
        self.rstd_from_ss(ss, mse, rstd, 1.0 / D, 1e-6, (ksm, "ss"), (ksm, "mse"), (ksm, "rstd"))
        self.stt(hbf, xin, rstd, gb, ALU.mult, ALU.mult, [kx, (ksm, "rstd"), "gb"], [khbf])

    def transpose_only(self, psum, hbf, khbf, hT, khT, tcol, tbanks):
        for half in range(2):
            b = tbanks[half]
            pst = psum[:, b, :].bitcast(BF16).rearrange("p (a b) -> p a b", a=8)
            for j in range(8):
                kc = half * 8 + j
                self.tr(pst[:, j, :], hbf[:, kc * 128:(kc + 1) * 128], [khbf], [("ps", b)])
            self.evac(hT[:, half * 8:(half + 1) * 8, tcol:tcol + 128], pst, [("ps", b)], [(khT, half, tcol)])

    def phase_A(self, A, psum, gi, S, T, gvec, WA, KT, QT, V, Z):
        P = self.P
        A.reset(self.conv_bytes if self.conv_left() > 0 else 0)
        ntile = S // 512
        nown = T // 512
        gb = A.alloc([128, D], F32)
        self.dma(gb, gvec[0, :].partition_broadcast(128), "misc", writes=["gb"])
        xin = [A.alloc([128, D], F32) for _ in range(2)]
        hbf = [A.alloc([128, D], BF16) for _ in range(4)]
        sms = [(A.alloc([128, 1], F32), A.alloc([128, 1], F32), A.alloc([128, 1], F32), ("smA", i)) for i in range(4)]
        hT = [A.alloc([128, 16, 512], BF16) for _ in range(2)]
        R = 6
        wblk = [A.alloc([128, 8, 512], BF16) for _ in range(R)]
        rct = [A.alloc([128, 512], F32) for _ in range(2)]
        rst = [A.alloc([128, 512], F32) for _ in range(2)]
        stg = [A.alloc([128, 4, 512], BF16) for _ in range(2)]
        zstg = [A.alloc([128, 4, 1024], BF16) for _ in range(2)]
        fsb = [A.alloc([128, 512], BF16) for _ in range(2)]
        t1 = [A.alloc([128, 512], F32) for _ in range(2)]
        t2 = [A.alloc([128, 512], F32) for _ in range(2)]
        AB = self.AB
        pmat = self.pmat
        xcnt = [0]

        def prologue(ti):
            hs = ti % 2
            self.seal([self.dma(rct[hs], self.rc[gi][:, ti * 512:(ti + 1) * 512], "rt%d" % hs, writes=[("rct", hs)]),
                       self.dma(rst[hs], self.rs[gi][:, ti * 512:(ti + 1) * 512], "rt%d" % hs, writes=[("rst", hs)])])
            for st in range(4):
                xs = xcnt[0] % 2
                xcnt[0] += 1
                r0 = ti * 512 + st * 128
                self.dma(xin[xs], self.x[gi][r0:r0 + 128, :], "x%d" % xs, writes=[("xin", xs)])
                self.norm_only(xin[xs], ("xin", xs), gb, sms[st], hbf[st], ("hbf", st))

        def prologue_tr(ti):
            hs = ti % 2
            for st in range(4):
                self.transpose_only(psum, hbf[st], ("hbf", st), hT[hs], ("hT", hs), st * 128, (6, 7))

        jobs = []
        for ti in range(ntile):
            for b in range(2):
                jobs.append((ti, "four", b, ("Z", b)))
            for b in range(2):
                jobs.append((ti, "tok", 4 + b, ("V", b)))
            for b in range(2):
                jobs.append((ti, "rope", 2 + b, ("K", b)))
            if ti < nown:
                for b in range(2):
                    jobs.append((ti, "rope", 6 + b, ("Q", b)))
        akinds = getattr(self, "akinds", None)
        if akinds is not None:
            jobs = [j for j in jobs if j[1] in akinds]
        issued = [0]
        nfills = 2 * len(jobs)

        def ensure(upto):
            while issued[0] <= min(upto, nfills - 1):
                f = issued[0]
                blk = jobs[f // 2][2]
                kh = f % 2
                s = f % R
                self.dma(wblk[s], WA[blk, :, kh * 8:(kh + 1) * 8, :], "w%d" % s, writes=[("wblk", s)])
                issued[0] += 1

        psc = [0]

        def bank():
            b = psc[0] % 6
            psc[0] += 1
            return b

        pend = []
        ucnt = [0]

        def flush():
            while pend:
                pend.pop(0)()

        prologue(0)
        prologue_tr(0)
        cur_t = -1
        for ji, (ti, kind, blk, dest) in enumerate(jobs):
            if ti != cur_t:
                cur_t = ti
                flush()
                if ti + 1 < ntile:
                    prologue(ti + 1)
            if ti + 1 < ntile and (ji + 1 == len(jobs) or jobs[ji + 1][0] != ti):
                prologue_tr(ti + 1)
            hs = ti % 2
            f0 = 2 * ji
            ensure(f0 + R - 1)
            if self.conv_left() > 0:
                self.conv_step(1, ("pool", "act", "pool", "dve"))
            ss_ = ji % 2
            r0 = ti * 512
            if kind == "tok":
                banks = [bank() for _ in range(4)]
                for kh in range(2):
                    s = (f0 + kh) % R
                    for st in range(4):
                        for k8 in range(8):
                            kc = kh * 8 + k8
                            self.mm(psum[:, banks[st], :], hT[hs][:, kc, st * 128:(st + 1) * 128], wblk[s][:, k8, :],
                                    kc == 0, kc == 15, [(("hT", hs), kc // 8, st * 128), ("wblk", s)], [("ps", banks[st])])
                flush()
                for st in range(4):
                    self.evac(stg[ss_][:, st, :], psum[:, banks[st], :], [("ps", banks[st])], [("stg", ss_, st)], act_share=2, of=3)
                dst = V[r0:r0 + 512, dest[1] * 512:(dest[1] + 1) * 512]
                self.dma(dst.rearrange("(s p) c -> p s c", p=128), stg[ss_], "st%d" % ss_, reads=[("stg", ss_, st) for st in range(4)])
                continue
            for oc in range(4):
                b = bank()
                for kh in range(2):
                    s = (f0 + kh) % R
                    for k8 in range(8):
                        kc = kh * 8 + k8
                        self.mm(psum[:, b, :], wblk[s][:, k8, oc * 128:(oc + 1) * 128], hT[hs][:, kc, :],
                                kc == 0, kc == 15, [(("hT", hs), kc // 8, c) for c in (0, 128, 256, 384)] + [("wblk", s)], [("ps", b)])
                flush()
                u = ucnt[0] % 2
                ucnt[0] += 1
                self.tcopy("act", fsb[u], psum[:, b, :], [("ps", b)], [("fsb", u), ("ps", b)])
                if kind == "rope":
                    self.tt("dve", t1[u], psum[:, b, :], rct[hs], ALU.mult, [("ps", b), ("rct", hs)], [("t1", u)])

                    def stage2(u=u, oc=oc, hs=hs, ss_=ss_, dest=dest, ti=ti):
                        b2 = bank()
                        self.mm(psum[:, b2, :], pmat, fsb[u], True, True, ["pmat", ("fsb", u)], [("ps", b2)])
                        self.tt("dve", t2[u], psum[:, b2, :], rst[hs], ALU.mult, [("ps", b2), ("rst", hs)], [("t2", u)])
                        self.tt("pool", stg[ss_][:, oc, :], t1[u], t2[u], ALU.add, [("t1", u), ("t2", u)], [("stg", ss_, oc)])
                        if oc == 3:
                            h0 = dest[1] * 4
                            dd = KT if dest[0] == "K" else QT
                            dst = dd[h0:h0 + 4, :, ti * 512:(ti + 1) * 512]
                            self.dma(dst.rearrange("h p c -> p h c"), stg[ss_], "st%d" % ss_, reads=[("stg", ss_, o_) for o_ in range(4)])
                    pend.append(stage2)
                else:
                    def stage2(u=u, oc=oc, ss_=ss_, dest=dest, r0=r0, blk=blk):
                        g = blk * 4 + oc
                        for sp in range(2):
                            b2 = bank()
                            for q_ in range(2):
                                st = sp * 2 + q_
                                self.mm(psum[:, b2, q_ * 256:(q_ + 1) * 256], fsb[u][:, st * 128:(st + 1) * 128], AB[:, g, :], True, True,
                                        [("fsb", u), ("AB", g)], [("ps", b2)])
                            self.evac(zstg[ss_][:, sp * 2:sp * 2 + 2, oc * 256:(oc + 1) * 256], psum[:, b2, :].rearrange("p (a b) -> p a b", a=2),
                                      [("ps", b2)], [("zstg", ss_, oc, sp)], act_share=1, of=3)
                        if oc == 3:
                            dst = Z[r0:r0 + 512, dest[1] * 1024:(dest[1] + 1) * 1024]
                            self.dma(dst.rearrange("(s p) c -> p s c", p=128), zstg[ss_], "c%d" % ss_,
                                     reads=[("zstg", ss_, o_, p_) for o_ in range(4) for p_ in range(2)])
                    pend.append(stage2)
        flush()

    def phase_B(self, A, psum, gi, S, T, KT, QT, V, MIXT, lam_b, gsb):
        P = self.P
        A.reset(self.conv_bytes if self.conv_left() > 0 else 0)
        nkt = S // 128
        nqb = T // 512
        KTs = [A.alloc([128, S], BF16) for _ in range(2)]
        QTz = [[A.alloc([128, T], BF16) for _ in range(2)] for _ in range(2)]
        VA = [A.alloc([128, nkt, 130], BF16) for _ in range(2)]
        pt = [A.alloc([128, 1024], BF16) for _ in range(3)]
        t32 = [A.alloc([128, 128], F32) for _ in range(2)]
        o32 = [A.alloc([128, 128], F32) for _ in range(2)]
        jk = A.alloc([128, 128], F32)
        onb = [A.alloc([128, 4, 128], BF16) for _ in range(2)]
        oTs = [A.alloc([128, 512], BF16) for _ in range(2)]
        sm = [[A.alloc([128, 1], F32) for _ in range(6)] for _ in range(2)]
        for s in range(2):
            va = VA[s]
            P.op("pool", lambda e, va=va: e.memset(va[:, :, 128:129], 1.0), writes=[("VAone", s)])
            z0 = QTz[0][s][64:128, :]
            z1 = QTz[1][s][0:64, :]
            P.op("pool", lambda e, z0=z0: e.memset(z0, 0.0), writes=[("QTzero", s, 0)])
            P.op("dve", lambda e, z1=z1: e.memset(z1, 0.0), writes=[("QTzero", s, 1)])

        def load_head(h):
            s = h % 2
            grp = [self.dma(KTs[s], KT[h], "hd%d" % s, writes=[("KTs", s)]),
                   self.dma(QTz[0][s][0:64, :], QT[h, 0:64, :], "hd%d" % s, writes=[("QTs", s, 0)]),
                   self.dma(QTz[1][s][64:128, :], QT[h, 64:128, :], "hd%d" % s, writes=[("QTs", s, 1)])]
            step = 16
            for k0 in range(0, nkt, step):
                k1 = min(nkt, k0 + step)
                grp.append(self.dma(VA[s][:, k0:k1, 0:128], V[k0 * 128:k1 * 128, h * 128:(h + 1) * 128].rearrange("(k p) c -> p k c", p=128),
                                    "hd%d" % s, writes=[("VA", s, k0)]))
            self.seal(grp)

        def acc_ap(c, j):
            a = c * 4 + j
            return psum[:, 4 + a // 3, (a % 3) * 129:(a % 3) * 129 + 129], 4 + a // 3, (a % 3 == 0)

        load_head(0)
        fin = [0]
        for h in range(NH):
            s = h % 2
            if h + 1 < NH:
                load_head(h + 1)
            vkeys = [("VA", s, k0) for k0 in range(0, nkt, 16)] + [("VAone", s)]
            for qb in range(nqb):
                qsl = slice(qb * 512, (qb + 1) * 512)
                if self.conv_left() > 0:
                    it_left = (NH - h) * nqb - qb
                    self.conv_step(-(-self.conv_left() // max(1, it_left - 2)) if it_left > 2 else self.conv_left(), ("pool",))

                def ST(kt):
                    sl = kt % 2
                    for c in range(2):
                        self.mm(psum[:, sl * 2 + c, :], KTs[s][:, kt * 128:(kt + 1) * 128], QTz[c][s][:, qsl],
                                True, True, [("KTs", s), ("QTs", s, c), ("QTzero", s, c)], [("pss", sl, c)])

                ST(0)
                for kt in range(nkt):
                    if kt + 1 < nkt:
                        ST(kt + 1)
                    sl = kt % 2
                    p3 = kt % 3
                    for c in range(2):
                        self.act(pt[p3][:, c * 512:(c + 1) * 512], psum[:, sl * 2 + c, :], AF.Exp, [("pss", sl, c)], [("pt", p3, c)], scale=0.125)
                    for c in range(2):
                        for j in range(4):
                            ap_, bank, first = acc_ap(c, j)
                            self.mm(ap_, pt[p3][:, c * 512 + j * 128:c * 512 + (j + 1) * 128], VA[s][:, kt, 0:129],
                                    (kt == 0 and first), kt == nkt - 1, [("pt", p3, c), ("VA", s, (kt // 16) * 16), ("VAone", s)], [("acc", bank)], skip=True)
                fs = fin[0] % 2
                fin[0] += 1
                for j in range(4):
                    a0, b0, _ = acc_ap(0, j)
                    a1, b1, _ = acc_ap(1, j)
                    u = j % 2
                    r0, r1, r1l, ss, mse, rstd = sm[u]
                    ku = ("smB", u)
                    P.op("dve", lambda e, r0=r0, a0=a0: e.reciprocal(out=r0, in_=a0[:, 128:129]), [("acc", b0)], [(ku, "r0")])
                    P.op("dve", lambda e, r1=r1, a1=a1: e.reciprocal(out=r1, in_=a1[:, 128:129]), [("acc", b1)], [(ku, "r1")])
                    self.tt("dve", r1l, r1, lam_b, ALU.mult, [(ku, "r1"), "lam_b"], [(ku, "r1l")])
                    self.ts(t32[u], a1[:, 0:128], r1l, None, ALU.mult, None, [("acc", b1), (ku, "r1l")], [("t32", u)])
                    self.stt(o32[u], a0[:, 0:128], r0, t32[u], ALU.mult, ALU.subtract, [("acc", b0), (ku, "r0"), ("t32", u)], [("o32", u)])
                    o_ = o32[u]
                    P.op("dve", lambda e, o_=o_, ss=ss: e.scalar_tensor_tensor(out=jk, in0=o_, scalar=1.0, in1=o_,
                                                                             op0=ALU.mult, op1=ALU.mult, accum_out=ss),
                         [("o32", u)], ["jk", (ku, "ss")])
                    self.rstd_from_ss(ss, mse, rstd, 1.0 / 128, 1e-5, (ku, "ss"), (ku, "mse"), (ku, "rstd"))
                    self.stt(onb[fs][:, j, :], o32[u], rstd, gsb, ALU.mult, ALU.mult, [("o32", u), (ku, "rstd"), "gsb"], [("onb", fs, j)])
                    pst = psum[:, 7, :].bitcast(BF16).rearrange("p (a b) -> p a b", a=8)
                    self.tr(pst[:, j, :], onb[fs][:, j, :], [("onb", fs, j)], [("pstB", 0)])
                pst = psum[:, 7, :].bitcast(BF16)
                self.tcopy("dve", oTs[fs], pst[:, 0:512], [("pstB", 0)], [("oTs", fs)])
                self.dma(MIXT[1024 + h * 128:1024 + (h + 1) * 128, qsl], oTs[fs], "st%d" % fs, reads=[("oTs", fs)])

    def phase_C(self, A, psum, gi, S, T, Z, MIXT):
        P = self.P
        A.reset()
        nch = S // 128
        NCH = min(32, nch)
        nfill = nch // NCH
        nkt = T // 128
        Zs = A.alloc([128, nch, 1024], BF16)
        R = 3
        dcs = [A.alloc([128, NCH, 128], BF16) for _ in range(R)]
        dss = [A.alloc([128, NCH, 128], BF16) for _ in range(R)]
        ysb = [A.alloc([128, 512], BF16) for _ in range(2)]
        yT = [A.alloc([128, 4, 128], BF16) for _ in range(2)]
        for half in range(2):
            step = 16
            zk = []
            for k0 in range(0, nch, step):
                k1 = min(nch, k0 + step)
                zk.append(self.dma(Zs[:, k0:k1, :], Z[k0 * 128:k1 * 128, half * 1024:(half + 1) * 1024].rearrange("(k p) c -> p k c", p=128),
                                   "z0", writes=[("Zs", k0)]))
            self.seal(zk)
            fl = [(kt, f) for kt in range(nkt) for f in range(nfill)]
            issued = [0]

            def ensure(upto):
                while issued[0] <= min(upto, len(fl) - 1):
                    i = issued[0]
                    kt, f = fl[i]
                    s = i % R
                    self.seal([self.dma(dcs[s], self.dc[gi][kt, :, f * NCH:(f + 1) * NCH, :], "df%d" % s, writes=[("dcs", s)]),
                               self.dma(dss[s], self.ds[gi][kt, :, f * NCH:(f + 1) * NCH, :], "df%d" % s, writes=[("dss", s)])])
                    issued[0] += 1

            for i, (kt, f) in enumerate(fl):
                ensure(i + R - 1)
                s = i % R
                b = kt % 2
                for n in range(NCH):
                    ncg = f * NCH + n
                    zrow = Zs[:, ncg, :].rearrange("p (g c) -> p g c", g=4)
                    self.mm(psum[:, b, :], dcs[s][:, n, :], zrow[:, :, 0:128], (ncg == 0), False, [("dcs", s), ("Zs", (ncg // 16) * 16)], [("psC", b)])
                    self.mm(psum[:, b, :], dss[s][:, n, :], zrow[:, :, 128:256], False, (ncg == nch - 1), [("dss", s), ("Zs", (ncg // 16) * 16)], [("psC", b)])
                if f == nfill - 1:
                    self.evac(ysb[b], psum[:, b, :], [("psC", b)], [("ysb", b)])
                    pst = psum[:, 6 + b, :].bitcast(BF16).rearrange("p (a b) -> p a b", a=8)
                    for g in range(4):
                        self.tr(pst[:, g, :], ysb[b][:, g * 128:(g + 1) * 128], [("ysb", b)], [("pstC", b)])
                    self.evac(yT[b], pst[:, 0:4, :], [("pstC", b)], [("yT", b)])
                    self.dma(MIXT[half * 512:(half + 1) * 512, kt * 128:(kt + 1) * 128].rearrange("(g p) c -> p g c", p=128), yT[b],
                             "st%d" % b, reads=[("yT", b)])

    def phase_D(self, A, psum, gi, S, T, gvec, WB, MIXT, ones_c):
        P = self.P
        A.reset()
        ntile = T // 512
        xres = A.alloc([128, 4, D], F32)
        actT = A.alloc([128, 16, 512], BF16)
        hT = A.alloc([128, 16, 512], BF16)
        hid = A.alloc([128, 24, 512], BF16)
        qT = [A.alloc([128, 4, 512], BF16) for _ in range(2)]
        R = 6
        wblk = [A.alloc([128, 8, 512], BF16) for _ in range(R)]
        kTm = A.alloc([128, 16, 256], BF16)
        vm = A.alloc([128, 2, D], BF16)
        gb = A.alloc([128, D], F32)
        hbf = [A.alloc([128, D], BF16) for _ in range(4)]
        pT = [A.alloc([128, 2, 512], BF16) for _ in range(2)]
        ocn = [A.alloc([128, 512], BF16) for _ in range(2)]
        sg = [A.alloc([128, 512], BF16) for _ in range(2)]
        sms = [(A.alloc([128, 1], F32), A.alloc([128, 1], F32), A.alloc([128, 1], F32), ("smD", i)) for i in range(4)]
        rl = [A.alloc([128, 1], F32) for _ in range(2)]
        memT = A.alloc([128, 16, 256], BF16)

        def blockfills(name, blk, kc0=0, kc1=16):
            out = []
            k = kc0
            while k < kc1:
                n = min(8, kc1 - k)
                out.append((name, blk, k, n))
                k += n
            return out

        fills = []
        for blk in range(4):
            fills += blockfills("w_ck", blk)
        for blk in range(4):
            fills += blockfills("w_cv", blk)
        halves = [(0, 6), (6, 11)]
        dstop = getattr(self, "dstop", 99)
        for ti in range(ntile):
            for blk in range(4):
                fills += blockfills("w_out", blk)
            if dstop <= 2:
                continue
            for blk in range(4):
                fills += blockfills("w_cq", blk)
            for blk in range(4):
                fills += blockfills("w_co", blk)
            if dstop <= 5:
                continue
            for (b0, b1) in halves:
                for blk in range(b0, b1):
                    fills += blockfills("w_gate", blk)
                    fills += blockfills("w_up", blk)
                for cb in range(4):
                    fills += blockfills("w_down", cb, b0 * 4, b1 * 4)
        issued = [0]
        used = [0]

        def ensure(upto):
            while issued[0] <= min(upto, len(fills) - 1):
                i = issued[0]
                name, blk, k0, n = fills[i]
                s = i % R
                self.dma(wblk[s][:, 0:n, :], WB[name][blk, :, k0:k0 + n, :], "w%d" % s, writes=[("wblk", s)])
                issued[0] += 1

        def next_fill(expect):
            i = used[0]
            assert fills[i][0] == expect, (fills[i], expect)
            ensure(i + R - 4)
            used[0] += 1
            return i % R, fills[i]

        self.dma(gb, gvec[2, :].partition_broadcast(128), "misc", writes=["gb"])
        for mt in range(2):
            self.dma(xres[:, mt, :], self.mem[gi][mt * 128:(mt + 1) * 128, :], "x%d" % mt, writes=[("xres", mt)])
            self.norm_transpose(psum, xres[:, mt, :], ("xres", mt), gb, sms[mt], hbf[mt], ("hbf", mt), memT, "memT", mt * 128, (6, 7))
        mk = [("memT", h, c) for h in range(2) for c in (0, 128)]
        evc = 0
        for blk in range(4):
            f = [next_fill("w_ck"), next_fill("w_ck")]
            for oc in range(4):
                b = (blk * 4 + oc) % 4
                for kh in range(2):
                    s = f[kh][0]
                    for k8 in range(8):
                        kc = kh * 8 + k8
                        self.mm(psum[:, b, 0:256], wblk[s][:, k8, oc * 128:(oc + 1) * 128], memT[:, kc, :], kc == 0, kc == 15,
                                [("wblk", s)] + [("memT", kc // 8, c) for c in (0, 128)], [("ps", b)])
                self.evac(kTm[:, blk * 4 + oc, :], psum[:, b, 0:256], [("ps", b)], [("kTm", blk * 4 + oc)])
        for blk in range(4):
            f = [next_fill("w_cv"), next_fill("w_cv")]
            for mt in range(2):
                b = (blk * 2 + mt) % 4
                for kh in range(2):
                    s = f[kh][0]
                    for k8 in range(8):
                        kc = kh * 8 + k8
                        self.mm(psum[:, b, :], memT[:, kc, mt * 128:(mt + 1) * 128], wblk[s][:, k8, :], kc == 0, kc == 15,
                                [("wblk", s), ("memT", kc // 8, mt * 128)], [("ps", b)])
                self.evac(vm[:, mt, blk * 512:(blk + 1) * 512], psum[:, b, :], [("ps", b)], [("vm", mt, blk)])

        def tokmajor_accum(name, lhs, lhs_keys, nk0, nk1, after_st=None):
            for cb in range(4):
                if cb == 3 and after_st is not None:
                    held = []
                    k = nk0
                    while k < nk1:
                        s, (nm, blk, k0, n) = next_fill(name)
                        assert blk == cb and k0 == k
                        held.append((s, k, n))
                        k += n
                    for st in range(4):
                        for (s, k, n) in held:
                            for k8 in range(n):
                                kc = k + k8
                                self.mm(psum[:, st, :], lhs[:, kc - lhs_base[0], st * 128:(st + 1) * 128], wblk[s][:, k8, :], kc == nk0, kc == nk1 - 1,
                                        [("wblk", s)] + lhs_keys(kc, st), [("ps", st)])
                        xs = xres[:, st, cb * 512:(cb + 1) * 512]
                        self.tt("dve", xs, psum[:, st, :], xs, ALU.add, [("ps", st), ("xres", st)], [("xres", st)])
                        after_st(st)
                    continue
                k = nk0
                while k < nk1:
                    s, (nm, blk, k0, n) = next_fill(name)
                    assert blk == cb and k0 == k
                    for st in range(4):
                        for k8 in range(n):
                            kc = k + k8
                            self.mm(psum[:, st, :], lhs[:, kc - lhs_base[0], st * 128:(st + 1) * 128], wblk[s][:, k8, :], kc == nk0, kc == nk1 - 1,
                                    [("wblk", s)] + lhs_keys(kc, st), [("ps", st)])
                    k += n
                for st in range(4):
                    xs = xres[:, st, cb * 512:(cb + 1) * 512]
                    self.tt("dve", xs, psum[:, st, :], xs, ALU.add, [("ps", st), ("xres", st)], [("xres", st)])

        def norm_st(st):
            self.norm_only(xres[:, st, :], ("xres", st), gb, sms[st], hbf[st], ("hbf", st))

        def transposes_all():
            for st in range(4):
                self.transpose_only(psum, hbf[st], ("hbf", st), hT, "hT", st * 128, (6, 7))

        def final_norm_st(st):
            ss, mse, rstd, ksm = sms[st]
            self.act(hbf[st], xres[:, st, :], AF.Square, [("xres", st)], [("hbf", st), (ksm, "ss")], accum=ss)
            self.rstd_from_ss(ss, mse, rstd, 1.0 / D, 1e-6, (ksm, "ss"), (ksm, "mse"), (ksm, "rstd"))
            self.stt(xres[:, st, :], xres[:, st, :], rstd, gb, ALU.mult, ALU.mult, [("xres", st), (ksm, "rstd"), "gb"], [("xres", st)])

        lhs_base = [0]

        def norm_all(grow):
            self.dma(gb, gvec[grow, :].partition_broadcast(128), "misc", writes=["gb"])
            for st in range(4):
                self.norm_transpose(psum, xres[:, st, :], ("xres", st), gb, sms[st % 2], hbf[st % 2], ("hbf", st % 2), hT, "hT", st * 128, (6, 7))

        hTk = lambda kc: [("hT", kc // 8, c) for c in (0, 128, 256, 384)]
        cross_scale = 512 ** -0.5

        for ti in range(ntile):
            r0 = ti * 512
            self.dma(xres, self.x[gi][r0:r0 + 512, :].rearrange("(s p) c -> p s c", p=128), "x0", writes=[("xres", st) for st in range(4)])
            self.dma(actT, MIXT[:, r0:r0 + 512].rearrange("(k p) c -> p k c", p=128), "x1", writes=[("actT", kc) for kc in range(16)])
            lhs_base[0] = 0
            self.dma(gb, gvec[1, :].partition_broadcast(128), "misc", writes=["gb"])
            tokmajor_accum("w_out", actT, lambda kc, st: [("actT", kc)], 0, 16, after_st=norm_st)
            def dump():
                self.dma(self.y[gi][r0:r0 + 512, :].rearrange("(s p) c -> p s c", p=128), xres, "out%d" % (ti % 2),
                         reads=[("xres", st) for st in range(4)])
            if dstop <= 1:
                dump()
                continue
            transposes_all()
            if dstop == 2:
                if ti == 0 and gi == 0:
                    dh = self.nc.dram_tensor("dbg_hT", [128, 16, 512], BF16, kind="ExternalOutput").ap()
                    self.dma(dh, hT, "misc", reads=[("hT", h_, c_) for h_ in range(2) for c_ in (0, 128, 256, 384)])
                    dm_ = self.nc.dram_tensor("dbg_memT", [128, 16, 256], BF16, kind="ExternalOutput").ap()
                    self.dma(dm_, memT, "misc", reads=[("memT", h_, c_) for h_ in range(2) for c_ in (0, 128)])
                    dk = self.nc.dram_tensor("dbg_kTm", [128, 16, 256], BF16, kind="ExternalOutput").ap()
                    self.dma(dk, kTm, "misc", reads=[("kTm", i_) for i_ in range(16)])
                    dv = self.nc.dram_tensor("dbg_vm", [128, 2, 2048], BF16, kind="ExternalOutput").ap()
                    self.dma(dv, vm, "misc", reads=[("vm", m_, b_) for m_ in range(2) for b_ in range(4)])
                dump()
                continue
            for hh in range(4):
                f = [next_fill("w_cq"), next_fill("w_cq")]
                qs = hh % 2
                for oc in range(4):
                    b = oc
                    for kh in range(2):
                        s = f[kh][0]
                        for k8 in range(8):
                            kc = kh * 8 + k8
                            self.mm(psum[:, b, :], wblk[s][:, k8, oc * 128:(oc + 1) * 128], hT[:, kc, :], kc == 0, kc == 15,
                                    [("wblk", s)] + hTk(kc), [("ps", b)])
                    self.evac(qT[qs][:, oc, :], psum[:, b, :], [("ps", b)], [("qT", qs, oc)])
                ps_ = hh % 2
                for mt in range(2):
                    b = 4 + mt
                    for dc_ in range(4):
                        self.mm(psum[:, b, :], kTm[:, hh * 4 + dc_, mt * 128:(mt + 1) * 128], qT[qs][:, dc_, :], dc_ == 0, dc_ == 3,
                                [("kTm", hh * 4 + dc_), ("qT", qs, dc_)], [("ps", b)])
                    self.act(pT[ps_][:, mt, :], psum[:, b, :], AF.Exp, [("ps", b)], [("pT", ps_, mt)], scale=cross_scale)
                for st in range(4):
                    b = st % 2
                    for mt in range(2):
                        self.mm(psum[:, b, :], pT[ps_][:, mt, st * 128:(st + 1) * 128], vm[:, mt, hh * 512:(hh + 1) * 512], mt == 0, mt == 1,
                                [("pT", ps_, mt), ("vm", mt, hh)], [("ps", b)])
                    lb = 2 + (st % 2)
                    for mt in range(2):
                        self.mm(psum[:, lb, 0:1], pT[ps_][:, mt, st * 128:(st + 1) * 128], ones_c, mt == 0, mt == 1,
                                [("pT", ps_, mt), "ones_c"], [("ps", lb)])
                    rl_ = rl[st % 2]
                    lsrc = psum[:, lb, 0:1]
                    P.op("dve", lambda e, rl_=rl_, lsrc=lsrc: e.reciprocal(out=rl_, in_=lsrc), [("ps", lb)], [("rl", st % 2)])
                    self.ts(ocn[st % 2], psum[:, b, :], rl_, None, ALU.mult, None, [("ps", b), ("rl", st % 2)], [("ocn", st % 2)])
                    tb = 6 + (st % 2)
                    pst = psum[:, tb, :].bitcast(BF16).rearrange("p (a b) -> p a b", a=8)
                    for dc_ in range(4):
                        self.tr(pst[:, dc_, :], ocn[st % 2][:, dc_ * 128:(dc_ + 1) * 128], [("ocn", st % 2)], [("ps", tb)])
                    self.evac(actT[:, hh * 4:(hh + 1) * 4, st * 128:(st + 1) * 128], pst[:, 0:4, :], [("ps", tb)],
                              [("actT", hh * 4 + d_) for d_ in range(4)])
            self.dma(gb, gvec[3, :].partition_broadcast(128), "misc", writes=["gb"])
            tokmajor_accum("w_co", actT, lambda kc, st: [("actT", kc)], 0, 16, after_st=norm_st)
            if dstop <= 5:
                dump()
                continue
            transposes_all()
            for (b0, b1) in halves:
                for blk in range(b0, b1):
                    fg = [next_fill("w_gate"), next_fill("w_gate")]
                    fu = [next_fill("w_up"), next_fill("w_up")]
                    for oc in range(4):
                        bg = (oc % 2) * 2
                        bu = bg + 1
                        for (ff, bb) in ((fg, bg), (fu, bu)):
                            for kh in range(2):
                                s = ff[kh][0]
                                for k8 in range(8):
                                    kc = kh * 8 + k8
                                    self.mm(psum[:, 4 + bb, :], wblk[s][:, k8, oc * 128:(oc + 1) * 128], hT[:, kc, :], kc == 0, kc == 15,
                                            [("wblk", s)] + hTk(kc), [("ps", 4 + bb)])
                        sgs = oc % 2
                        self.act(sg[sgs], psum[:, 4 + bg, :], AF.Silu, [("ps", 4 + bg)], [("sg", sgs)])
                        hc = (blk - b0) * 4 + oc
                        self.tt("dve", hid[:, hc, :], psum[:, 4 + bu, :], sg[sgs], ALU.mult, [("ps", 4 + bu), ("sg", sgs)], [("hid", hc)])
                lhs_base[0] = b0 * 4
                last_half = (b1 == 11)
                if last_half:
                    self.dma(gb, gvec[4, :].partition_broadcast(128), "misc", writes=["gb"])
                tokmajor_accum("w_down", hid, lambda kc, st: [("hid", kc - lhs_base[0])], b0 * 4, b1 * 4,
                               after_st=final_norm_st if last_half else None)
            if dstop <= 8:
                dump()
                continue
            self.dma(self.y[gi][r0:r0 + 512, :].rearrange("(s p) c -> p s c", p=128), xres, "out%d" % (ti % 2),
                     reads=[("xres", st) for st in range(4)])
        assert used[0] == len(fills), (used[0], len(fills))


_TABLE_CACHE = {}


def _perm(S, own0, T):
    own = np.arange(own0, own0 + T)
    rest = np.concatenate([np.arange(0, own0), np.arange(own0 + T, S)])
    return np.concatenate([own, rest]).astype(np.int64)


def _rope_tables(S, perm):
    inv = (1.0 / (np.float32(10000.0) ** (np.arange(0, 64, 2, dtype=np.float32) / np.float32(64)))).astype(np.float32)
    pos = perm.astype(np.float32)
    ang = (pos[:, None] * inv[None, :]).astype(np.float32)
    ang = np.concatenate([ang, ang], axis=1)
    c = np.cos(ang).astype(np.float32)
    s = np.sin(ang).astype(np.float32)
    sgn = np.concatenate([-np.ones(32, np.float32), np.ones(32, np.float32)])
    s = s * sgn[None, :]
    cT = np.ascontiguousarray(np.concatenate([c, c], axis=1).T)
    sT = np.ascontiguousarray(np.concatenate([s, s], axis=1).T)
    return cT, sT


def _dft_tables(S, perm, own0, T):
    key = (S, own0, T)
    if key in _TABLE_CACHE:
        return _TABLE_CACHE[key]
    k = np.arange(own0, own0 + T, dtype=np.int64)
    scale = 1.0 / np.sqrt(S * 128.0)
    nchunk = S // 128
    dc = np.empty((T // 128, 128, nchunk, 128), dtype=bf16_np)
    ds = np.empty((T // 128, 128, nchunk, 128), dtype=bf16_np)
    tab_c = (np.cos(2 * np.pi * np.arange(S) / S) * scale).astype(np.float32)
    tab_s = (-np.sin(2 * np.pi * np.arange(S) / S) * scale).astype(np.float32)
    n2 = perm.reshape(nchunk, 128)
    for kt in range(T // 128):
        kk = k[kt * 128:(kt + 1) * 128]
        m = (n2[:, :, None] * kk[None, None, :]) % S
        dc[kt] = tab_c[m].transpose(1, 0, 2).astype(bf16_np)
        ds[kt] = tab_s[m].transpose(1, 0, 2).astype(bf16_np)
    _TABLE_CACHE[key] = (dc, ds)
    return dc, ds


def _shared_inputs(inp):
    f32 = np.float32
    w_in = np.ascontiguousarray(np.asarray(inp["w_in"], f32)[0])
    m = np.arange(128)
    pm = np.zeros((128, 128), np.float32)
    pm[(m // 64) * 64 + ((m % 64) + 32) % 64, m] = 1.0
    c = np.arange(128)
    ang = 2 * np.pi * np.outer(c, c) / 128.0
    ccsc = np.concatenate([np.cos(ang), np.sin(ang)], axis=1).astype(f32)
    lam = np.concatenate([np.asarray(inp[n], f32)[0] for n in ("lambda_q1", "lambda_k1", "lambda_q2", "lambda_k2")])[None, :]
    gvec = np.stack([np.asarray(inp["g_mix"], f32)[0], np.asarray(inp["g_cross"], f32)[0], np.asarray(inp["g_mem"], f32)[0],
                     np.asarray(inp["g_ffn"], f32)[0], np.asarray(inp["g_final"], f32)])
    sh = {
        "w_in": w_in, "pmat": pm.astype(bf16_np),
        "w_f": np.ascontiguousarray(np.asarray(inp["w_fourier"], f32)[0]),
        "ccsc": np.ascontiguousarray(ccsc), "lam": np.ascontiguousarray(lam), "gvec": np.ascontiguousarray(gvec),
        "g_sub": np.ascontiguousarray(np.asarray(inp["g_subln"], f32)[0]),
        "ident": np.eye(128, dtype=np.float32).astype(bf16_np),
    }
    for n in ("w_out", "w_cq", "w_ck", "w_cv", "w_co", "w_gate", "w_up", "w_down"):
        sh[n] = np.ascontiguousarray(np.asarray(inp[n], f32)[0])
    return sh


def _group_inputs(gi, xseq, memseq, own0, T):
    S = xseq.shape[0]
    perm = _perm(S, own0, T)
    cT, sT = _rope_tables(S, perm)
    dc, ds = _dft_tables(S, perm, own0, T)
    return {
        "x%d" % gi: np.ascontiguousarray(xseq[perm]),
        "mem%d" % gi: np.ascontiguousarray(memseq),
        "rc%d" % gi: cT, "rs%d" % gi: sT, "dc%d" % gi: dc, "ds%d" % gi: ds,
    }


_NC_CACHE = {}


def _get_nc(groups):
    key = tuple(groups)
    if key not in _NC_CACHE:
        _NC_CACHE[key] = Builder(list(groups)).build()
    return _NC_CACHE[key]


def kernel(**inputs):
    f32 = np.float32
    xp = np.asarray(inputs["x_prompt"], f32)
    xs = np.asarray(inputs["x_sample"], f32)
    mp = np.asarray(inputs["mem_prompt"], f32)
    ms = np.asarray(inputs["mem_sample"], f32)
    B, S0, _ = xp.shape
    B1, S1, _ = xs.shape
    n = 8
    T0 = S0 * B // n
    T1 = S1 * B1 // n
    cp = n // B
    cs = n // B1
    groups = ((S0, T0), (S1, T1))
    nc = _get_nc(groups)
    sh = _shared_inputs(inputs)
    in_maps = []
    for c in range(n):
        m = dict(sh)
        m.update(_group_inputs(0, xp[c // cp], mp[c // cp], (c % cp) * T0, T0))
        m.update(_group_inputs(1, xs[c // cs], ms[c // cs], (c % cs) * T1, T1))
        in_maps.append(m)
    res = run_bass_kernel_spmd(nc, in_maps, core_ids=list(range(n)))
    yp = np.empty((B, S0, D), f32)
    ys = np.empty((B1, S1, D), f32)
    for c in range(n):
        r = res.results[c]
        yp[c // cp, (c % cp) * T0:(c % cp + 1) * T0] = r["y0"]
        ys[c // cs, (c % cs) * T1:(c % cs + 1) * T1] = r["y1"]
    return yp, ys
```
